# Optimizing a Trainium2 kernel written in Bass

```python
import jax
import jax.numpy as jnp
from jax import lax
import numpy as np

D_MODEL = 1024
BATCH = 16
SEQ = 4096
DEPTH = 2

CHUNK = 64
Q_BLOCK = 128
NORM_EPS = 1e-6

MLA_HEADS = 8
MLA_Q_LORA = 512
MLA_KV_LORA = 256
MLA_NOPE = 128
MLA_ROPE = 64
MLA_V = 128
MLA_WIDTH = MLA_HEADS * MLA_V
ROPE_BASE = 10000.0

RWKV_HEAD = 64
RWKV_HEADS = D_MODEL // RWKV_HEAD
RWKV_WIDTH = RWKV_HEADS * RWKV_HEAD
RWKV_DECAY_LORA = 64
RWKV_AAA_LORA = 64
RWKV_SHIFT_COLS = 3 * RWKV_WIDTH + RWKV_DECAY_LORA + RWKV_AAA_LORA
RWKV_GN_EPS = 64e-5

GLA_HEADS = 4
GLA_KEY_WIDTH = D_MODEL // 2
GLA_WIDTH = D_MODEL
GLA_DK = GLA_KEY_WIDTH // GLA_HEADS
GLA_DV = GLA_WIDTH // GLA_HEADS
GLA_GATE_LORA = 16
GLA_GATE_NORM = 16.0

IN_SIZES = (
    MLA_Q_LORA, MLA_KV_LORA, MLA_ROPE, MLA_WIDTH,
    RWKV_SHIFT_COLS, RWKV_WIDTH,
    GLA_KEY_WIDTH, GLA_KEY_WIDTH, GLA_WIDTH, GLA_GATE_LORA, GLA_WIDTH,
    D_MODEL, D_MODEL, D_MODEL,
)
N_IN = sum(IN_SIZES)

kernel_name = "hybrid_mla_rwkv7_gla_stream_block"


def _split(p, sizes):
    outs = []
    start = 0
    for size in sizes:
        outs.append(p[..., start:start + size])
        start += size
    return outs


def _rms_norm(x, gain, eps=NORM_EPS):
    xf = x.astype(jnp.float32)
    y = xf * lax.rsqrt(jnp.mean(xf * xf, axis=-1, keepdims=True) + eps)
    return (y * gain.astype(jnp.float32)).astype(x.dtype)


def _rope_tables(positions):
    inv_freq = ROPE_BASE ** (-jnp.arange(0, MLA_ROPE, 2, dtype=jnp.float32) / MLA_ROPE)
    ang = positions.astype(jnp.float32)[..., None] * inv_freq
    ang = jnp.concatenate([ang, ang], axis=-1)
    return jnp.cos(ang), jnp.sin(ang)


def _apply_rope(x, cos, sin):
    xf = x.astype(jnp.float32)
    half = xf.shape[-1] // 2
    rot = jnp.concatenate([-xf[..., half:], xf[..., :half]], axis=-1)
    return (xf * cos + rot * sin).astype(x.dtype)


def _chunk_causal_attention(q_nope, q_rope, k_nope, k_rope, v):
    b, s, h, _ = q_nope.shape
    n_blocks = s // Q_BLOCK
    scale = (MLA_NOPE + MLA_ROPE) ** -0.5
    key_chunk = jnp.arange(s) // CHUNK

    def block(args):
        qn, qr, blk = args
        scores = (jnp.einsum('bqhd,bkhd->bhqk', qn, k_nope, preferred_element_type=jnp.float32)
                  + jnp.einsum('bqhr,bkr->bhqk', qr, k_rope, preferred_element_type=jnp.float32))
        q_chunk = (blk * Q_BLOCK + jnp.arange(Q_BLOCK)) // CHUNK
        allowed = key_chunk[None, :] <= q_chunk[:, None]
        scores = jnp.where(allowed, scores * scale, -jnp.inf)
        probs = jax.nn.softmax(scores, axis=-1).astype(v.dtype)
        return jnp.einsum('bhqk,bkhv->bqhv', probs, v)

    qn_b = q_nope.reshape(b, n_blocks, Q_BLOCK, h, MLA_NOPE).transpose(1, 0, 2, 3, 4)
    qr_b = q_rope.reshape(b, n_blocks, Q_BLOCK, h, MLA_ROPE).transpose(1, 0, 2, 3, 4)
    out = lax.map(block, (qn_b, qr_b, jnp.arange(n_blocks)))
    return out.transpose(1, 0, 2, 3, 4).reshape(b, s, h, MLA_V)


def _mla_branch(c_q, c_kv, k_rope, z, cos, sin, q_norm, kv_norm, w_uq, w_ukv, w_o):
    b, s, _ = c_q.shape
    c_q = _rms_norm(c_q, q_norm)
    c_kv = _rms_norm(c_kv, kv_norm)
    q = (c_q @ w_uq).reshape(b, s, MLA_HEADS, MLA_NOPE + MLA_ROPE)
    kv = (c_kv @ w_ukv).reshape(b, s, MLA_HEADS, MLA_NOPE + MLA_V)
    q_nope, q_rope = q[..., :MLA_NOPE], q[..., MLA_NOPE:]
    k_nope, v = kv[..., :MLA_NOPE], kv[..., MLA_NOPE:]
    q_rope = _apply_rope(q_rope, cos[:, :, None, :], sin[:, :, None, :])
    k_rope = _apply_rope(k_rope, cos, sin)
    o = _chunk_causal_attention(q_nope, q_rope, k_nope, k_rope, v).reshape(b, s, MLA_WIDTH)
    return (o * jax.nn.silu(z)) @ w_o


def _rwkv7_branch(streams, z, mu, w0, w2, a0, a2, k_k, k_a, r_k, ln_w, ln_b, w_o):
    b, s, _ = streams.shape
    f32 = jnp.float32
    prev = jnp.pad(streams, ((0, 0), (1, 0), (0, 0)))[:, :-1]
    streams = streams + mu * (prev - streams)
    r, k, v, w_l, a_l = _split(streams, (RWKV_WIDTH, RWKV_WIDTH, RWKV_WIDTH,
                                         RWKV_DECAY_LORA, RWKV_AAA_LORA))
    w = -jax.nn.softplus(-(w0 + jnp.tanh(w_l) @ w2)) - 0.5
    decay = jnp.exp(-jnp.exp(w.astype(f32)))
    a = jax.nn.sigmoid(a0 + a_l @ a2)
    heads = lambda t: t.astype(f32).reshape(b, s, RWKV_HEADS, RWKV_HEAD)
    kk = heads(k * k_k)
    kk = kk / jnp.maximum(jnp.sqrt(jnp.sum(kk * kk, axis=-1, keepdims=True)), 1e-12)
    k = k * (1 + (a - 1) * k_a)
    r_h, k_h, v_h, w_h, a_h = heads(r), heads(k), heads(v), heads(decay), heads(a)

    def step(state, inp):
        r_t, w_t, k_t, v_t, kk_t, a_t = inp
        sa = jnp.einsum('bhvk,bhk->bhv', state, -kk_t)
        state = (state * w_t[:, :, None, :]
                 + sa[..., None] * (kk_t * a_t)[:, :, None, :]
                 + v_t[..., None] * k_t[:, :, None, :])
        return state, jnp.einsum('bhvk,bhk->bhv', state, r_t)

    to_steps = lambda t: t.transpose(1, 0, 2, 3)
    state0 = jnp.zeros((b, RWKV_HEADS, RWKV_HEAD, RWKV_HEAD), f32)
    _, y = lax.scan(step, state0, tuple(to_steps(t) for t in (r_h, w_h, k_h, v_h, kk, a_h)))
    y = y.transpose(1, 0, 2, 3)
    mean = jnp.mean(y, axis=-1, keepdims=True)
    var = jnp.mean(jnp.square(y - mean), axis=-1, keepdims=True)
    y = ((y - mean) * lax.rsqrt(var + RWKV_GN_EPS)).reshape(b, s, RWKV_WIDTH)
    y = y * ln_w.astype(f32) + ln_b.astype(f32)
    bonus = jnp.sum(r_h * k_h * r_k.astype(f32), axis=-1, keepdims=True) * v_h
    y = (y + bonus.reshape(b, s, RWKV_WIDTH)).astype(z.dtype)
    return (y * jax.nn.silu(z)) @ w_o


def _gla_branch(q, k, v, g_l, z, w2, g_bias, norm_g, w_o):
    b, s, _ = q.shape
    nc = s // CHUNK
    f32 = jnp.float32
    q = q.astype(f32).reshape(b, nc, CHUNK, GLA_HEADS, GLA_DK) * (GLA_DK ** -0.5)
    k = k.astype(f32).reshape(b, nc, CHUNK, GLA_HEADS, GLA_DK)
    v = v.astype(f32).reshape(b, nc, CHUNK, GLA_HEADS, GLA_DV)
    log_a = jax.nn.log_sigmoid((g_l @ w2 + g_bias).astype(f32)) / GLA_GATE_NORM
    cum = jnp.cumsum(log_a.reshape(b, nc, CHUNK, GLA_HEADS, GLA_DK), axis=2)
    cum_last = cum[:, :, -1:]
    q_dec = q * jnp.exp(cum)
    k_inv = k * jnp.exp(-cum)
    k_to_end = k * jnp.exp(cum_last - cum)
    attn = jnp.einsum('bnihd,bnjhd->bnhij', q_dec, k_inv)
    attn = jnp.where(jnp.tril(jnp.ones((CHUNK, CHUNK), bool)), attn, 0.0)
    o_intra = jnp.einsum('bnhij,bnjhv->bnihv', attn, v)

    def step(state, inp):
        q_c, k_c, v_c, d_c = inp
        o_c = jnp.einsum('bihd,bhdv->bihv', q_c, state)
        state = state * d_c[..., None] + jnp.einsum('bjhd,bjhv->bhdv', k_c, v_c)
        return state, o_c

    mv = lambda t: jnp.moveaxis(t, 1, 0)
    state0 = jnp.zeros((b, GLA_HEADS, GLA_DK, GLA_DV), f32)
    _, o_inter = lax.scan(step, state0, (mv(q_dec), mv(k_to_end), mv(v), mv(jnp.exp(cum_last[:, :, 0]))))
    o = o_intra + mv(o_inter)
    o = o * lax.rsqrt(jnp.mean(o * o, axis=-1, keepdims=True) + NORM_EPS) * norm_g.astype(f32)
    o = o.reshape(b, s, GLA_WIDTH).astype(z.dtype)
    return (o * jax.nn.silu(z)) @ w_o


def setup_inputs(seed: int = 0) -> dict:
    key = jax.random.key(seed)
    ks = jax.random.split(key, 32)
    L = DEPTH
    f32 = jnp.float32
    nrm = lambda k, shape, sc: jax.random.normal(k, shape, f32) * sc
    gain = lambda k, shape: 1.0 + nrm(k, shape, 0.05)
    positions = (jax.random.randint(ks[2], (BATCH, 1), 0, 4096, dtype=jnp.int32)
                 + jnp.arange(SEQ, dtype=jnp.int32)[None, :])
    return {
        "x": nrm(ks[0], (BATCH, SEQ, D_MODEL), 1.0),
        "c": nrm(ks[1], (BATCH, D_MODEL), 1.0),
        "positions": positions,
        "ada_w": nrm(ks[3], (L, D_MODEL, 3 * D_MODEL), 0.5 * D_MODEL ** -0.5),
        "ada_b": nrm(ks[4], (L, 3 * D_MODEL), 0.02),
        "norm_pre": gain(ks[5], (L, D_MODEL)),
        "norm_post": gain(ks[6], (L, D_MODEL)),
        "w_in": nrm(ks[7], (L, D_MODEL, N_IN), D_MODEL ** -0.5),
        "rwkv_mu": jax.random.uniform(ks[8], (L, RWKV_SHIFT_COLS), f32),
        "mla_q_norm": gain(ks[9], (L, MLA_Q_LORA)),
        "mla_kv_norm": gain(ks[10], (L, MLA_KV_LORA)),
        "mla_w_uq": nrm(ks[11], (L, MLA_Q_LORA, MLA_HEADS * (MLA_NOPE + MLA_ROPE)), MLA_Q_LORA ** -0.5),
        "mla_w_ukv": nrm(ks[12], (L, MLA_KV_LORA, MLA_HEADS * (MLA_NOPE + MLA_V)), MLA_KV_LORA ** -0.5),
        "mla_w_o": nrm(ks[13], (L, MLA_WIDTH, D_MODEL), MLA_WIDTH ** -0.5),
        "rwkv_w0": jax.random.uniform(ks[14], (L, RWKV_WIDTH), f32, -6.0, -1.0),
        "rwkv_w2": nrm(ks[15], (L, RWKV_DECAY_LORA, RWKV_WIDTH), 0.1 * RWKV_DECAY_LORA ** -0.5),
        "rwkv_a0": nrm(ks[16], (L, RWKV_WIDTH), 0.1),
        "rwkv_a2": nrm(ks[17], (L, RWKV_AAA_LORA, RWKV_WIDTH), 0.1 * RWKV_AAA_LORA ** -0.5),
        "rwkv_k_k": 0.85 + nrm(ks[18], (L, RWKV_WIDTH), 0.05),
        "rwkv_k_a": gain(ks[19], (L, RWKV_WIDTH)),
        "rwkv_r_k": nrm(ks[20], (L, RWKV_HEADS, RWKV_HEAD), 0.1),
        "rwkv_ln_w": gain(ks[21], (L, RWKV_WIDTH)),
        "rwkv_ln_b": nrm(ks[22], (L, RWKV_WIDTH), 0.02),
        "rwkv_w_o": nrm(ks[23], (L, RWKV_WIDTH, D_MODEL), RWKV_WIDTH ** -0.5),
        "gla_w2": nrm(ks[24], (L, GLA_GATE_LORA, GLA_KEY_WIDTH), GLA_GATE_LORA ** -0.5),
        "gla_b": nrm(ks[25], (L, GLA_KEY_WIDTH), 0.1),
        "gla_norm": gain(ks[26], (L, GLA_DV)),
        "gla_w_o": nrm(ks[27], (L, GLA_WIDTH, D_MODEL), GLA_WIDTH ** -0.5),
        "w_out": nrm(ks[28], (L, D_MODEL, D_MODEL), D_MODEL ** -0.5),
    }


def reference(x, c, positions, ada_w, ada_b, norm_pre, norm_post, w_in, rwkv_mu,
              mla_q_norm, mla_kv_norm, mla_w_uq, mla_w_ukv, mla_w_o,
              rwkv_w0, rwkv_w2, rwkv_a0, rwkv_a2, rwkv_k_k, rwkv_k_a, rwkv_r_k,
              rwkv_ln_w, rwkv_ln_b, rwkv_w_o,
              gla_w2, gla_b, gla_norm, gla_w_o, w_out):
    cos, sin = _rope_tables(positions)
    c_act = jax.nn.silu(c)
    for l in range(DEPTH):
        mod = c_act @ ada_w[l] + ada_b[l]
        shift, scale, gate = jnp.split(mod, 3, axis=-1)
        h = _rms_norm(x, norm_pre[l]) * (1 + scale[:, None, :]) + shift[:, None, :]
        p = h @ w_in[l]
        (m_cq, m_ckv, m_kr, m_z, r_streams, r_z,
         g_q, g_k, g_v, g_l, g_z, gate_a, gate_b, gate_c) = _split(p, IN_SIZES)
        o_mla = _mla_branch(m_cq, m_ckv, m_kr, m_z, cos, sin, mla_q_norm[l], mla_kv_norm[l],
                            mla_w_uq[l], mla_w_ukv[l], mla_w_o[l])
        o_rwkv = _rwkv7_branch(r_streams, r_z, rwkv_mu[l], rwkv_w0[l], rwkv_w2[l], rwkv_a0[l],
                               rwkv_a2[l], rwkv_k_k[l], rwkv_k_a[l], rwkv_r_k[l],
                               rwkv_ln_w[l], rwkv_ln_b[l], rwkv_w_o[l])
        o_gla = _gla_branch(g_q, g_k, g_v, g_l, g_z, gla_w2[l], gla_b[l], gla_norm[l], gla_w_o[l])
        merged = (jax.nn.sigmoid(gate_a) * o_mla + jax.nn.sigmoid(gate_b) * o_rwkv
                  + jax.nn.sigmoid(gate_c) * o_gla)
        y = merged @ w_out[l]
        x = x + gate[:, None, :] * _rms_norm(y, norm_post[l])
    return x
```

```python
import contextlib
import numpy as np
import concourse.bass as bass
import concourse.mybir as mybir
from concourse.bass_utils import run_bass_kernel_spmd

F32 = mybir.dt.float32
BF16 = mybir.dt.bfloat16
I32 = mybir.dt.int32
AF = mybir.ActivationFunctionType
ALU = mybir.AluOpType
AX = mybir.AxisListType

D = 1024
NB = 2
L = 2
N_IN = 12240
EPS = 1e-6


class V:
    __slots__ = ("ap", "key", "box")

    def __init__(self, ap, key, box):
        self.ap, self.key, self.box = ap, key, box

    def r(self, pat, **kw):
        return V(self.ap.rearrange(pat, **kw), self.key, self.box)

    def ix(self, *idx):
        return V(self.ap[idx], self.key, self.box)


class Buf:
    def __init__(self, name, t, shape):
        self.name, self.t, self.shape = name, t, shape

    def __getitem__(self, idx):
        if not isinstance(idx, tuple):
            idx = (idx, slice(None))
        p, f = idx
        p0, p1, _ = p.indices(self.shape[0])
        f0, f1, _ = f.indices(self.shape[1])
        return V(self.t[p0:p1, f0:f1], self.name, (p0, p1, f0, f1))


def _ovl(a, b):
    return a[0] < b[1] and b[0] < a[1] and a[2] < b[3] and b[2] < a[3]


def _contains(a, b):
    return a[0] <= b[0] and a[1] >= b[1] and a[2] <= b[2] and a[3] >= b[3]


ENGS = ("pe", "act", "dve", "pool", "sp")


class Sched:
    NDMA = 12

    def __init__(self, nc, es):
        self.nc = nc
        self.es = es
        self.ops = []
        self.recs = {}
        self.nops = 0
        self.sem = {e: es.enter_context(nc.semaphore("s_" + e)) for e in ENGS}
        self.cnt = {e: 0 for e in ENGS}
        self.dsem = {q: [es.enter_context(nc.semaphore("d_%s%d" % (q, i))) for i in range(self.NDMA)]
                     for q in ("sp", "pool")}
        self.dn = {"sp": 0, "pool": 0}
        self.done = {}
        self.waited = {e: {} for e in ENGS}
        self.n_inst = 0
        self.eidx = {}
        self.eidx_n = {}

    def _simulate(self, trace):
        sv = self.simv = getattr(self, "simv", {})
        pos = {e_: 0 for e_ in ENGS}
        prog = True
        while prog:
            prog = False
            for e_ in ENGS:
                while pos[e_] < len(trace[e_]):
                    oid, waits, inc = trace[e_][pos[e_]]
                    if all(sv.get(k_, 0) >= v_ for k_, v_ in waits):
                        if inc is not None:
                            sv[inc[0]] = sv.get(inc[0], 0) + inc[1]
                        pos[e_] += 1
                        prog = True
                    else:
                        break
        stuck = {e_: trace[e_][pos[e_]] for e_ in ENGS if pos[e_] < len(trace[e_])}
        if stuck:
            print("DEADLOCK", stuck, {k_: v_ for k_, v_ in sv.items()})
            raise RuntimeError("deadlock in sync plan")

    @contextlib.contextmanager
    def scope(self):
        old = self.es
        with contextlib.ExitStack() as es:
            self.es = es
            try:
                yield
                self.flush()
            finally:
                self.es = old

    def sbuf(self, name, shape, dt):
        self.uid = getattr(self, "uid", 0) + 1
        name = "%s_u%d" % (name, self.uid)
        t = self.es.enter_context(self.nc.sbuf_tensor(name, list(shape), dt))
        return Buf(name, t, shape)

    def psum(self, name, shape, dt):
        t = self.es.enter_context(self.nc.psum_tensor(name, list(shape), dt))
        return Buf(name, t, shape)

    def dram(self, name, shape, dt, kind="Internal"):
        t = self.nc.dram_tensor(name, list(shape), dt, kind=kind)
        return Buf(name, t, shape)

    def _deps(self, reads, writes):
        deps = set()
        for v in reads:
            for rec in self.recs.get(v.key, ()):
                if rec[1] is not None and _ovl(rec[0], v.box):
                    deps.add(rec[1])
        for v in writes:
            for rec in self.recs.get(v.key, ()):
                if _ovl(rec[0], v.box):
                    if rec[1] is not None:
                        deps.add(rec[1])
                    deps.update(rec[2])
        return deps

    def _update(self, oid, reads, writes):
        for v in reads:
            lst = self.recs.setdefault(v.key, [])
            for rec in lst:
                if rec[0] == v.box:
                    rec[2].append(oid)
                    break
            else:
                lst.append([v.box, None, [oid]])
        for v in writes:
            lst = self.recs.setdefault(v.key, [])
            keep = [rec for rec in lst if not _contains(v.box, rec[0])]
            keep.append([v.box, oid, []])
            self.recs[v.key] = keep

    def op(self, eng, fn, reads=(), writes=(), dma=False, ptr=()):
        oid = self.nops
        self.nops += 1
        deps = self._deps(list(reads) + list(ptr), writes)
        hdeps = self._deps(ptr, ()) if ptr else set()
        if eng != "pe" and not dma:
            ei = self.eidx_n.get(eng, 0)
            for d in self._deps(list(reads), ()):
                pe_ = self.eidx.get(d)
                if pe_ is not None and pe_[0] == eng and ei - pe_[1] <= 3:
                    hdeps.add(d)
        self.eidx_n[eng] = self.eidx_n.get(eng, 0) + 1
        self.eidx[oid] = (eng, self.eidx_n[eng] - 1)
        self._update(oid, list(reads) + list(ptr), writes)
        self.ops.append((oid, eng, fn, deps, dma, hdeps))
        return oid

    def dma(self, out, in_, q="sp", **kw):
        return self.op(q, lambda e: e.dma_start(out=out.ap, in_=in_.ap, **kw), [in_], [out], dma=True)

    def mm(self, out, lhsT, rhs, start=True, stop=True):
        return self.op("pe", lambda e: e.matmul(out.ap, lhsT.ap, rhs.ap, start=start, stop=stop),
                       [lhsT, rhs], [out])

    def tr(self, out, in_, ident):
        return self.op("pe", lambda e: e.transpose(out.ap, in_.ap, ident.ap), [in_, ident], [out])

    def act(self, out, in_, func, bias=None, scale=None, accum=None, eng="act"):
        rd = [in_]
        pt = []
        kw = {}
        if bias is not None:
            if isinstance(bias, V):
                pt.append(bias); kw["bias"] = bias.ap
            else:
                kw["bias"] = float(bias)
        if scale is not None:
            if isinstance(scale, V):
                pt.append(scale); kw["scale"] = scale.ap
            else:
                kw["scale"] = float(scale)
        wr = [out]
        if accum is not None:
            wr.append(accum); kw["accum_out"] = accum.ap
        return self.op("act", lambda e: e.activation(out=out.ap, in_=in_.ap, func=func, **kw), rd, wr, ptr=pt)

    def tt(self, out, a, b, op, eng="dve"):
        return self.op(eng, lambda e: e.tensor_tensor(out=out.ap, in0=a.ap, in1=b.ap, op=op), [a, b], [out])

    def ts(self, out, a, s1, op0, s2=None, op1=None, eng="dve", accum=None):
        rd = [a]
        s1a = s1.ap if isinstance(s1, V) else float(s1)
        s2a = None if s2 is None else (s2.ap if isinstance(s2, V) else float(s2))
        pt = []
        if isinstance(s1, V): pt.append(s1)
        if isinstance(s2, V): pt.append(s2)
        kw = {}
        if op1 is not None: kw["op1"] = op1
        wr = [out]
        if accum is not None:
            kw["accum_out"] = accum.ap; wr.append(accum)
        return self.op(eng, lambda e: e.tensor_scalar(out=out.ap, in0=a.ap, scalar1=s1a, scalar2=s2a, op0=op0, **kw),
                       rd, wr, ptr=pt)

    def stt(self, out, a, s, b, op0, op1, eng="dve"):
        rd = [a, b]
        sa = s.ap if isinstance(s, V) else float(s)
        pt = [s] if isinstance(s, V) else []
        return self.op(eng, lambda e: e.scalar_tensor_tensor(out=out.ap, in0=a.ap, scalar=sa, in1=b.ap, op0=op0, op1=op1),
                       rd, [out], ptr=pt)

    def copy(self, out, in_, eng="dve"):
        if eng == "act":
            return self.op("act", lambda e: e.activation(out=out.ap, in_=in_.ap, func=AF.Copy), [in_], [out])
        return self.op(eng, lambda e: e.tensor_copy(out=out.ap, in_=in_.ap), [in_], [out])

    def memset(self, out, val, eng="pool"):
        return self.op(eng, lambda e: e.memset(out.ap, val), [], [out])

    def recip(self, out, in_):
        return self.op("dve", lambda e: e.reciprocal(out=out.ap, in_=in_.ap), [in_], [out])

    def flush(self):
        ops = self.ops
        self.ops = []
        if not ops:
            return
        eng_of = {o[0]: o[1] for o in ops}
        need = set()
        for oid, eng, fn, deps, dma, hdeps in ops:
            for d in deps:
                if d in self.done:
                    continue
                if d in eng_of and (eng_of[d] != eng or dma or d in hdeps):
                    need.add(d)
        last = {}
        for oid, eng, fn, deps, dma, hdeps in ops:
            last[eng] = oid
        need.update(last.values())
        plan = {e: [] for e in ENGS}
        for oid, eng, fn, deps, dma, hdeps in ops:
            if dma:
                i = self.dn[eng]
                self.dn[eng] += 1
                tok = ("dma", eng, i % self.NDMA, 16 * (i // self.NDMA + 1), i)
                self.done[oid] = tok
            elif oid in need:
                self.cnt[eng] += 1
                tok = ("eng", eng, self.cnt[eng])
                self.done[oid] = tok
            else:
                tok = None
                self.done[oid] = ("impl", eng)
            plan[eng].append((oid, fn, deps, dma, tok, hdeps))
        nc = self.nc
        trace = {e_: [] for e_ in ENGS}
        self._trace = trace
        with nc.Block() as block:
            def mk(eng):
                def body(e):
                    wd = self.waited[eng]
                    for oid, fn, deps, dma, tok, hdeps in plan[eng]:
                        waits = {}
                        for d in deps:
                            dt = self.done.get(d)
                            if dt is None or dt[0] == "impl":
                                continue
                            if dt[0] == "eng":
                                if dt[1] == eng and not dma and d not in hdeps:
                                    continue
                                s = self.sem[dt[1]]
                                val = dt[2]
                                k = ("e", dt[1])
                            else:
                                s = self.dsem[dt[1]][dt[2]]
                                val = dt[3]
                                k = ("d", dt[1], dt[2])
                            if wd.get(k, 0) >= val:
                                continue
                            if waits.get(k, (None, 0))[1] < val:
                                waits[k] = (s, val)
                        if dma:
                            i = tok[4]
                            if i >= self.NDMA:
                                k = ("d", eng, tok[2])
                                val = tok[3] - 16
                                if wd.get(k, 0) < val and waits.get(k, (None, 0))[1] < val:
                                    waits[k] = (self.dsem[eng][tok[2]], val)
                        for k, (s, val) in waits.items():
                            e.wait_ge(s, val)
                            wd[k] = val
                            self.n_inst += 1
                        ins = fn(e)
                        self.n_inst += 1
                        inc = None
                        if tok is not None:
                            if tok[0] == "dma":
                                ins.then_inc(self.dsem[eng][tok[2]], 16)
                                inc = (("d", eng, tok[2]), 16)
                            else:
                                ins.then_inc(self.sem[eng], 1)
                                inc = (("e", eng), 1)
                        trace[eng].append((oid, [(k_, v_[1]) for k_, v_ in waits.items()], inc))
                    if eng in ("sp", "pool"):
                        n = self.dn[eng]
                        for j in range(self.NDMA):
                            if n > j:
                                val = 16 * ((n - 1 - j) // self.NDMA + 1)
                                k = ("d", eng, j)
                                if wd.get(k, 0) < val:
                                    e.wait_ge(self.dsem[eng][j], val)
                                    wd[k] = val
                return body
            block.tensor(mk("pe"))
            block.scalar(mk("act"))
            block.vector(mk("dve"))
            block.gpsimd(mk("pool"))
            block.sync(mk("sp"))
        if getattr(self, "check", False):
            self._simulate(trace)
        self.recs = {}
        self.done = {}
        self.eidx = {}


SEGS = [
    ("cq", 0, 512, "copy"), ("ckv", 512, 256, "copy"), ("kr", 768, 64, "copy"), ("mz", 832, 1024, "silu"),
    ("rr", 1856, 1024, "shift"), ("rk", 2880, 1024, "shift"), ("rv", 3904, 1024, "shift"),
    ("rwa", 4928, 128, "shift"), ("rz", 5056, 1024, "silu"),
    ("gq", 6080, 512, "copy"), ("gk", 6592, 512, "copy"), ("gv", 7104, 1024, "copy"), ("gl", 8128, 16, "copy"),
    ("gz", 8144, 1024, "silu"), ("ga", 9168, 1024, "sigmoid"), ("gb", 10192, 1024, "sigmoid"),
    ("gc", 11216, 1024, "sigmoid"),
]
SEG_ROW = {}
_r = 0
for _n, _c, _w, _e in SEGS:
    SEG_ROW[_n] = _r
    _r += _w
PT_ROWS = _r


class K:
    pass


def v3(v, pat, **kw):
    return v.r(pat, **kw)


def build(T=4096, dbg=False, phases=("mla", "rwkv", "gla"), nl=L):
    nc = bass.Bass("TRN2", target_bir_lowering=False)
    k = K()
    k.phases = phases
    import os
    k.rstage = float(os.environ.get('K_RSTAGE', '3'))
    k.ne = int(os.environ.get('K_NE', '2'))
    k.lvx = int(os.environ.get('K_LVX', '3'))
    k.dhb = int(os.environ.get('K_DHB', '7'))
    k.nl = nl
    k.dbg = dbg
    k.dumped = set()
    k.T = T
    es = contextlib.ExitStack()
    S = Sched(nc, es)
    k.S = S
    ext_in = lambda name, shape, dt=F32: S.dram(name, shape, dt, kind="ExternalInput")
    k.x_in = ext_in("x", [NB * T, D])
    k.c_in = ext_in("c", [NB, D])
    k.pos_in = ext_in("positions", [NB, T], I32)
    k.ada_w = ext_in("ada_w", [L * D, 3 * D])
    k.ada_b = ext_in("ada_b", [L, 3 * D])
    k.norm_pre = ext_in("norm_pre", [L, D])
    k.norm_post = ext_in("norm_post", [L, D])
    k.w_in = ext_in("w_in", [L * D, N_IN])
    k.rwkv_mu = ext_in("rwkv_mu", [L * 3200, 1])
    k.mla_q_norm = ext_in("mla_q_norm", [L * 512, 1])
    k.mla_kv_norm = ext_in("mla_kv_norm", [L * 256, 1])
    k.mla_w_uq = ext_in("mla_w_uq", [L * 512, 1536])
    k.mla_w_ukv = ext_in("mla_w_ukv", [L * 256, 2048])
    k.mla_w_o = ext_in("mla_w_o", [L * 1024, 1024])
    k.rwkv_w0 = ext_in("rwkv_w0", [L * 1024, 1])
    k.rwkv_w2 = ext_in("rwkv_w2", [L * 64, 1024])
    k.rwkv_a0 = ext_in("rwkv_a0", [L * 1024, 1])
    k.rwkv_a2 = ext_in("rwkv_a2", [L * 64, 1024])
    k.rwkv_k_k = ext_in("rwkv_k_k", [L * 1024, 1])
    k.rwkv_k_a = ext_in("rwkv_k_a", [L * 1024, 1])
    k.rwkv_r_k = ext_in("rwkv_r_k", [L * 1024, 1])
    k.rwkv_ln_w = ext_in("rwkv_ln_w", [L * 1024, 1])
    k.rwkv_ln_b = ext_in("rwkv_ln_b", [L * 1024, 1])
    k.rwkv_w_o = ext_in("rwkv_w_o", [L * 1024, 1024])
    k.gla_w2 = ext_in("gla_w2", [L * 16, 512])
    k.gla_b = ext_in("gla_b", [L * 512, 1])
    k.gla_norm = ext_in("gla_norm", [L * 256, 1])
    k.gla_w_o = ext_in("gla_w_o", [L * 1024, 1024])
    k.w_out = ext_in("w_out", [L * 1024, 1024])
    k.rope_freq = ext_in("rope_freq", [64, 1])
    k.out = S.dram("out", [NB * T, D], F32, kind="ExternalOutput")
    okind = "ExternalOutput" if dbg else "Internal"
    k.xs = S.dram("xs", [NB * T, D], F32)
    k.mod = S.dram("modd", [L * NB, 3 * D], F32, kind=okind)
    k.PT = S.dram("PT", [PT_ROWS, T], F32, kind=okind)
    k.osz = [S.dram("osz%d" % i, [D, T], BF16, kind=okind) for i in range(3)]

    k.ident = S.sbuf("ident", [128, 128], BF16)
    k.ones = S.sbuf("onesb", [128, 128], BF16)
    k.identf = S.sbuf("identf", [128, 128], F32)
    S.memset(k.ones[:, :], 1.0)
    S.memset(k.identf[:, :], 1.0)
    S.op("pool", lambda e: e.affine_select(out=k.identf[:, :].ap, in_=k.identf[:, :].ap, pattern=[[1, 128]],
                                           compare_op=ALU.is_equal, fill=0.0, base=0, channel_multiplier=-1),
         [k.identf[:, :]], [k.identf[:, :]])
    S.copy(k.ident[:, :], k.identf[:, :], eng="pool")
    k.ps = [S.psum("ps%d" % i, [128, 512], F32) for i in range(7)]
    _pb0 = S.psum("pb0", [128, 1024], BF16)
    k.pb = [_pb0, _pb0]
    k_eps(S, k)
    lin_consts(S, k)
    S.flush()

    phase_mod(S, k)
    for l in range(nl):
        for b in range(NB):
            phase_norm(S, k, l, b)
            phase_inproj(S, k, l, b)
            if "mla" in k.phases:
                phase_mla(S, k, l, b)
            if "rwkv" in k.phases:
                phase_rwkv(S, k, l, b)
            if "gla" in k.phases:
                phase_gla(S, k, l, b)
            phase_final(S, k, l, b)
    S.flush()
    es.close()
    k.n_inst = S.n_inst
    return nc, k


def phase_mod(S, k):
    with S.scope():
        cT = S.sbuf("cT", [128, 8 * NB], F32)
        sT = S.sbuf("sT", [128, 8 * NB], F32)
        for b in range(NB):
            S.dma(cT[:, :].r("p (c b) -> p c b", b=NB).ix(slice(None), slice(None), slice(b, b + 1)),
                  V(k.c_in.t[b:b + 1, :].rearrange("o (c k) -> k c o", k=128), "c", (b, b + 1, 0, D)),
                  allow_slow_non_contiguous=True)
        S.act(sT[:, :], cT[:, :], AF.Silu)
        wa = [S.sbuf("wa%d" % i, [128, 8 * 512], F32) for i in range(2)]
        bb = S.sbuf("adab", [NB, 3 * D], F32)
        msb = S.sbuf("msb", [NB, 3 * D], F32)
        i = 0
        for l in range(L):
            S.dma(bb[:, :], V(k.ada_b.t[l:l + 1, :].partition_broadcast(NB), "ada_b", (l, l + 1, 0, 3 * D)))
            for cc in range(6):
                w = wa[i % 2]
                S.dma(w[:, :].r("p (c n) -> p c n", c=8),
                      V(k.ada_w.t[l * D:(l + 1) * D, cc * 512:(cc + 1) * 512].rearrange("(c k) n -> k c n", k=128),
                        "ada_w", (l * D, (l + 1) * D, cc * 512, (cc + 1) * 512)))
                ps = k.ps[i % 2]
                for kc in range(8):
                    S.mm(ps[0:NB, :], sT[:, kc * NB:(kc + 1) * NB], w[:, kc * 512:(kc + 1) * 512],
                         start=(kc == 0), stop=(kc == 7))
                S.tt(msb[:, cc * 512:(cc + 1) * 512], ps[0:NB, :], bb[:, cc * 512:(cc + 1) * 512], ALU.add)
                i += 1
            S.dma(k.mod[l * NB:(l + 1) * NB, :], msb[:, :], q="pool")


def bcast_row(buf, r, c0, c1, npart=128):
    return V(buf.t[r:r + 1, c0:c1].partition_broadcast(npart), buf.name, (r, r + 1, c0, c1))


def phase_norm(S, k, l, b):
    T = k.T
    k.es_hT = contextlib.ExitStack()
    old = S.es
    S.es = k.es_hT
    k.hT = S.sbuf("hT", [128, 8 * T], BF16)
    S.es = old
    with S.scope():
        Gb = S.sbuf("Gb", [128, D], F32)
        npb = S.sbuf("npb", [128, D], F32)
        shb = S.sbuf("shb", [128, D], F32)
        S.dma(Gb[:, :], bcast_row(k.mod, l * NB + b, D, 2 * D))
        S.dma(npb[:, :], bcast_row(k.norm_pre, l, 0, D))
        S.dma(shb[:, :], bcast_row(k.mod, l * NB + b, 0, D))
        S.stt(Gb[:, :], Gb[:, :], 1.0, npb[:, :], ALU.add, ALU.mult)
        xin = k.x_in if l == 0 else k.xs
        xt = [S.sbuf("xt%d" % i, [128, D], F32) for i in range(3)]
        junk = S.sbuf("junk", [128, D], F32)
        tmp = [S.sbuf("tmp%d" % i, [128, D], F32) for i in range(2)]
        hb = [S.sbuf("hb%d" % i, [128, D], BF16) for i in range(2)]
        st = [S.sbuf("st%d" % i, [128, 4], F32) for i in range(2)]
        for tt in range(T // 128):
            x = xt[tt % 3]
            s = st[tt % 2]
            S.dma(x[:, :], xin[b * T + tt * 128: b * T + (tt + 1) * 128, :])
            S.act(junk[:, :], x[:, :], AF.Square, accum=s[:, 0:1])
            S.act(s[:, 1:2], s[:, 0:1], AF.Sqrt, scale=1.0 / D, bias=k_eps(S, k))
            S.recip(s[:, 2:3], s[:, 1:2])
            S.stt(tmp[tt % 2][:, :], x[:, :], s[:, 2:3], Gb[:, :], ALU.mult, ALU.mult)
            S.tt(hb[tt % 2][:, :], tmp[tt % 2][:, :], shb[:, :], ALU.add, eng="pool")
            pb = k.pb[tt % 2]
            for c in range(8):
                S.tr(pb[:, c * 128:(c + 1) * 128], hb[tt % 2][:, c * 128:(c + 1) * 128], k.ident[:, :])
            dst = V(k.hT.t[:, :].rearrange("p (c t) -> p c t", c=8)[:, :, tt * 128:(tt + 1) * 128], "hT",
                    (0, 128, tt * 128, (tt + 1) * 128))
            if tt % 2 == 0:
                S.op("act", lambda e, dst=dst, pb=pb: e.activation(out=dst.ap, in_=pb[:, :].ap.rearrange("p (c t) -> p c t", c=8), func=AF.Copy),
                     [pb[:, :]], [dst])
            else:
                S.op("dve", lambda e, dst=dst, pb=pb: e.tensor_copy(out=dst.ap, in_=pb[:, :].ap.rearrange("p (c t) -> p c t", c=8)),
                     [pb[:, :]], [dst])


def k_eps(S, k):
    if not hasattr(k, "epsc"):
        k.epsc = S.sbuf("epsc", [128, 4], F32)
        S.memset(k.epsc[:, 0:1], EPS)
        S.memset(k.epsc[:, 1:2], 64e-5)
        S.memset(k.epsc[:, 2:3], 1.0)
        S.memset(k.epsc[:, 3:4], 0.0)
    return k.epsc[:, 0:1]


def phase_inproj(S, k, l, b):
    T = k.T
    blocks = []
    for name, c0, w, epi in SEGS:
        for j in range(0, w, 128):
            bw = min(128, w - j)
            blocks.append((name, SEG_ROW[name] + j, c0 + j, bw, epi))
    with S.scope():
        wf = [S.sbuf("wf%d" % i, [128, 8 * 128], F32) for i in range(2)]
        wb = [S.sbuf("wb%d" % i, [128, 8 * 128], BF16) for i in range(2)]
        raw = [S.sbuf("raw%d" % i, [128, T + 1], F32) for i in range(2)]
        stg = [S.sbuf("stg%d" % i, [128, 512], F32) for i in range(4)]
        tmp = [S.sbuf("ptmp%d" % i, [128, 512], F32) for i in range(2)]
        mu = [S.sbuf("mu%d" % i, [128, 2], F32) for i in range(2)]
        for r in raw:
            S.memset(r[:, 0:1], 0.0)
        ri = 0
        si = 0
        pi = 0
        ei = 0
        for bi, (name, row, col, bw, epi) in enumerate(blocks):
            f, w = wf[bi % 2], wb[bi % 2]
            S.dma(f[:, 0:8 * bw].r("p (c n) -> p c n", c=8),
                  V(k.w_in.t[l * D:(l + 1) * D, col:col + bw].rearrange("(c k) n -> k c n", k=128), "w_in",
                    (l * D, (l + 1) * D, col, col + bw)))
            S.copy(w[:, 0:8 * bw], f[:, 0:8 * bw], eng="pool")
            if epi == "shift":
                m = mu[ri % 2]
                rw = raw[ri % 2]
                ri += 1
                mo = col - 1856
                S.dma(m[0:bw, 0:1], k.rwkv_mu[l * 3200 + mo: l * 3200 + mo + bw, 0:1])
                S.ts(m[0:bw, 1:2], m[0:bw, 0:1], -1.0, ALU.mult, 1.0, ALU.add, eng="pool")
            for tc in range(T // 512):
                ps = k.ps[pi % 6]
                pi += 1
                for kc in range(8):
                    S.mm(ps[0:bw, :], w[:, kc * bw:(kc + 1) * bw], k.hT[:, kc * T + tc * 512: kc * T + (tc + 1) * 512],
                         start=(kc == 0), stop=(kc == 7))
                st = stg[si % 4]
                si += 1
                if epi == "shift":
                    S.copy(rw[0:bw, 1 + tc * 512: 1 + (tc + 1) * 512], ps[0:bw, :], eng="act")
                    tp = tmp[tc % 2]
                    S.ts(tp[0:bw, :], rw[0:bw, 1 + tc * 512: 1 + (tc + 1) * 512], m[0:bw, 1:2], ALU.mult)
                    S.stt(st[0:bw, :], rw[0:bw, tc * 512: (tc + 1) * 512], m[0:bw, 0:1], tp[0:bw, :], ALU.mult, ALU.add)
                elif epi == "copy":
                    S.copy(st[0:bw, :], ps[0:bw, :], eng=("act" if ei % 2 else "dve"))
                    ei += 1
                else:
                    fn = {"silu": AF.Silu, "sigmoid": AF.Sigmoid}[epi]
                    S.act(st[0:bw, :], ps[0:bw, :], fn)
                S.dma(k.PT[row:row + bw, tc * 512:(tc + 1) * 512], st[0:bw, :], q="pool")
    k.es_hT.close()


def phase_final(S, k, l, b):
    T = k.T
    with S.scope():
        wts = []
        wf = [S.sbuf("fwf%d" % i, [128, 8 * 128], F32) for i in range(2)]
        srcs = [k.mla_w_o, k.rwkv_w_o, k.gla_w_o, k.w_out]
        i = 0
        for wi, src in enumerate(srcs):
            wbuf = S.sbuf("fw%d" % wi, [128, 8 * 1024], BF16)
            wts.append(wbuf)
            for q8 in range(8):
                f = wf[i % 2]
                i += 1
                S.dma(f[:, :].r("p (c n) -> p c n", c=8),
                      V(src.t[l * D:(l + 1) * D, q8 * 128:(q8 + 1) * 128].rearrange("(c k) n -> k c n", k=128),
                        src.name, (l * D, (l + 1) * D, q8 * 128, (q8 + 1) * 128)))
                dst = V(wbuf.t[:, :].rearrange("p (c n) -> p c n", c=8)[:, :, q8 * 128:(q8 + 1) * 128], wbuf.name,
                        (0, 128, 0, 8 * 1024))
                S.op("pool", lambda e, dst=dst, f=f: e.tensor_copy(out=dst.ap, in_=f[:, :].ap.rearrange("p (c n) -> p c n", c=8)),
                     [f[:, :]], [dst])
        GN = S.sbuf("GN", [128, D], F32)
        npb = S.sbuf("fnpb", [128, D], F32)
        S.dma(GN[:, :], bcast_row(k.mod, l * NB + b, 2 * D, 3 * D))
        S.dma(npb[:, :], bcast_row(k.norm_post, l, 0, D))
        S.tt(GN[:, :], GN[:, :], npb[:, :], ALU.mult)
        osz = [[S.sbuf("fo%d_%d" % (x, i), [128, 8 * 512], BF16) for i in range(1)] for x in range(3)]
        gt = [[S.sbuf("fg%d_%d" % (x, i), [128, 512], F32) for i in range(2)] for x in range(3)]
        mg = [S.sbuf("fmg%d" % i, [128, 8 * 512], BF16) for i in range(2)]
        ma = [S.sbuf("fma%d" % i, [128, 512], F32) for i in range(2)]
        mb = [S.sbuf("fmb%d" % i, [128, 512], F32) for i in range(2)]
        xt = [S.sbuf("fx%d" % i, [128, D], F32) for i in range(2)]
        xo = [S.sbuf("fxo%d" % i, [128, D], F32) for i in range(2)]
        junk = S.sbuf("fjunk", [128, 512], F32)
        st = [S.sbuf("fst%d" % i, [128, 8], F32) for i in range(2)]
        xin = k.x_in if l == 0 else k.xs
        xout = k.xs if l < k.nl - 1 else k.out
        gi = 0
        pi = 0
        for tc in range(T // 512):
            for x in range(3):
                S.dma(osz[x][0][:, :].r("p (c t) -> p c t", c=8),
                      V(k.osz[x].t[:, tc * 512:(tc + 1) * 512].rearrange("(c k) t -> k c t", k=128), k.osz[x].name,
                        (0, D, tc * 512, (tc + 1) * 512)))
            m = mg[tc % 2]
            for ob in range(8):
                pss = []
                for x in range(3):
                    ps = k.ps[pi % 6]
                    pi += 1
                    pss.append(ps)
                    for kc in range(8):
                        S.mm(ps[:, :], wts[x][:, kc * 1024 + ob * 128: kc * 1024 + (ob + 1) * 128],
                             osz[x][0][:, kc * 512:(kc + 1) * 512], start=(kc == 0), stop=(kc == 7))
                g = [gt[x][gi % 2] for x in range(3)]
                gi += 1
                for x, nm in enumerate(("ga", "gb", "gc")):
                    r0 = SEG_ROW[nm] + ob * 128
                    S.dma(g[x][:, :], k.PT[r0:r0 + 128, tc * 512:(tc + 1) * 512])
                a_, b_ = ma[ob % 2], mb[ob % 2]
                S.tt(a_[:, :], pss[0][:, :], g[0][:, :], ALU.mult)
                S.tt(b_[:, :], pss[1][:, :], g[1][:, :], ALU.mult)
                S.tt(a_[:, :], a_[:, :], b_[:, :], ALU.add, eng="pool")
                S.tt(b_[:, :], pss[2][:, :], g[2][:, :], ALU.mult)
                S.tt(m[:, ob * 512:(ob + 1) * 512], a_[:, :], b_[:, :], ALU.add, eng="pool")
            for tb in range(4):
                tg = tc * 4 + tb
                x_ = xt[tg % 2]
                o_ = xo[tg % 2]
                s = st[tg % 2]
                S.dma(x_[:, :], xin[b * T + tg * 128: b * T + (tg + 1) * 128, :])
                pss = []
                for half in range(2):
                    ps = k.ps[pi % 6]
                    pi += 1
                    pss.append(ps)
                    for ob in range(8):
                        S.mm(ps[:, :], m[:, ob * 512 + tb * 128: ob * 512 + (tb + 1) * 128],
                             wts[3][:, ob * 1024 + half * 512: ob * 1024 + (half + 1) * 512],
                             start=(ob == 0), stop=(ob == 7))
                    S.act(junk[:, :], ps[:, :], AF.Square, accum=s[:, half:half + 1])
                S.tt(s[:, 2:3], s[:, 0:1], s[:, 1:2], ALU.add)
                S.act(s[:, 3:4], s[:, 2:3], AF.Sqrt, scale=1.0 / D, bias=k_eps(S, k))
                S.recip(s[:, 4:5], s[:, 3:4])
                for half in range(2):
                    sl = slice(half * 512, (half + 1) * 512)
                    S.stt(o_[:, sl], pss[half][:, :], s[:, 4:5], GN[:, sl], ALU.mult, ALU.mult)
                    S.tt(o_[:, sl], o_[:, sl], x_[:, sl], ALU.add, eng="pool")
                S.dma(xout[b * T + tg * 128: b * T + (tg + 1) * 128, :], o_[:, :], q="pool")


def dump(S, k, name, buf, dt):
    if not k.dbg or name in k.dumped:
        return
    k.dumped.add(name)
    d = S.dram("dbg_" + name, list(buf.shape), dt, kind="ExternalOutput")
    S.dma(d[:, :], buf[:, :], q="pool")


def dview(buf, r0, r1, c0, c1, pat=None, **kw):
    ap = buf.t[r0:r1, c0:c1]
    if pat:
        ap = ap.rearrange(pat, **kw)
    return V(ap, buf.name, (r0, r1, c0, c1))


def col_load(S, dst, src, r0, n, q="sp"):
    S.dma(dst, dview(src, r0, r0 + n * 128, 0, 1, "(c k) o -> k (c o)", k=128), q=q, allow_slow_non_contiguous=True)


def rope_tables(S, k, b, cosT, sinT):
    T = k.T
    with S.scope():
        ff = S.sbuf("ff", [64, 2], F32)
        S.dma(ff[:, 0:1], k.rope_freq[:, :])
        pi_ = S.sbuf("posi", [64, T], I32)
        ang = S.sbuf("ang", [64, T], F32)
        kf = S.sbuf("kf", [64, T], F32)
        S.dma(pi_[:, :], bcast_row(k.pos_in, b, 0, T, 64))
        S.copy(ang[:, :], pi_[:, :])
        S.ts(ang[:, :], ang[:, :], ff[:, 0:1], ALU.mult)
        TWO_PI = float(2 * np.pi)
        MAGIC = 12582912.0
        PI_LO = 3.1415925
        for which, dst in ((0, sinT), (1, cosT)):
            off = 0.0 if which == 0 else float(np.pi / 2)
            S.ts(dst[:, :], ang[:, :], off, ALU.add)
            S.ts(kf[:, :], dst[:, :], 1.0 / TWO_PI, ALU.mult)
            S.ts(kf[:, :], kf[:, :], MAGIC, ALU.add)
            S.ts(kf[:, :], kf[:, :], MAGIC, ALU.subtract)
            S.stt(dst[:, :], kf[:, :], -TWO_PI, dst[:, :], ALU.mult, ALU.add)
            S.ts(dst[:, :], dst[:, :], PI_LO, ALU.min, -PI_LO, ALU.max)
            S.act(dst[:, :], dst[:, :], AF.Sin)
        S.ts(sinT[0:32, :], sinT[0:32, :], -1.0, ALU.mult)


def phase_mla(S, k, l, b):
    T = k.T
    nt = T // 512
    nblk = T // 128
    SC = float((128 + 64) ** -0.5)
    with S.scope():
        cqn = S.sbuf("cqn", [128, 4 * T], BF16)
        ckvn = S.sbuf("ckvn", [128, 2 * T], BF16)
        krT = S.sbuf("krT", [64, T], BF16)
        cosT = S.sbuf("cosT", [64, T], F32)
        sinT = S.sbuf("sinT", [64, T], F32)
        gq = S.sbuf("gq", [128, 8], F32)
        col_load(S, gq[:, 0:4], k.mla_q_norm, l * 512, 4)
        col_load(S, gq[:, 4:6], k.mla_kv_norm, l * 256, 2)
        rope_tables(S, k, b, cosT, sinT)
        with S.scope():
            cfb = [S.sbuf("cf%d" % i, [128, 4 * 512], F32) for i in range(2)]
            sqb = [S.sbuf("sq%d" % i, [128, 4 * 512], BF16) for i in range(2)]
            sd = [S.sbuf("sd%d" % i, [128, 512], F32) for i in range(2)]
            i = 0
            for name, nb_, dst, gcol in (("cq", 4, cqn, 0), ("ckv", 2, ckvn, 4)):
                r0 = SEG_ROW[name]
                for tc in range(nt):
                    cf, sq, s_ = cfb[i % 2], sqb[i % 2], sd[i % 2]
                    ps = k.ps[i % 2]
                    i += 1
                    S.dma(cf[:, 0:nb_ * 512].r("p (c t) -> p c t", c=nb_),
                          dview(k.PT, r0, r0 + nb_ * 128, tc * 512, (tc + 1) * 512, "(c k) t -> k c t", k=128))
                    S.act(sq[:, 0:nb_ * 512], cf[:, 0:nb_ * 512], AF.Square)
                    for c in range(nb_):
                        S.mm(ps[:, :], k.ones[:, :], sq[:, c * 512:(c + 1) * 512], start=(c == 0), stop=(c == nb_ - 1))
                    S.act(s_[:, :], ps[:, :], AF.Sqrt, scale=1.0 / (nb_ * 128), bias=k.epsc[:, 0:1])
                    S.recip(s_[:, :], s_[:, :])
                    for c in range(nb_):
                        S.stt(dst[:, c * T + tc * 512: c * T + (tc + 1) * 512], cf[:, c * 512:(c + 1) * 512],
                              gq[:, gcol + c:gcol + c + 1], s_[:, :], ALU.mult, ALU.mult)
            r0 = SEG_ROW["kr"]
            for tc in range(nt):
                cf = cfb[tc % 2]
                sl = slice(tc * 512, (tc + 1) * 512)
                S.dma(cf[0:64, 0:512], k.PT[r0:r0 + 64, sl])
                S.dma(cf[0:32, 512:1024], k.PT[r0 + 32:r0 + 64, sl])
                S.dma(cf[32:64, 512:1024], k.PT[r0:r0 + 32, sl])
                S.tt(cf[0:64, 1024:1536], cf[0:64, 0:512], cosT[:, sl], ALU.mult)
                S.tt(cf[0:64, 1536:2048], cf[0:64, 512:1024], sinT[:, sl], ALU.mult, eng="pool")
                S.tt(krT[:, sl], cf[0:64, 1024:1536], cf[0:64, 1536:2048], ALU.add)
        QN = S.sbuf("QN", [128, T], BF16)
        QR = S.sbuf("QR", [64, T], BF16)
        KN = S.sbuf("KN", [128, T], BF16)
        VA = S.sbuf("VA", [128, nblk * 130], BF16)
        S.op("pool", lambda e: e.memset(VA[:, :].ap.rearrange("p (n c) -> p n c", c=130)[:, :, 128:130], 1.0), [], [VA[:, :]])
        wqf = S.sbuf("wqf", [128, 4 * 256], F32)
        wqb = S.sbuf("wqb", [128, 4 * 256], BF16)
        wkf = S.sbuf("wkf", [128, 2 * 256], F32)
        wkb = S.sbuf("wkb", [128, 2 * 256], BF16)
        rt = [S.sbuf("rt%d" % i, [64, 512], F32) for i in range(2)]
        pt = [S.sbuf("ptt%d" % i, [128, 512], BF16) for i in range(4)]
        mz = [S.sbuf("mzz%d" % i, [128, 512], F32) for i in range(2)]
        ost = [S.sbuf("ost%d" % i, [128, 512], BF16) for i in range(2)]
        on = [S.sbuf("on%d" % i, [128, 128], BF16) for i in range(2)]
        rl = [S.sbuf("rl%d" % i, [128, 1], F32) for i in range(2)]
        ui = 0
        oi = 0
        for h in range(8):
            wq3 = wqf[:, :].r("p (c n) -> p c n", c=4)
            c0 = h * 192
            S.dma(wq3.ix(slice(None), slice(None), slice(0, 192)),
                  dview(k.mla_w_uq, l * 512, (l + 1) * 512, c0, c0 + 192, "(c k) n -> k c n", k=128))
            S.dma(wq3.ix(slice(None), slice(None), slice(192, 224)),
                  dview(k.mla_w_uq, l * 512, (l + 1) * 512, c0 + 160, c0 + 192, "(c k) n -> k c n", k=128))
            S.dma(wq3.ix(slice(None), slice(None), slice(224, 256)),
                  dview(k.mla_w_uq, l * 512, (l + 1) * 512, c0 + 128, c0 + 160, "(c k) n -> k c n", k=128))
            S.copy(wqb[:, :], wqf[:, :], eng="pool")
            S.dma(wkf[:, :].r("p (c n) -> p c n", c=2),
                  dview(k.mla_w_ukv, l * 256, (l + 1) * 256, h * 256, (h + 1) * 256, "(c k) n -> k c n", k=128))
            S.copy(wkb[:, :], wkf[:, :], eng="pool")
            for tc in range(nt):
                sl = slice(tc * 512, (tc + 1) * 512)
                p0, p1, p2, p3, p4 = k.ps[0], k.ps[1], k.ps[2], k.ps[3], k.ps[4]
                for kc in range(4):
                    S.mm(p0[:, :], wqb[:, kc * 256:kc * 256 + 128], cqn[:, kc * T + tc * 512:kc * T + (tc + 1) * 512],
                         start=(kc == 0), stop=(kc == 3))
                S.copy(QN[:, sl], p0[:, :], eng="act")
                for kc in range(4):
                    S.mm(p1[0:64, :], wqb[:, kc * 256 + 128:kc * 256 + 192], cqn[:, kc * T + tc * 512:kc * T + (tc + 1) * 512],
                         start=(kc == 0), stop=(kc == 3))
                for kc in range(4):
                    S.mm(p2[0:64, :], wqb[:, kc * 256 + 192:kc * 256 + 256], cqn[:, kc * T + tc * 512:kc * T + (tc + 1) * 512],
                         start=(kc == 0), stop=(kc == 3))
                S.tt(rt[0][:, :], p1[0:64, :], cosT[:, sl], ALU.mult)
                S.tt(rt[1][:, :], p2[0:64, :], sinT[:, sl], ALU.mult)
                S.tt(QR[:, sl], rt[0][:, :], rt[1][:, :], ALU.add, eng="pool")
                for kc in range(2):
                    S.mm(p3[:, :], wkb[:, kc * 256:kc * 256 + 128], ckvn[:, kc * T + tc * 512:kc * T + (tc + 1) * 512],
                         start=(kc == 0), stop=(kc == 1))
                S.copy(KN[:, sl], p3[:, :], eng="act")
                for tb in range(4):
                    t0 = tc * 512 + tb * 128
                    for kc in range(2):
                        S.mm(p4[:, tb * 128:(tb + 1) * 128], ckvn[:, kc * T + t0:kc * T + t0 + 128],
                             wkb[:, kc * 256 + 128:kc * 256 + 256], start=(kc == 0), stop=(kc == 1))
                dst = V(VA.t[:, tc * 4 * 130:(tc + 1) * 4 * 130].rearrange("p (n c) -> p n c", c=130)[:, :, 0:128], VA.name,
                        (0, 128, tc * 4 * 130, (tc + 1) * 4 * 130))
                S.op("dve", lambda e, dst=dst, p4=p4: e.tensor_copy(out=dst.ap, in_=p4[:, :].ap.rearrange("p (n c) -> p n c", c=128)),
                     [p4[:, :]], [dst])
            if h == 7 and l == k.nl - 1 and b == 1:
                for nm, bf in (("QN", QN), ("QR", QR), ("KN", KN), ("VA", VA), ("krT", krT), ("cqn", cqn), ("ckvn", ckvn)):
                    dump(S, k, nm, bf, BF16)
                dump(S, k, "cosT", cosT, F32)
                dump(S, k, "sinT", sinT, F32)
            units = [(qc, kb) for qc in range(nt) for kb in range(4 * qc + 4)]
            pend = []

            def pv(qc, kb, ptile, j0):
                nonlocal oi
                for j in range(j0, 4):
                    ob = k.ps[2 + j]
                    qoff = (j - j0) * 128
                    S.mm(ob[:, 0:130], ptile[:, qoff:qoff + 128], VA[:, kb * 130:(kb + 1) * 130],
                         start=(kb == 0), stop=(kb == 4 * qc + j))
                    if kb == 4 * qc + j:
                        r_, o_ = rl[oi % 2], on[oi % 2]
                        pb = k.pb[oi % 2]
                        oi += 1
                        S.recip(r_[:, :], ob[:, 128:129])
                        S.ts(o_[:, :], ob[:, 0:128], r_[:, 0:1], ALU.mult)
                        S.tr(pb[:, 0:128], o_[:, :], k.ident[:, :])
                        S.tt(ost[qc % 2][:, j * 128:(j + 1) * 128], pb[:, 0:128], mz[qc % 2][:, j * 128:(j + 1) * 128], ALU.mult)
                        if j == 3:
                            S.dma(k.osz[0][h * 128:(h + 1) * 128, qc * 512:(qc + 1) * 512], ost[qc % 2][:, :], q="pool")

            for (qc, kb) in units:
                if kb == 0:
                    r0 = SEG_ROW["mz"] + h * 128
                    S.dma(mz[qc % 2][:, :], k.PT[r0:r0 + 128, qc * 512:(qc + 1) * 512])
                j0 = max(0, kb - 4 * qc)
                n = 512 - j0 * 128
                q0 = qc * 512 + j0 * 128
                ps = (k.ps[0], k.ps[1], k.ps[6])[ui % 3]
                ptile = pt[ui % 4]
                ui += 1
                S.mm(ps[:, 0:n], KN[:, kb * 128:(kb + 1) * 128], QN[:, q0:q0 + n], start=True, stop=False)
                S.mm(ps[:, 0:n], krT[:, kb * 128:(kb + 1) * 128], QR[:, q0:q0 + n], start=False, stop=True)
                S.act(ptile[:, 0:n], ps[:, 0:n], AF.Exp, scale=SC)
                if kb >= 4 * qc:
                    S.memset(ptile[64:128, 0:64], 0.0, eng="pool")
                pend.append((qc, kb, ptile, j0))
                if len(pend) > 2:
                    pv(*pend.pop(0))
            while pend:
                pv(*pend.pop(0))


def lin_consts(S, k):
    f = S.sbuf("mtmp", [128, 128], F32)
    k.mIU = S.sbuf("mIU", [128, 128], F32)
    k.mSU = S.sbuf("mSU", [128, 128], F32)
    k.mSL = S.sbuf("mSL", [128, 128], F32)
    for m, pat, cm, cmp_ in ((k.mIU, 1, -1, ALU.is_ge), (k.mSU, 1, -1, ALU.is_gt), (k.mSL, -1, 1, ALU.is_gt)):
        S.memset(f[:, :], 1.0)
        S.op("pool", lambda e, m=m, pat=pat, cm=cm, cmp_=cmp_: e.affine_select(
            out=m[:, :].ap, in_=f[:, :].ap, pattern=[[pat, 128]], compare_op=cmp_, fill=0.0, base=0, channel_multiplier=cm),
            [f[:, :]], [m[:, :]])
    k.rmask = S.sbuf("rmask", [128, 512], F32)
    S.memset(k.rmask[:, :], 1.0)
    S.op("pool", lambda e: e.memset(k.rmask[:, :].ap.rearrange("p (n c) -> p n c", c=128)[:, :, 0:1], 0.0), [], [k.rmask[:, :]])
    k.bones = S.sbuf("bones", [128, 128], BF16)
    S.memset(k.bones[:, :], 0.0)
    S.memset(k.bones[0:64, 0:64], 1.0)
    S.memset(k.bones[64:128, 64:128], 1.0)


def scan_cum(S, out, la, rmask):
    S.op("dve", lambda e: e.tensor_tensor_scan(out=out.ap, data0=rmask.ap, data1=la.ap, initial=0.0, op0=ALU.mult, op1=ALU.add),
         [rmask, la], [out])


def c3(v, c=128):
    return v.r("p (n c) -> p n c", c=c)


def phase_rwkv(S, k, l, b):
    T = k.T
    nt = T // 512
    with S.scope():
        TW = S.sbuf("TW", [64, T], BF16)
        AL = S.sbuf("AL", [64, T], BF16)
        w2f = S.sbuf("w2f", [64, 2048], F32)
        w2b = S.sbuf("w2b", [64, 2048], BF16)
        S.dma(w2f[:, 0:1024], k.rwkv_w2[l * 64:(l + 1) * 64, :])
        S.dma(w2f[:, 1024:2048], k.rwkv_a2[l * 64:(l + 1) * 64, :])
        S.copy(w2b[:, :], w2f[:, :], eng="pool")
        cols = S.sbuf("rcols", [128, 8 * 9], F32)
        for i, src in enumerate((k.rwkv_w0, k.rwkv_a0, k.rwkv_k_k, k.rwkv_k_a, k.rwkv_k_a, k.rwkv_r_k, k.rwkv_ln_w, k.rwkv_ln_b)):
            col_load(S, cols[:, i * 8:(i + 1) * 8], src, l * 1024, 8)
        S.ts(cols[:, 32:40], cols[:, 32:40], -1.0, ALU.mult, 1.0, ALU.add)
        rwa = SEG_ROW["rwa"]
        f = [S.sbuf("rf%d" % i, [128, 512], F32) for i in range(20)]
        for tc in range(nt):
            sl = slice(tc * 512, (tc + 1) * 512)
            S.dma(f[0][0:64, :], k.PT[rwa:rwa + 64, sl])
            S.dma(f[1][0:64, :], k.PT[rwa + 64:rwa + 128, sl])
            S.act(TW[:, sl], f[0][0:64, :], AF.Tanh)
            S.copy(AL[:, sl], f[1][0:64, :])
        ARH = S.sbuf("ARH", [128, 1024], BF16)
        BH = S.sbuf("BH", [128, 512], BF16)
        KH = S.sbuf("KH", [128, 512], BF16)
        FE = [S.sbuf("FE%d" % i, [128, 512], BF16) for i in range(4)]
        TK = [S.sbuf("TK%d" % i, [128, 512], BF16) for i in range(4)]
        WT = S.sbuf("WT", [128, 512], BF16)
        ZT = [[S.sbuf("ZT%d_%d" % (i, j), [128, 256], mybir.dt.float32r) for j in range(2)] for i in range(8)]
        NP = [[S.sbuf("NP%d_%d" % (i, j), [128, 128], mybir.dt.float32r) for j in range(2)] for i in range(8)]
        TTb = [S.sbuf("TTb%d" % i, [128, 128], BF16) for i in range(8)]
        AB = [S.sbuf("AB%d" % i, [128, 128], BF16) for i in range(8)]
        AK = [S.sbuf("AK%d" % i, [128, 128], BF16) for i in range(8)]
        AR = [S.sbuf("AR%d" % i, [128, 128], BF16) for i in range(8)]
        GS = [S.sbuf("GS%d" % i, [128, 64], BF16) for i in range(8)]
        UT = [S.sbuf("UT%d" % i, [128, 64], F32) for i in range(8)]
        St = S.sbuf("St", [128, 64], F32)
        Stb = S.sbuf("Stb", [128, 64], BF16)
        Ub = [S.sbuf("Ub%d" % i, [128, 128], BF16) for i in range(2)]
        pcb = S.sbuf("pcb", [128, 4], F32)
        sm = [S.sbuf("sm%d" % i, [128, 16], F32) for i in range(2)]
        sqy = S.sbuf("sqy", [128, 128], F32)
        yn = [S.sbuf("yn%d" % i, [128, 128], BF16) for i in range(2)]
        yf = [S.sbuf("yf%d" % i, [128, 128], F32) for i in range(2)]
        ost = [S.sbuf("rost%d" % i, [128, 512], BF16) for i in range(2)]
        pi = 0

        def PS():
            nonlocal pi
            pi += 1
            return k.ps[pi % 6]

        for hb in range(8):
            cw0, ca0, ckk, cka, comka, crk, clw, clb = [cols[:, i * 8 + hb:i * 8 + hb + 1] for i in range(8)]
            S.memset(St[:, :], 0.0)
            S.memset(Stb[:, :], 0.0)
            for tc in range(nt):
                sl = slice(tc * 512, (tc + 1) * 512)
                r_, kx, v_, lw, a_, kt, kk, kp, cum, e1, e2, ka, bon, rz, tmp, tmp2 = f[:16]
                for dst, nm in ((r_, "rr"), (kx, "rk"), (v_, "rv"), (rz, "rz")):
                    r0 = SEG_ROW[nm] + hb * 128
                    S.dma(dst[:, :], k.PT[r0:r0 + 128, sl])
                p = PS()
                S.mm(p[:, :], w2b[:, hb * 128:(hb + 1) * 128], TW[:, sl])
                S.act(lw[:, :], p[:, :], AF.Sigmoid, bias=cw0)
                S.ts(lw[:, :], lw[:, :], -0.6065306597126334, ALU.mult)
                p = PS()
                S.mm(p[:, :], w2b[:, 1024 + hb * 128:1024 + (hb + 1) * 128], AL[:, sl])
                S.act(a_[:, :], p[:, :], AF.Sigmoid, bias=ca0)
                S.ts(kt[:, :], kx[:, :], ckk, ALU.mult)
                S.act(FE[0][:, :], kt[:, :], AF.Square)
                p = PS()
                S.mm(p[:, :], k.bones[:, :], FE[0][:, :])
                S.act(tmp[:, :], p[:, :], AF.Sqrt)
                S.ts(tmp[:, :], tmp[:, :], 1e-12, ALU.max)
                S.recip(tmp[:, :], tmp[:, :])
                S.tt(kk[:, :], kt[:, :], tmp[:, :], ALU.mult)
                S.ts(tmp[:, :], a_[:, :], cka, ALU.mult, comka, ALU.add)
                S.tt(kp[:, :], kx[:, :], tmp[:, :], ALU.mult)
                S.tt(tmp[:, :], r_[:, :], kp[:, :], ALU.mult, eng="pool")
                S.ts(FE[0][:, :], tmp[:, :], crk, ALU.mult)
                p = PS()
                S.mm(p[:, :], k.bones[:, :], FE[0][:, :])
                S.tt(bon[:, :], v_[:, :], p[:, :], ALU.mult)
                scan_cum(S, cum[:, :], lw[:, :], k.rmask[:, :])
                cend = c3(cum[:, :]).ix(slice(None), slice(None), slice(127, 128))
                S.op("act", lambda e, cend=cend: e.activation(out=pcb[:, :].ap.rearrange("p (n o) -> p n o", o=1), in_=cend.ap, func=AF.Exp),
                     [cum[:, :]], [pcb[:, :]])
                S.act(e1[:, :], cum[:, :], AF.Exp)
                A3 = ARH[:, :].r("p (n two t) -> p n two t", two=2, t=128)
                S.op("dve", lambda e, A3=A3, r_=r_, e1=e1: e.tensor_tensor(out=A3.ap[:, :, 1, :], in0=c3(r_[:, :]).ap, in1=c3(e1[:, :]).ap, op=ALU.mult),
                     [r_[:, :], e1[:, :]], [ARH[:, :]])
                S.act(e2[:, :], cum[:, :], AF.Exp, scale=-1.0)
                S.tt(ka[:, :], kk[:, :], a_[:, :], ALU.mult, eng="pool")
                S.tt(BH[:, :], ka[:, :], e2[:, :], ALU.mult)
                S.tt(KH[:, :], kp[:, :], e2[:, :], ALU.mult)
                S.tt(tmp[:, :], cum[:, :], lw[:, :], ALU.subtract, eng="pool")
                S.act(e1[:, :], tmp[:, :], AF.Exp)
                S.stt(FE[0][:, :], kk[:, :], -1.0, e1[:, :], ALU.mult, ALU.mult)
                S.op("pool", lambda e, A3=A3: e.tensor_copy(out=A3.ap[:, :, 0, :], in_=c3(FE[0][:, :]).ap), [FE[0][:, :]], [ARH[:, :]])
                S.op("dve", lambda e, tmp2=tmp2, cum=cum, cend=cend: e.tensor_tensor(
                    out=c3(tmp2[:, :]).ap, in0=cend.ap.broadcast_to([128, 4, 128]), in1=c3(cum[:, :]).ap, op=ALU.subtract),
                    [cum[:, :]], [tmp2[:, :]])
                S.act(e2[:, :], tmp2[:, :], AF.Exp)
                S.tt(FE[1][:, :], ka[:, :], e2[:, :], ALU.mult)
                S.tt(FE[2][:, :], kp[:, :], e2[:, :], ALU.mult)
                S.copy(FE[3][:, :], v_[:, :], eng="pool")
                for i in range(4):
                    pb = k.pb[i % 2]
                    for c in range(4):
                        S.tr(pb[:, c * 128:(c + 1) * 128], FE[i][:, c * 128:(c + 1) * 128], k.ident[:, :])
                    S.copy(TK[i][:, :], pb[:, 0:512], eng=("act" if i % 2 else "dve"))
                if k.rstage < 1.1:
                    continue
                for c in range(4):
                    for e_ in range(k.ne):
                        ii = c * 2 + e_
                        rows = slice(e_ * 64, (e_ + 1) * 64)
                        ah = ARH[rows, c * 256:c * 256 + 128]
                        arh = ARH[rows, c * 256:c * 256 + 256]
                        bh = BH[rows, c * 128:(c + 1) * 128]
                        kh = KH[rows, c * 128:(c + 1) * 128]
                        p = PS()
                        S.mm(p[:, 0:128], ah, bh)
                        S.tt(NP[ii][0][:, :], p[:, 0:128], k.mSL[:, :], ALU.mult)
                        p = PS()
                        S.mm(p[:, 128:384], bh, arh)
                        S.tt(ZT[ii][0][:, 0:128], p[:, 128:256], k.mSU[:, :], ALU.mult)
                        S.tt(AB[ii][:, :], p[:, 256:384], k.mIU[:, :], ALU.mult)
                        p = PS()
                        S.mm(p[:, 0:256], kh, arh)
                        S.tt(AK[ii][:, :], p[:, 0:128], k.mSU[:, :], ALU.mult)
                        S.tt(AR[ii][:, :], p[:, 128:256], k.mIU[:, :], ALU.mult)
                        S.tt(ZT[ii][1][:, 128:256], ZT[ii][0][:, 0:128], k.identf[:, :], ALU.add, eng="pool")
                for lv in range(1, 7):
                    cur, nxt = (lv - 1) % 2, lv % 2
                    for ii in range(8):
                        p = PS()
                        S.mm(p[:, 0:128], ZT[ii][cur][:, 0:128], NP[ii][cur][:, :])
                        S.copy(NP[ii][nxt][:, :], p[:, 0:128], eng="act")
                        if 2 <= lv < 6:
                            p = PS()
                            S.mm(p[:, 0:256], NP[ii][cur][:, :], ZT[ii][cur][:, 0:256])
                            S.copy(ZT[ii][nxt][:, 0:128], p[:, 0:128], eng="dve")
                            S.tt(ZT[ii][nxt][:, 128:256], p[:, 128:256], ZT[ii][cur][:, 128:256], ALU.add)
                        elif lv < 6:
                            p = PS()
                            S.mm(p[:, 0:128], NP[ii][cur][:, :], ZT[ii][cur][:, 0:128])
                            S.copy(ZT[ii][nxt][:, 0:128], p[:, 0:128], eng="act")
                        else:
                            p = PS()
                            S.mm(p[:, 0:128], NP[ii][cur][:, :], ZT[ii][cur][:, 128:256])
                            S.tt(ZT[ii][nxt][:, 128:256], p[:, 0:128], ZT[ii][cur][:, 128:256], ALU.add)
                for ii in range(8):
                    p = PS()
                    S.mm(p[:, 0:128], NP[ii][0][:, :], ZT[ii][0][:, 128:256])
                    S.tt(TTb[ii][:, :], p[:, 0:128], ZT[ii][0][:, 128:256], ALU.add)
                fin = 6 % 2
                for c in range(4):
                    for e_ in range(k.ne if k.rstage > 1.5 else 0):
                        ii = c * 2 + e_
                        rows = slice(e_ * 64, (e_ + 1) * 64)
                        tt_ = TTb[ii][:, :]
                        hs = slice(c * 128 + e_ * 64, c * 128 + (e_ + 1) * 64)
                        p = PS()
                        S.mm(p[rows, 0:128], TK[0][:, hs], tt_)
                        S.copy(WT[rows, c * 128:(c + 1) * 128], p[rows, 0:128], eng="act")
                        p = PS()
                        S.mm(p[:, 128:192], AK[ii][:, :], TK[3][:, hs])
                        S.copy(GS[ii][:, :], p[:, 128:192])
                        p = PS()
                        S.mm(p[:, 256:320], tt_, GS[ii][:, :])
                        S.copy(UT[ii][:, :], p[:, 256:320], eng="act")
                if k.rstage < 3:
                    continue
                for c in range(4):
                    ub = Ub[c % 2]
                    pUs = (PS(), PS())
                    pYs = (PS(), PS())
                    R = [slice(e_ * 64, (e_ + 1) * 64) for e_ in range(2)]
                    HS = [slice(c * 128 + e_ * 64, c * 128 + (e_ + 1) * 64) for e_ in range(2)]
                    for e_ in range(2):
                        S.mm(pUs[e_][:, 0:64], WT[R[e_], c * 128:(c + 1) * 128], Stb[R[e_], :])
                    for e_ in range(2):
                        ii = c * 2 + e_
                        S.mm(pYs[e_][:, 0:64], ARH[R[e_], c * 256 + 128:c * 256 + 256], Stb[R[e_], :], start=True, stop=False)
                        S.mm(pYs[e_][:, 0:64], AR[ii][:, :], TK[3][:, HS[e_]], start=False, stop=False)
                    for e_ in range(2):
                        ii = c * 2 + e_
                        S.tt(ub[:, R[e_]], pUs[e_][:, 0:64], UT[ii][:, :], ALU.add)
                    for e_ in range(2):
                        ii = c * 2 + e_
                        S.mm(pYs[e_][:, 0:64], AB[ii][:, :], ub[:, R[e_]], start=False, stop=True)
                    pSs = (PS(), PS())
                    for e_ in range(2):
                        S.mm(pSs[e_][R[e_], 0:64], TK[1][:, HS[e_]], ub[:, R[e_]], start=True, stop=False)
                        S.mm(pSs[e_][R[e_], 0:64], TK[2][:, HS[e_]], TK[3][:, HS[e_]], start=False, stop=True)
                    for e_ in range(2):
                        S.stt(St[R[e_], :], St[R[e_], :], pcb[R[e_], c:c + 1], pSs[e_][R[e_], 0:64], ALU.mult, ALU.add)
                    S.copy(Stb[:, :], St[:, :], eng="act")

                    s_ = sm[c % 2]
                    for e_ in range(2):
                        S.op("dve", lambda e, s_=s_, e_=e_, py=pYs[e_]: e.tensor_reduce(out=s_[:, e_:e_ + 1].ap, in_=py[:, 0:64].ap, axis=AX.X, op=ALU.add),
                             [pYs[e_][:, 0:64]], [s_[:, e_:e_ + 1]])
                        S.act(sqy[:, e_ * 64:(e_ + 1) * 64], pYs[e_][:, 0:64], AF.Square)
                    S.op("dve", lambda e, s_=s_: e.tensor_reduce(out=s_[:, 2:4].ap, in_=sqy[:, :].ap.rearrange("p (h n) -> p h n", h=2), axis=AX.X, op=ALU.add),
                         [sqy[:, :]], [s_[:, 2:4]])
                    S.ts(s_[:, 4:6], s_[:, 0:2], 1.0 / 64, ALU.mult)
                    S.tt(s_[:, 6:8], s_[:, 4:6], s_[:, 4:6], ALU.mult)
                    S.stt(s_[:, 8:10], s_[:, 2:4], 1.0 / 64, s_[:, 6:8], ALU.mult, ALU.subtract)
                    S.act(s_[:, 10:12], s_[:, 8:10], AF.Sqrt, bias=k.epsc[:, 1:2])
                    S.recip(s_[:, 12:14], s_[:, 10:12])
                    y_ = yn[c % 2]
                    for e_ in range(2):
                        hc = slice(e_ * 64, (e_ + 1) * 64)
                        S.ts(y_[:, hc], pYs[e_][:, 0:64], s_[:, 4 + e_:5 + e_], ALU.subtract, s_[:, 12 + e_:13 + e_], ALU.mult)
                    pb = k.pb[c % 2]
                    S.tr(pb[:, 0:128], y_[:, :], k.ident[:, :])
                    yf_ = yf[c % 2]
                    S.act(yf_[:, :], pb[:, 0:128], AF.Identity, scale=clw, bias=clb)
                    S.tt(yf_[:, :], yf_[:, :], bon[:, c * 128:(c + 1) * 128], ALU.add, eng="pool")
                    S.tt(ost[tc % 2][:, c * 128:(c + 1) * 128], yf_[:, :], rz[:, c * 128:(c + 1) * 128], ALU.mult, eng="pool")
                S.dma(k.osz[1][hb * 128:(hb + 1) * 128, sl], ost[tc % 2][:, :], q="pool")
                if k.dbg and hb == k.dhb and tc == 0 and b == 1:
                    for nm, bf, dt in (("lw", lw, F32), ("a", a_, F32), ("kk", kk, F32), ("kp", kp, F32), ("cum", cum, F32), ("bon", bon, F32),
                                       ("ARH", ARH, BF16), ("BH", BH, BF16), ("KH", KH, BF16), ("TK0", TK[0], BF16), ("TK1", TK[1], BF16),
                                       ("TK2", TK[2], BF16), ("TK3", TK[3], BF16), ("WT", WT, BF16), ("NP0", NP[0][0], mybir.dt.float32r),
                                       ("ZT0", ZT[0][0], mybir.dt.float32r), ("UT0", UT[0], F32), ("AB0", AB[0], BF16), ("AK0", AK[0], BF16), ("AR0", AR[0], BF16),
                                       ("St", St, F32), ("pcb", pcb, F32)):
                        dump(S, k, "r_" + nm, bf, dt)


PHASES = ("mla", "rwkv", "gla")


def _core_inputs(ins, core, T):
    d = {}
    bs = slice(core * NB, (core + 1) * NB)
    d["x"] = np.ascontiguousarray(ins["x"][bs, :T]).reshape(NB * T, D)
    d["c"] = np.ascontiguousarray(ins["c"][bs])
    d["positions"] = np.ascontiguousarray(ins["positions"][bs, :T]).astype(np.int32)
    fr = (10000.0 ** (-np.arange(0, 64, 2, dtype=np.float32) / 64)).astype(np.float32)
    d["rope_freq"] = np.concatenate([fr, fr]).reshape(64, 1)
    for k_, v in ins.items():
        if k_ in ("x", "c", "positions"):
            continue
        v = np.ascontiguousarray(v, dtype=np.float32)
        if k_ in ("ada_b", "norm_pre", "norm_post"):
            d[k_] = v
        elif v.ndim == 2 or k_ == "rwkv_r_k":
            d[k_] = v.reshape(-1, 1)
        else:
            d[k_] = v.reshape(v.shape[0] * v.shape[1], v.shape[2])
    return d


def kernel(**inputs):
    T = inputs["x"].shape[1]
    nc, k = build(T=T, dbg=False, phases=PHASES)
    in_maps = [_core_inputs(inputs, c, T) for c in range(8)]
    res = run_bass_kernel_spmd(nc, in_maps, core_ids=list(range(8)))
    out = np.stack([r["out"].reshape(NB, T, D) for r in res.results], axis=0).reshape(8 * NB, T, D)
    return out.astype(np.float32)


def phase_gla(S, k, l, b):
    T = k.T
    nt = T // 512
    DKS = float(128 ** -0.5)
    with S.scope():
        w2f = S.sbuf("gw2f", [16, 512], F32)
        S.dma(w2f[:, :], k.gla_w2[l * 16:(l + 1) * 16, :])
        cols = S.sbuf("gcols", [128, 8], F32)
        col_load(S, cols[:, 0:4], k.gla_b, l * 512, 4)
        col_load(S, cols[:, 4:6], k.gla_norm, l * 256, 2)
        S.ts(cols[:, 0:4], cols[:, 0:4], -1.0, ALU.mult)
        f = [S.sbuf("gf%d" % i, [128, 512], F32) for i in range(12)]
        glT = S.sbuf("glT", [16, 512], F32)
        QH = S.sbuf("gQH", [128, 512], BF16)
        KH = S.sbuf("gKH", [128, 512], BF16)
        KEf = S.sbuf("gKEf", [128, 512], BF16)
        VBf = [S.sbuf("gVBf%d" % i, [128, 512], BF16) for i in range(2)]
        KEk = S.sbuf("gKEk", [128, 512], BF16)
        VTk = S.sbuf("gVTk", [128, 4 * 256], BF16)
        AT = [S.sbuf("gAT%d" % i, [128, 128], BF16) for i in range(2)]
        St = S.sbuf("gSt", [128, 256], F32)
        Stb = S.sbuf("gStb", [128, 256], BF16)
        pcb = S.sbuf("gpcb", [128, 4], F32)
        sm = [S.sbuf("gsm%d" % i, [128, 4], F32) for i in range(2)]
        junk = S.sbuf("gjunk", [128, 256], F32)
        yn = [S.sbuf("gyn%d" % i, [128, 256], BF16) for i in range(2)]
        ost = [S.sbuf("gost%d" % i, [128, 2 * 512], BF16) for i in range(2)]
        pi = 0

        def PS():
            nonlocal pi
            pi += 1
            return k.ps[pi % 6]

        for h in range(4):
            S.memset(St[:, :], 0.0)
            S.memset(Stb[:, :], 0.0)
            for tc in range(nt):
                sl = slice(tc * 512, (tc + 1) * 512)
                q_, k_, la, cum, e1, tmp, gz0, gz1, v0, v1 = f[:10]
                S.dma(q_[:, :], k.PT[SEG_ROW["gq"] + h * 128:SEG_ROW["gq"] + (h + 1) * 128, sl])
                S.dma(k_[:, :], k.PT[SEG_ROW["gk"] + h * 128:SEG_ROW["gk"] + (h + 1) * 128, sl])
                S.dma(glT[:, :], k.PT[SEG_ROW["gl"]:SEG_ROW["gl"] + 16, sl])
                for j, (vv, gg) in enumerate(((v0, gz0), (v1, gz1))):
                    r0 = SEG_ROW["gv"] + h * 256 + j * 128
                    S.dma(vv[:, :], k.PT[r0:r0 + 128, sl])
                    r0 = SEG_ROW["gz"] + h * 256 + j * 128
                    S.dma(gg[:, :], k.PT[r0:r0 + 128, sl])
                p = PS()
                S.mm(p[:, :], w2f[:, h * 128:(h + 1) * 128], glT[:, :])
                S.act(la[:, :], p[:, :], AF.Exp, scale=-1.0, bias=cols[:, h:h + 1])
                S.act(la[:, :], la[:, :], AF.Ln, bias=k.epsc[:, 2:3])
                S.ts(la[:, :], la[:, :], -1.0 / 16.0, ALU.mult)
                scan_cum(S, cum[:, :], la[:, :], k.rmask[:, :])
                cend = c3(cum[:, :]).ix(slice(None), slice(None), slice(127, 128))
                S.op("act", lambda e, cend=cend: e.activation(out=pcb[:, :].ap.rearrange("p (n o) -> p n o", o=1), in_=cend.ap, func=AF.Exp),
                     [cum[:, :]], [pcb[:, :]])
                S.act(e1[:, :], cum[:, :], AF.Exp)
                S.stt(QH[:, :], q_[:, :], DKS, e1[:, :], ALU.mult, ALU.mult)
                S.act(e1[:, :], cum[:, :], AF.Exp, scale=-1.0)
                S.tt(KH[:, :], k_[:, :], e1[:, :], ALU.mult)
                S.op("dve", lambda e, tmp=tmp, cum=cum, cend=cend: e.tensor_tensor(
                    out=c3(tmp[:, :]).ap, in0=cend.ap.broadcast_to([128, 4, 128]), in1=c3(cum[:, :]).ap, op=ALU.subtract),
                    [cum[:, :]], [tmp[:, :]])
                S.act(e1[:, :], tmp[:, :], AF.Exp)
                S.tt(KEf[:, :], k_[:, :], e1[:, :], ALU.mult)
                S.copy(VBf[0][:, :], v0[:, :], eng="pool")
                S.copy(VBf[1][:, :], v1[:, :], eng="pool")
                pb = k.pb[0]
                for c in range(4):
                    S.tr(pb[:, c * 128:(c + 1) * 128], KEf[:, c * 128:(c + 1) * 128], k.ident[:, :])
                S.copy(KEk[:, :], pb[:, 0:512], eng="act")
                pb = k.pb[1]
                for c in range(4):
                    for j in range(2):
                        S.tr(pb[:, c * 256 + j * 128:c * 256 + (j + 1) * 128], VBf[j][:, c * 128:(c + 1) * 128], k.ident[:, :])
                S.copy(VTk[:, :], pb[:, 0:1024])
                for c in range(4):
                    cs_ = slice(c * 128, (c + 1) * 128)
                    vt = VTk[:, c * 256:(c + 1) * 256]
                    p = PS()
                    S.mm(p[:, 0:128], KH[:, cs_], QH[:, cs_])
                    at = AT[c % 2]
                    S.tt(at[:, :], p[:, 0:128], k.mIU[:, :], ALU.mult)
                    pY = PS()
                    S.mm(pY[:, 0:256], QH[:, cs_], Stb[:, :], start=True, stop=False)
                    S.mm(pY[:, 0:256], at[:, :], vt, start=False, stop=True)
                    pS_ = PS()
                    S.mm(pS_[:, 0:256], KEk[:, cs_], vt)
                    S.stt(St[:, :], St[:, :], pcb[:, c:c + 1], pS_[:, 0:256], ALU.mult, ALU.add)
                    S.copy(Stb[:, :], St[:, :], eng="act")
                    s_ = sm[c % 2]
                    S.act(junk[:, :], pY[:, 0:256], AF.Square, accum=s_[:, 0:1])
                    S.act(s_[:, 1:2], s_[:, 0:1], AF.Sqrt, scale=1.0 / 256, bias=k.epsc[:, 0:1])
                    S.recip(s_[:, 2:3], s_[:, 1:2])
                    y_ = yn[c % 2]
                    S.ts(y_[:, :], pY[:, 0:256], s_[:, 2:3], ALU.mult)
                    pb2 = k.pb[c % 2]
                    for j, gg in enumerate((gz0, gz1)):
                        S.tr(pb2[:, j * 128:(j + 1) * 128], y_[:, j * 128:(j + 1) * 128], k.ident[:, :])
                        S.stt(ost[tc % 2][:, j * 512 + c * 128:j * 512 + (c + 1) * 128], pb2[:, j * 128:(j + 1) * 128],
                              cols[:, 4 + j:5 + j], gg[:, cs_], ALU.mult, ALU.mult)
                for j in range(2):
                    r0 = h * 256 + j * 128
                    S.dma(k.osz[2][r0:r0 + 128, sl], ost[tc % 2][:, j * 512:(j + 1) * 512], q="pool")
```

```python
import contextlib
import numpy as np
import concourse.bass as bass
import concourse.mybir as mybir
from concourse.bass_utils import run_bass_kernel_spmd

F32 = mybir.dt.float32
BF16 = mybir.dt.bfloat16
I32 = mybir.dt.int32
AF = mybir.ActivationFunctionType
ALU = mybir.AluOpType
AX = mybir.AxisListType

D = 1024
NB = 2
L = 2
N_IN = 12240
EPS = 1e-6


class V:
    __slots__ = ("ap", "key", "box")

    def __init__(self, ap, key, box):
        self.ap, self.key, self.box = ap, key, box

    def r(self, pat, **kw):
        return V(self.ap.rearrange(pat, **kw), self.key, self.box)

    def ix(self, *idx):
        return V(self.ap[idx], self.key, self.box)


class Buf:
    def __init__(self, name, t, shape):
        self.name, self.t, self.shape = name, t, shape

    def __getitem__(self, idx):
        if not isinstance(idx, tuple):
            idx = (idx, slice(None))
        p, f = idx
        p0, p1, _ = p.indices(self.shape[0])
        f0, f1, _ = f.indices(self.shape[1])
        return V(self.t[p0:p1, f0:f1], self.name, (p0, p1, f0, f1))


def _ovl(a, b):
    return a[0] < b[1] and b[0] < a[1] and a[2] < b[3] and b[2] < a[3]


def _contains(a, b):
    return a[0] <= b[0] and a[1] >= b[1] and a[2] <= b[2] and a[3] >= b[3]


ENGS = ("pe", "act", "dve", "pool", "sp")


class Sched:
    NDMA = 12

    def __init__(self, nc, es):
        self.nc = nc
        self.es = es
        self.ops = []
        self.recs = {}
        self.nops = 0
        self.sem = {e: es.enter_context(nc.semaphore("s_" + e)) for e in ENGS}
        self.cnt = {e: 0 for e in ENGS}
        self.dsem = {q: [es.enter_context(nc.semaphore("d_%s%d" % (q, i))) for i in range(self.NDMA)]
                     for q in ("sp", "pool")}
        self.dn = {"sp": 0, "pool": 0}
        self.done = {}
        self.waited = {e: {} for e in ENGS}
        self.n_inst = 0
        self.eidx = {}
        self.eidx_n = {}

    def _simulate(self, trace):
        sv = self.simv = getattr(self, "simv", {})
        pos = {e_: 0 for e_ in ENGS}
        prog = True
        while prog:
            prog = False
            for e_ in ENGS:
                while pos[e_] < len(trace[e_]):
                    oid, waits, inc = trace[e_][pos[e_]]
                    if all(sv.get(k_, 0) >= v_ for k_, v_ in waits):
                        if inc is not None:
                            sv[inc[0]] = sv.get(inc[0], 0) + inc[1]
                        pos[e_] += 1
                        prog = True
                    else:
                        break
        stuck = {e_: trace[e_][pos[e_]] for e_ in ENGS if pos[e_] < len(trace[e_])}
        if stuck:
            print("DEADLOCK", stuck, {k_: v_ for k_, v_ in sv.items()})
            raise RuntimeError("deadlock in sync plan")

    @contextlib.contextmanager
    def scope(self):
        old = self.es
        with contextlib.ExitStack() as es:
            self.es = es
            try:
                yield
                self.flush()
            finally:
                self.es = old

    def sbuf(self, name, shape, dt):
        self.uid = getattr(self, "uid", 0) + 1
        name = "%s_u%d" % (name, self.uid)
        t = self.es.enter_context(self.nc.sbuf_tensor(name, list(shape), dt))
        return Buf(name, t, shape)

    def psum(self, name, shape, dt):
        t = self.es.enter_context(self.nc.psum_tensor(name, list(shape), dt))
        return Buf(name, t, shape)

    def dram(self, name, shape, dt, kind="Internal"):
        t = self.nc.dram_tensor(name, list(shape), dt, kind=kind)
        return Buf(name, t, shape)

    def _deps(self, reads, writes):
        deps = set()
        for v in reads:
            for rec in self.recs.get(v.key, ()):
                if rec[1] is not None and _ovl(rec[0], v.box):
                    deps.add(rec[1])
        for v in writes:
            for rec in self.recs.get(v.key, ()):
                if _ovl(rec[0], v.box):
                    if rec[1] is not None:
                        deps.add(rec[1])
                    deps.update(rec[2])
        return deps

    def _update(self, oid, reads, writes):
        for v in reads:
            lst = self.recs.setdefault(v.key, [])
            for rec in lst:
                if rec[0] == v.box:
                    rec[2].append(oid)
                    break
            else:
                lst.append([v.box, None, [oid]])
        for v in writes:
            lst = self.recs.setdefault(v.key, [])
            keep = [rec for rec in lst if not _contains(v.box, rec[0])]
            keep.append([v.box, oid, []])
            self.recs[v.key] = keep

    def op(self, eng, fn, reads=(), writes=(), dma=False, ptr=()):
        oid = self.nops
        self.nops += 1
        deps = self._deps(list(reads) + list(ptr), writes)
        hdeps = self._deps(ptr, ()) if ptr else set()
        if eng != "pe" and not dma:
            ei = self.eidx_n.get(eng, 0)
            for d in self._deps(list(reads), ()):
                pe_ = self.eidx.get(d)
                if pe_ is not None and pe_[0] == eng and ei - pe_[1] <= 3:
                    hdeps.add(d)
        self.eidx_n[eng] = self.eidx_n.get(eng, 0) + 1
        self.eidx[oid] = (eng, self.eidx_n[eng] - 1)
        self._update(oid, list(reads) + list(ptr), writes)
        self.ops.append((oid, eng, fn, deps, dma, hdeps))
        return oid

    def dma(self, out, in_, q="sp", **kw):
        return self.op(q, lambda e: e.dma_start(out=out.ap, in_=in_.ap, **kw), [in_], [out], dma=True)

    def mm(self, out, lhsT, rhs, start=True, stop=True):
        return self.op("pe", lambda e: e.matmul(out.ap, lhsT.ap, rhs.ap, start=start, stop=stop),
                       [lhsT, rhs], [out])

    def tr(self, out, in_, ident):
        return self.op("pe", lambda e: e.transpose(out.ap, in_.ap, ident.ap), [in_, ident], [out])

    def act(self, out, in_, func, bias=None, scale=None, accum=None, eng="act"):
        rd = [in_]
        pt = []
        kw = {}
        if bias is not None:
            if isinstance(bias, V):
                pt.append(bias); kw["bias"] = bias.ap
            else:
                kw["bias"] = float(bias)
        if scale is not None:
            if isinstance(scale, V):
                pt.append(scale); kw["scale"] = scale.ap
            else:
                kw["scale"] = float(scale)
        wr = [out]
        if accum is not None:
            wr.append(accum); kw["accum_out"] = accum.ap
        return self.op("act", lambda e: e.activation(out=out.ap, in_=in_.ap, func=func, **kw), rd, wr, ptr=pt)

    def tt(self, out, a, b, op, eng="dve"):
        return self.op(eng, lambda e: e.tensor_tensor(out=out.ap, in0=a.ap, in1=b.ap, op=op), [a, b], [out])

    def ts(self, out, a, s1, op0, s2=None, op1=None, eng="dve", accum=None):
        rd = [a]
        s1a = s1.ap if isinstance(s1, V) else float(s1)
        s2a = None if s2 is None else (s2.ap if isinstance(s2, V) else float(s2))
        pt = []
        if isinstance(s1, V): pt.append(s1)
        if isinstance(s2, V): pt.append(s2)
        kw = {}
        if op1 is not None: kw["op1"] = op1
        wr = [out]
        if accum is not None:
            kw["accum_out"] = accum.ap; wr.append(accum)
        return self.op(eng, lambda e: e.tensor_scalar(out=out.ap, in0=a.ap, scalar1=s1a, scalar2=s2a, op0=op0, **kw),
                       rd, wr, ptr=pt)

    def stt(self, out, a, s, b, op0, op1, eng="dve"):
        rd = [a, b]
        sa = s.ap if isinstance(s, V) else float(s)
        pt = [s] if isinstance(s, V) else []
        return self.op(eng, lambda e: e.scalar_tensor_tensor(out=out.ap, in0=a.ap, scalar=sa, in1=b.ap, op0=op0, op1=op1),
                       rd, [out], ptr=pt)

    def copy(self, out, in_, eng="dve"):
        if eng == "act":
            return self.op("act", lambda e: e.activation(out=out.ap, in_=in_.ap, func=AF.Copy), [in_], [out])
        return self.op(eng, lambda e: e.tensor_copy(out=out.ap, in_=in_.ap), [in_], [out])

    def memset(self, out, val, eng="pool"):
        return self.op(eng, lambda e: e.memset(out.ap, val), [], [out])

    def recip(self, out, in_):
        return self.op("dve", lambda e: e.reciprocal(out=out.ap, in_=in_.ap), [in_], [out])

    def flush(self):
        ops = self.ops
        self.ops = []
        if not ops:
            return
        eng_of = {o[0]: o[1] for o in ops}
        need = set()
        for oid, eng, fn, deps, dma, hdeps in ops:
            for d in deps:
                if d in self.done:
                    continue
                if d in eng_of and (eng_of[d] != eng or dma or d in hdeps):
                    need.add(d)
        last = {}
        for oid, eng, fn, deps, dma, hdeps in ops:
            last[eng] = oid
        need.update(last.values())
        plan = {e: [] for e in ENGS}
        for oid, eng, fn, deps, dma, hdeps in ops:
            if dma:
                i = self.dn[eng]
                self.dn[eng] += 1
                tok = ("dma", eng, i % self.NDMA, 16 * (i // self.NDMA + 1), i)
                self.done[oid] = tok
            elif oid in need:
                self.cnt[eng] += 1
                tok = ("eng", eng, self.cnt[eng])
                self.done[oid] = tok
            else:
                tok = None
                self.done[oid] = ("impl", eng)
            plan[eng].append((oid, fn, deps, dma, tok, hdeps))
        nc = self.nc
        trace = {e_: [] for e_ in ENGS}
        self._trace = trace
        with nc.Block() as block:
            def mk(eng):
                def body(e):
                    wd = self.waited[eng]
                    for oid, fn, deps, dma, tok, hdeps in plan[eng]:
                        waits = {}
                        for d in deps:
                            dt = self.done.get(d)
                            if dt is None or dt[0] == "impl":
                                continue
                            if dt[0] == "eng":
                                if dt[1] == eng and not dma and d not in hdeps:
                                    continue
                                s = self.sem[dt[1]]
                                val = dt[2]
                                k = ("e", dt[1])
                            else:
                                s = self.dsem[dt[1]][dt[2]]
                                val = dt[3]
                                k = ("d", dt[1], dt[2])
                            if wd.get(k, 0) >= val:
                                continue
                            if waits.get(k, (None, 0))[1] < val:
                                waits[k] = (s, val)
                        if dma:
                            i = tok[4]
                            if i >= self.NDMA:
                                k = ("d", eng, tok[2])
                                val = tok[3] - 16
                                if wd.get(k, 0) < val and waits.get(k, (None, 0))[1] < val:
                                    waits[k] = (self.dsem[eng][tok[2]], val)
                        for k, (s, val) in waits.items():
                            e.wait_ge(s, val)
                            wd[k] = val
                            self.n_inst += 1
                        ins = fn(e)
                        self.n_inst += 1
                        inc = None
                        if tok is not None:
                            if tok[0] == "dma":
                                ins.then_inc(self.dsem[eng][tok[2]], 16)
                                inc = (("d", eng, tok[2]), 16)
                            else:
                                ins.then_inc(self.sem[eng], 1)
                                inc = (("e", eng), 1)
                        trace[eng].append((oid, [(k_, v_[1]) for k_, v_ in waits.items()], inc))
                    if eng in ("sp", "pool"):
                        n = self.dn[eng]
                        for j in range(self.NDMA):
                            if n > j:
                                val = 16 * ((n - 1 - j) // self.NDMA + 1)
                                k = ("d", eng, j)
                                if wd.get(k, 0) < val:
                                    e.wait_ge(self.dsem[eng][j], val)
                                    wd[k] = val
                return body
            block.tensor(mk("pe"))
            block.scalar(mk("act"))
            block.vector(mk("dve"))
            block.gpsimd(mk("pool"))
            block.sync(mk("sp"))
        if getattr(self, "check", False):
            self._simulate(trace)
        self.recs = {}
        self.done = {}
        self.eidx = {}


SEGS = [
    ("cq", 0, 512, "copy"), ("ckv", 512, 256, "copy"), ("kr", 768, 64, "copy"), ("mz", 832, 1024, "silu"),
    ("rr", 1856, 1024, "shift"), ("rk", 2880, 1024, "shift"), ("rv", 3904, 1024, "shift"),
    ("rwa", 4928, 128, "shift"), ("rz", 5056, 1024, "silu"),
    ("gq", 6080, 512, "copy"), ("gk", 6592, 512, "copy"), ("gv", 7104, 1024, "copy"), ("gl", 8128, 16, "copy"),
    ("gz", 8144, 1024, "silu"), ("ga", 9168, 1024, "sigmoid"), ("gb", 10192, 1024, "sigmoid"),
    ("gc", 11216, 1024, "sigmoid"),
]
SEG_ROW = {}
_r = 0
for _n, _c, _w, _e in SEGS:
    SEG_ROW[_n] = _r
    _r += _w
PT_ROWS = _r


class K:
    pass


def v3(v, pat, **kw):
    return v.r(pat, **kw)


def build(T=4096, dbg=False, phases=("mla", "rwkv", "gla"), nl=L):
    nc = bass.Bass("TRN2", target_bir_lowering=False)
    k = K()
    k.phases = phases
    import os
    k.rstage = float(os.environ.get('K_RSTAGE', '3'))
    k.ne = int(os.environ.get('K_NE', '2'))
    k.lvx = int(os.environ.get('K_LVX', '3'))
    k.dhb = int(os.environ.get('K_DHB', '7'))
    k.nl = nl
    k.dbg = dbg
    k.dumped = set()
    k.T = T
    es = contextlib.ExitStack()
    S = Sched(nc, es)
    k.S = S
    ext_in = lambda name, shape, dt=F32: S.dram(name, shape, dt, kind="ExternalInput")
    k.x_in = ext_in("x", [NB * T, D])
    k.c_in = ext_in("c", [NB, D])
    k.pos_in = ext_in("positions", [NB, T], I32)
    k.ada_w = ext_in("ada_w", [L * D, 3 * D])
    k.ada_b = ext_in("ada_b", [L, 3 * D])
    k.norm_pre = ext_in("norm_pre", [L, D])
    k.norm_post = ext_in("norm_post", [L, D])
    k.w_in = ext_in("w_in", [L * D, N_IN])
    k.rwkv_mu = ext_in("rwkv_mu", [L * 3200, 1])
    k.mla_q_norm = ext_in("mla_q_norm", [L * 512, 1])
    k.mla_kv_norm = ext_in("mla_kv_norm", [L * 256, 1])
    k.mla_w_uq = ext_in("mla_w_uq", [L * 512, 1536])
    k.mla_w_ukv = ext_in("mla_w_ukv", [L * 256, 2048])
    k.mla_w_o = ext_in("mla_w_o", [L * 1024, 1024])
    k.rwkv_w0 = ext_in("rwkv_w0", [L * 1024, 1])
    k.rwkv_w2 = ext_in("rwkv_w2", [L * 64, 1024])
    k.rwkv_a0 = ext_in("rwkv_a0", [L * 1024, 1])
    k.rwkv_a2 = ext_in("rwkv_a2", [L * 64, 1024])
    k.rwkv_k_k = ext_in("rwkv_k_k", [L * 1024, 1])
    k.rwkv_k_a = ext_in("rwkv_k_a", [L * 1024, 1])
    k.rwkv_r_k = ext_in("rwkv_r_k", [L * 1024, 1])
    k.rwkv_ln_w = ext_in("rwkv_ln_w", [L * 1024, 1])
    k.rwkv_ln_b = ext_in("rwkv_ln_b", [L * 1024, 1])
    k.rwkv_w_o = ext_in("rwkv_w_o", [L * 1024, 1024])
    k.gla_w2 = ext_in("gla_w2", [L * 16, 512])
    k.gla_b = ext_in("gla_b", [L * 512, 1])
    k.gla_norm = ext_in("gla_norm", [L * 256, 1])
    k.gla_w_o = ext_in("gla_w_o", [L * 1024, 1024])
    k.w_out = ext_in("w_out", [L * 1024, 1024])
    k.rope_freq = ext_in("rope_freq", [64, 1])
    k.out = S.dram("out", [NB * T, D], F32, kind="ExternalOutput")
    okind = "ExternalOutput" if dbg else "Internal"
    k.xs = S.dram("xs", [NB * T, D], F32)
    k.mod = S.dram("modd", [L * NB, 3 * D], F32, kind=okind)
    k.PT = S.dram("PT", [PT_ROWS, T], F32, kind=okind)
    k.osz = [S.dram("osz%d" % i, [D, T], BF16, kind=okind) for i in range(3)]

    k.ident = S.sbuf("ident", [128, 128], BF16)
    k.ones = S.sbuf("onesb", [128, 128], BF16)
    k.identf = S.sbuf("identf", [128, 128], F32)
    S.memset(k.ones[:, :], 1.0)
    S.memset(k.identf[:, :], 1.0)
    S.op("pool", lambda e: e.affine_select(out=k.identf[:, :].ap, in_=k.identf[:, :].ap, pattern=[[1, 128]],
                                           compare_op=ALU.is_equal, fill=0.0, base=0, channel_multiplier=-1),
         [k.identf[:, :]], [k.identf[:, :]])
    S.copy(k.ident[:, :], k.identf[:, :], eng="pool")
    k.ps = [S.psum("ps%d" % i, [128, 512], F32) for i in range(6)]
    k.pb = [S.psum("pb%d" % i, [128, 1024], BF16) for i in range(2)]
    k_eps(S, k)
    lin_consts(S, k)
    S.flush()

    phase_mod(S, k)
    for l in range(nl):
        for b in range(NB):
            phase_norm(S, k, l, b)
            phase_inproj(S, k, l, b)
            if "mla" in k.phases:
                phase_mla(S, k, l, b)
            if "rwkv" in k.phases:
                phase_rwkv(S, k, l, b)
            if "gla" in k.phases:
                phase_gla(S, k, l, b)
            phase_final(S, k, l, b)
    S.flush()
    es.close()
    k.n_inst = S.n_inst
    return nc, k


def phase_mod(S, k):
    with S.scope():
        cT = S.sbuf("cT", [128, 8 * NB], F32)
        sT = S.sbuf("sT", [128, 8 * NB], F32)
        for b in range(NB):
            S.dma(cT[:, :].r("p (c b) -> p c b", b=NB).ix(slice(None), slice(None), slice(b, b + 1)),
                  V(k.c_in.t[b:b + 1, :].rearrange("o (c k) -> k c o", k=128), "c", (b, b + 1, 0, D)),
                  allow_slow_non_contiguous=True)
        S.act(sT[:, :], cT[:, :], AF.Silu)
        wa = [S.sbuf("wa%d" % i, [128, 8 * 512], F32) for i in range(2)]
        bb = S.sbuf("adab", [NB, 3 * D], F32)
        msb = S.sbuf("msb", [NB, 3 * D], F32)
        i = 0
        for l in range(L):
            S.dma(bb[:, :], V(k.ada_b.t[l:l + 1, :].partition_broadcast(NB), "ada_b", (l, l + 1, 0, 3 * D)))
            for cc in range(6):
                w = wa[i % 2]
                S.dma(w[:, :].r("p (c n) -> p c n", c=8),
                      V(k.ada_w.t[l * D:(l + 1) * D, cc * 512:(cc + 1) * 512].rearrange("(c k) n -> k c n", k=128),
                        "ada_w", (l * D, (l + 1) * D, cc * 512, (cc + 1) * 512)))
                ps = k.ps[i % 2]
                for kc in range(8):
                    S.mm(ps[0:NB, :], sT[:, kc * NB:(kc + 1) * NB], w[:, kc * 512:(kc + 1) * 512],
                         start=(kc == 0), stop=(kc == 7))
                S.tt(msb[:, cc * 512:(cc + 1) * 512], ps[0:NB, :], bb[:, cc * 512:(cc + 1) * 512], ALU.add)
                i += 1
            S.dma(k.mod[l * NB:(l + 1) * NB, :], msb[:, :], q="pool")


def bcast_row(buf, r, c0, c1, npart=128):
    return V(buf.t[r:r + 1, c0:c1].partition_broadcast(npart), buf.name, (r, r + 1, c0, c1))


def phase_norm(S, k, l, b):
    T = k.T
    k.es_hT = contextlib.ExitStack()
    old = S.es
    S.es = k.es_hT
    k.hT = S.sbuf("hT", [128, 8 * T], BF16)
    S.es = old
    with S.scope():
        Gb = S.sbuf("Gb", [128, D], F32)
        npb = S.sbuf("npb", [128, D], F32)
        shb = S.sbuf("shb", [128, D], F32)
        S.dma(Gb[:, :], bcast_row(k.mod, l * NB + b, D, 2 * D))
        S.dma(npb[:, :], bcast_row(k.norm_pre, l, 0, D))
        S.dma(shb[:, :], bcast_row(k.mod, l * NB + b, 0, D))
        S.stt(Gb[:, :], Gb[:, :], 1.0, npb[:, :], ALU.add, ALU.mult)
        xin = k.x_in if l == 0 else k.xs
        xt = [S.sbuf("xt%d" % i, [128, D], F32) for i in range(3)]
        junk = S.sbuf("junk", [128, D], F32)
        tmp = [S.sbuf("tmp%d" % i, [128, D], F32) for i in range(2)]
        hb = [S.sbuf("hb%d" % i, [128, D], BF16) for i in range(2)]
        st = [S.sbuf("st%d" % i, [128, 4], F32) for i in range(2)]
        for tt in range(T // 128):
            x = xt[tt % 3]
            s = st[tt % 2]
            S.dma(x[:, :], xin[b * T + tt * 128: b * T + (tt + 1) * 128, :])
            S.act(junk[:, :], x[:, :], AF.Square, accum=s[:, 0:1])
            S.act(s[:, 1:2], s[:, 0:1], AF.Sqrt, scale=1.0 / D, bias=k_eps(S, k))
            S.recip(s[:, 2:3], s[:, 1:2])
            S.stt(tmp[tt % 2][:, :], x[:, :], s[:, 2:3], Gb[:, :], ALU.mult, ALU.mult)
            S.tt(hb[tt % 2][:, :], tmp[tt % 2][:, :], shb[:, :], ALU.add, eng="pool")
            pb = k.pb[tt % 2]
            for c in range(8):
                S.tr(pb[:, c * 128:(c + 1) * 128], hb[tt % 2][:, c * 128:(c + 1) * 128], k.ident[:, :])
            dst = V(k.hT.t[:, :].rearrange("p (c t) -> p c t", c=8)[:, :, tt * 128:(tt + 1) * 128], "hT",
                    (0, 128, tt * 128, (tt + 1) * 128))
            if tt % 2 == 0:
                S.op("act", lambda e, dst=dst, pb=pb: e.activation(out=dst.ap, in_=pb[:, :].ap.rearrange("p (c t) -> p c t", c=8), func=AF.Copy),
                     [pb[:, :]], [dst])
            else:
                S.op("dve", lambda e, dst=dst, pb=pb: e.tensor_copy(out=dst.ap, in_=pb[:, :].ap.rearrange("p (c t) -> p c t", c=8)),
                     [pb[:, :]], [dst])


def k_eps(S, k):
    if not hasattr(k, "epsc"):
        k.epsc = S.sbuf("epsc", [128, 4], F32)
        S.memset(k.epsc[:, 0:1], EPS)
        S.memset(k.epsc[:, 1:2], 64e-5)
        S.memset(k.epsc[:, 2:3], 1.0)
        S.memset(k.epsc[:, 3:4], 0.0)
    return k.epsc[:, 0:1]


def phase_inproj(S, k, l, b):
    T = k.T
    blocks = []
    for name, c0, w, epi in SEGS:
        for j in range(0, w, 128):
            bw = min(128, w - j)
            blocks.append((name, SEG_ROW[name] + j, c0 + j, bw, epi))
    with S.scope():
        wf = [S.sbuf("wf%d" % i, [128, 8 * 128], F32) for i in range(2)]
        wb = [S.sbuf("wb%d" % i, [128, 8 * 128], BF16) for i in range(2)]
        raw = [S.sbuf("raw%d" % i, [128, T + 1], F32) for i in range(2)]
        stg = [S.sbuf("stg%d" % i, [128, 512], F32) for i in range(4)]
        tmp = [S.sbuf("ptmp%d" % i, [128, 512], F32) for i in range(2)]
        mu = [S.sbuf("mu%d" % i, [128, 2], F32) for i in range(2)]
        for r in raw:
            S.memset(r[:, 0:1], 0.0)
        ri = 0
        si = 0
        pi = 0
        ei = 0
        for bi, (name, row, col, bw, epi) in enumerate(blocks):
            f, w = wf[bi % 2], wb[bi % 2]
            S.dma(f[:, 0:8 * bw].r("p (c n) -> p c n", c=8),
                  V(k.w_in.t[l * D:(l + 1) * D, col:col + bw].rearrange("(c k) n -> k c n", k=128), "w_in",
                    (l * D, (l + 1) * D, col, col + bw)))
            S.copy(w[:, 0:8 * bw], f[:, 0:8 * bw], eng="pool")
            if epi == "shift":
                m = mu[ri % 2]
                rw = raw[ri % 2]
                ri += 1
                mo = col - 1856
                S.dma(m[0:bw, 0:1], k.rwkv_mu[l * 3200 + mo: l * 3200 + mo + bw, 0:1])
                S.ts(m[0:bw, 1:2], m[0:bw, 0:1], -1.0, ALU.mult, 1.0, ALU.add, eng="pool")
            for tc in range(T // 512):
                ps = k.ps[pi % 6]
                pi += 1
                for kc in range(8):
                    S.mm(ps[0:bw, :], w[:, kc * bw:(kc + 1) * bw], k.hT[:, kc * T + tc * 512: kc * T + (tc + 1) * 512],
                         start=(kc == 0), stop=(kc == 7))
                st = stg[si % 4]
                si += 1
                if epi == "shift":
                    S.copy(rw[0:bw, 1 + tc * 512: 1 + (tc + 1) * 512], ps[0:bw, :], eng="act")
                    tp = tmp[tc % 2]
                    S.ts(tp[0:bw, :], rw[0:bw, 1 + tc * 512: 1 + (tc + 1) * 512], m[0:bw, 1:2], ALU.mult)
                    S.stt(st[0:bw, :], rw[0:bw, tc * 512: (tc + 1) * 512], m[0:bw, 0:1], tp[0:bw, :], ALU.mult, ALU.add)
                elif epi == "copy":
                    S.copy(st[0:bw, :], ps[0:bw, :], eng=("act" if ei % 2 else "dve"))
                    ei += 1
                else:
                    fn = {"silu": AF.Silu, "sigmoid": AF.Sigmoid}[epi]
                    S.act(st[0:bw, :], ps[0:bw, :], fn)
                S.dma(k.PT[row:row + bw, tc * 512:(tc + 1) * 512], st[0:bw, :], q="pool")
    k.es_hT.close()


def phase_final(S, k, l, b):
    T = k.T
    with S.scope():
        wts = []
        wf = [S.sbuf("fwf%d" % i, [128, 8 * 128], F32) for i in range(2)]
        srcs = [k.mla_w_o, k.rwkv_w_o, k.gla_w_o, k.w_out]
        i = 0
        for wi, src in enumerate(srcs):
            wbuf = S.sbuf("fw%d" % wi, [128, 8 * 1024], BF16)
            wts.append(wbuf)
            for q8 in range(8):
                f = wf[i % 2]
                i += 1
                S.dma(f[:, :].r("p (c n) -> p c n", c=8),
                      V(src.t[l * D:(l + 1) * D, q8 * 128:(q8 + 1) * 128].rearrange("(c k) n -> k c n", k=128),
                        src.name, (l * D, (l + 1) * D, q8 * 128, (q8 + 1) * 128)))
                dst = V(wbuf.t[:, :].rearrange("p (c n) -> p c n", c=8)[:, :, q8 * 128:(q8 + 1) * 128], wbuf.name,
                        (0, 128, 0, 8 * 1024))
                S.op("pool", lambda e, dst=dst, f=f: e.tensor_copy(out=dst.ap, in_=f[:, :].ap.rearrange("p (c n) -> p c n", c=8)),
                     [f[:, :]], [dst])
        GN = S.sbuf("GN", [128, D], F32)
        npb = S.sbuf("fnpb", [128, D], F32)
        S.dma(GN[:, :], bcast_row(k.mod, l * NB + b, 2 * D, 3 * D))
        S.dma(npb[:, :], bcast_row(k.norm_post, l, 0, D))
        S.tt(GN[:, :], GN[:, :], npb[:, :], ALU.mult)
        osz = [[S.sbuf("fo%d_%d" % (x, i), [128, 8 * 512], BF16) for i in range(1)] for x in range(3)]
        gt = [[S.sbuf("fg%d_%d" % (x, i), [128, 512], F32) for i in range(2)] for x in range(3)]
        mg = [S.sbuf("fmg%d" % i, [128, 8 * 512], BF16) for i in range(2)]
        ma = [S.sbuf("fma%d" % i, [128, 512], F32) for i in range(2)]
        mb = [S.sbuf("fmb%d" % i, [128, 512], F32) for i in range(2)]
        xt = [S.sbuf("fx%d" % i, [128, D], F32) for i in range(2)]
        xo = [S.sbuf("fxo%d" % i, [128, D], F32) for i in range(2)]
        junk = S.sbuf("fjunk", [128, 512], F32)
        st = [S.sbuf("fst%d" % i, [128, 8], F32) for i in range(2)]
        xin = k.x_in if l == 0 else k.xs
        xout = k.xs if l < k.nl - 1 else k.out
        gi = 0
        pi = 0
        for tc in range(T // 512):
            for x in range(3):
                S.dma(osz[x][0][:, :].r("p (c t) -> p c t", c=8),
                      V(k.osz[x].t[:, tc * 512:(tc + 1) * 512].rearrange("(c k) t -> k c t", k=128), k.osz[x].name,
                        (0, D, tc * 512, (tc + 1) * 512)))
            m = mg[tc % 2]
            for ob in range(8):
                pss = []
                for x in range(3):
                    ps = k.ps[pi % 6]
                    pi += 1
                    pss.append(ps)
                    for kc in range(8):
                        S.mm(ps[:, :], wts[x][:, kc * 1024 + ob * 128: kc * 1024 + (ob + 1) * 128],
                             osz[x][0][:, kc * 512:(kc + 1) * 512], start=(kc == 0), stop=(kc == 7))
                g = [gt[x][gi % 2] for x in range(3)]
                gi += 1
                for x, nm in enumerate(("ga", "gb", "gc")):
                    r0 = SEG_ROW[nm] + ob * 128
                    S.dma(g[x][:, :], k.PT[r0:r0 + 128, tc * 512:(tc + 1) * 512])
                a_, b_ = ma[ob % 2], mb[ob % 2]
                S.tt(a_[:, :], pss[0][:, :], g[0][:, :], ALU.mult)
                S.tt(b_[:, :], pss[1][:, :], g[1][:, :], ALU.mult)
                S.tt(a_[:, :], a_[:, :], b_[:, :], ALU.add, eng="pool")
                S.tt(b_[:, :], pss[2][:, :], g[2][:, :], ALU.mult)
                S.tt(m[:, ob * 512:(ob + 1) * 512], a_[:, :], b_[:, :], ALU.add, eng="pool")
            for tb in range(4):
                tg = tc * 4 + tb
                x_ = xt[tg % 2]
                o_ = xo[tg % 2]
                s = st[tg % 2]
                S.dma(x_[:, :], xin[b * T + tg * 128: b * T + (tg + 1) * 128, :])
                pss = []
                for half in range(2):
                    ps = k.ps[pi % 6]
                    pi += 1
                    pss.append(ps)
                    for ob in range(8):
                        S.mm(ps[:, :], m[:, ob * 512 + tb * 128: ob * 512 + (tb + 1) * 128],
                             wts[3][:, ob * 1024 + half * 512: ob * 1024 + (half + 1) * 512],
                             start=(ob == 0), stop=(ob == 7))
                    S.act(junk[:, :], ps[:, :], AF.Square, accum=s[:, half:half + 1])
                S.tt(s[:, 2:3], s[:, 0:1], s[:, 1:2], ALU.add)
                S.act(s[:, 3:4], s[:, 2:3], AF.Sqrt, scale=1.0 / D, bias=k_eps(S, k))
                S.recip(s[:, 4:5], s[:, 3:4])
                for half in range(2):
                    sl = slice(half * 512, (half + 1) * 512)
                    S.stt(o_[:, sl], pss[half][:, :], s[:, 4:5], GN[:, sl], ALU.mult, ALU.mult)
                    S.tt(o_[:, sl], o_[:, sl], x_[:, sl], ALU.add, eng="pool")
                S.dma(xout[b * T + tg * 128: b * T + (tg + 1) * 128, :], o_[:, :], q="pool")


def dump(S, k, name, buf, dt):
    if not k.dbg or name in k.dumped:
        return
    k.dumped.add(name)
    d = S.dram("dbg_" + name, list(buf.shape), dt, kind="ExternalOutput")
    S.dma(d[:, :], buf[:, :], q="pool")


def dview(buf, r0, r1, c0, c1, pat=None, **kw):
    ap = buf.t[r0:r1, c0:c1]
    if pat:
        ap = ap.rearrange(pat, **kw)
    return V(ap, buf.name, (r0, r1, c0, c1))


def col_load(S, dst, src, r0, n, q="sp"):
    S.dma(dst, dview(src, r0, r0 + n * 128, 0, 1, "(c k) o -> k (c o)", k=128), q=q, allow_slow_non_contiguous=True)


def rope_tables(S, k, b, cosT, sinT):
    T = k.T
    with S.scope():
        ff = S.sbuf("ff", [64, 2], F32)
        S.dma(ff[:, 0:1], k.rope_freq[:, :])
        pi_ = S.sbuf("posi", [64, T], I32)
        ang = S.sbuf("ang", [64, T], F32)
        kf = S.sbuf("kf", [64, T], F32)
        S.dma(pi_[:, :], bcast_row(k.pos_in, b, 0, T, 64))
        S.copy(ang[:, :], pi_[:, :])
        S.ts(ang[:, :], ang[:, :], ff[:, 0:1], ALU.mult)
        TWO_PI = float(2 * np.pi)
        MAGIC = 12582912.0
        PI_LO = 3.1415925
        for which, dst in ((0, sinT), (1, cosT)):
            off = 0.0 if which == 0 else float(np.pi / 2)
            S.ts(dst[:, :], ang[:, :], off, ALU.add)
            S.ts(kf[:, :], dst[:, :], 1.0 / TWO_PI, ALU.mult)
            S.ts(kf[:, :], kf[:, :], MAGIC, ALU.add)
            S.ts(kf[:, :], kf[:, :], MAGIC, ALU.subtract)
            S.stt(dst[:, :], kf[:, :], -TWO_PI, dst[:, :], ALU.mult, ALU.add)
            S.ts(dst[:, :], dst[:, :], PI_LO, ALU.min, -PI_LO, ALU.max)
            S.act(dst[:, :], dst[:, :], AF.Sin)
        S.ts(sinT[0:32, :], sinT[0:32, :], -1.0, ALU.mult)


def phase_mla(S, k, l, b):
    T = k.T
    nt = T // 512
    nblk = T // 128
    SC = float((128 + 64) ** -0.5)
    with S.scope():
        cqn = S.sbuf("cqn", [128, 4 * T], BF16)
        ckvn = S.sbuf("ckvn", [128, 2 * T], BF16)
        krT = S.sbuf("krT", [64, T], BF16)
        cosT = S.sbuf("cosT", [64, T], F32)
        sinT = S.sbuf("sinT", [64, T], F32)
        gq = S.sbuf("gq", [128, 8], F32)
        col_load(S, gq[:, 0:4], k.mla_q_norm, l * 512, 4)
        col_load(S, gq[:, 4:6], k.mla_kv_norm, l * 256, 2)
        rope_tables(S, k, b, cosT, sinT)
        with S.scope():
            cfb = [S.sbuf("cf%d" % i, [128, 4 * 512], F32) for i in range(2)]
            sqb = [S.sbuf("sq%d" % i, [128, 4 * 512], BF16) for i in range(2)]
            sd = [S.sbuf("sd%d" % i, [128, 512], F32) for i in range(2)]
            i = 0
            for name, nb_, dst, gcol in (("cq", 4, cqn, 0), ("ckv", 2, ckvn, 4)):
                r0 = SEG_ROW[name]
                for tc in range(nt):
                    cf, sq, s_ = cfb[i % 2], sqb[i % 2], sd[i % 2]
                    ps = k.ps[i % 2]
                    i += 1
                    S.dma(cf[:, 0:nb_ * 512].r("p (c t) -> p c t", c=nb_),
                          dview(k.PT, r0, r0 + nb_ * 128, tc * 512, (tc + 1) * 512, "(c k) t -> k c t", k=128))
                    S.act(sq[:, 0:nb_ * 512], cf[:, 0:nb_ * 512], AF.Square)
                    for c in range(nb_):
                        S.mm(ps[:, :], k.ones[:, :], sq[:, c * 512:(c + 1) * 512], start=(c == 0), stop=(c == nb_ - 1))
                    S.act(s_[:, :], ps[:, :], AF.Sqrt, scale=1.0 / (nb_ * 128), bias=k.epsc[:, 0:1])
                    S.recip(s_[:, :], s_[:, :])
                    for c in range(nb_):
                        S.stt(dst[:, c * T + tc * 512: c * T + (tc + 1) * 512], cf[:, c * 512:(c + 1) * 512],
                              gq[:, gcol + c:gcol + c + 1], s_[:, :], ALU.mult, ALU.mult)
            r0 = SEG_ROW["kr"]
            for tc in range(nt):
                cf = cfb[tc % 2]
                sl = slice(tc * 512, (tc + 1) * 512)
                S.dma(cf[0:64, 0:512], k.PT[r0:r0 + 64, sl])
                S.dma(cf[0:32, 512:1024], k.PT[r0 + 32:r0 + 64, sl])
                S.dma(cf[32:64, 512:1024], k.PT[r0:r0 + 32, sl])
                S.tt(cf[0:64, 1024:1536], cf[0:64, 0:512], cosT[:, sl], ALU.mult)
                S.tt(cf[0:64, 1536:2048], cf[0:64, 512:1024], sinT[:, sl], ALU.mult, eng="pool")
                S.tt(krT[:, sl], cf[0:64, 1024:1536], cf[0:64, 1536:2048], ALU.add)
        QN = S.sbuf("QN", [128, T], BF16)
        QR = S.sbuf("QR", [64, T], BF16)
        KN = S.sbuf("KN", [128, T], BF16)
        VA = S.sbuf("VA", [128, nblk * 130], BF16)
        S.op("pool", lambda e: e.memset(VA[:, :].ap.rearrange("p (n c) -> p n c", c=130)[:, :, 128:130], 1.0), [], [VA[:, :]])
        wqf = S.sbuf("wqf", [128, 4 * 256], F32)
        wqb = S.sbuf("wqb", [128, 4 * 256], BF16)
        wkf = S.sbuf("wkf", [128, 2 * 256], F32)
        wkb = S.sbuf("wkb", [128, 2 * 256], BF16)
        rt = [S.sbuf("rt%d" % i, [64, 512], F32) for i in range(2)]
        pt = [S.sbuf("ptt%d" % i, [128, 512], BF16) for i in range(3)]
        mz = [S.sbuf("mzz%d" % i, [128, 512], F32) for i in range(2)]
        ost = [S.sbuf("ost%d" % i, [128, 512], BF16) for i in range(2)]
        on = [S.sbuf("on%d" % i, [128, 128], BF16) for i in range(2)]
        rl = [S.sbuf("rl%d" % i, [128, 1], F32) for i in range(2)]
        ui = 0
        oi = 0
        for h in range(8):
            wq3 = wqf[:, :].r("p (c n) -> p c n", c=4)
            c0 = h * 192
            S.dma(wq3.ix(slice(None), slice(None), slice(0, 192)),
                  dview(k.mla_w_uq, l * 512, (l + 1) * 512, c0, c0 + 192, "(c k) n -> k c n", k=128))
            S.dma(wq3.ix(slice(None), slice(None), slice(192, 224)),
                  dview(k.mla_w_uq, l * 512, (l + 1) * 512, c0 + 160, c0 + 192, "(c k) n -> k c n", k=128))
            S.dma(wq3.ix(slice(None), slice(None), slice(224, 256)),
                  dview(k.mla_w_uq, l * 512, (l + 1) * 512, c0 + 128, c0 + 160, "(c k) n -> k c n", k=128))
            S.copy(wqb[:, :], wqf[:, :], eng="pool")
            S.dma(wkf[:, :].r("p (c n) -> p c n", c=2),
                  dview(k.mla_w_ukv, l * 256, (l + 1) * 256, h * 256, (h + 1) * 256, "(c k) n -> k c n", k=128))
            S.copy(wkb[:, :], wkf[:, :], eng="pool")
            for tc in range(nt):
                sl = slice(tc * 512, (tc + 1) * 512)
                p0, p1, p2, p3, p4 = k.ps[0], k.ps[1], k.ps[2], k.ps[3], k.ps[4]
                for kc in range(4):
                    S.mm(p0[:, :], wqb[:, kc * 256:kc * 256 + 128], cqn[:, kc * T + tc * 512:kc * T + (tc + 1) * 512],
                         start=(kc == 0), stop=(kc == 3))
                S.copy(QN[:, sl], p0[:, :], eng="act")
                for kc in range(4):
                    S.mm(p1[0:64, :], wqb[:, kc * 256 + 128:kc * 256 + 192], cqn[:, kc * T + tc * 512:kc * T + (tc + 1) * 512],
                         start=(kc == 0), stop=(kc == 3))
                for kc in range(4):
                    S.mm(p2[0:64, :], wqb[:, kc * 256 + 192:kc * 256 + 256], cqn[:, kc * T + tc * 512:kc * T + (tc + 1) * 512],
                         start=(kc == 0), stop=(kc == 3))
                S.tt(rt[0][:, :], p1[0:64, :], cosT[:, sl], ALU.mult)
                S.tt(rt[1][:, :], p2[0:64, :], sinT[:, sl], ALU.mult)
                S.tt(QR[:, sl], rt[0][:, :], rt[1][:, :], ALU.add, eng="pool")
                for kc in range(2):
                    S.mm(p3[:, :], wkb[:, kc * 256:kc * 256 + 128], ckvn[:, kc * T + tc * 512:kc * T + (tc + 1) * 512],
                         start=(kc == 0), stop=(kc == 1))
                S.copy(KN[:, sl], p3[:, :], eng="act")
                for tb in range(4):
                    t0 = tc * 512 + tb * 128
                    for kc in range(2):
                        S.mm(p4[:, tb * 128:(tb + 1) * 128], ckvn[:, kc * T + t0:kc * T + t0 + 128],
                             wkb[:, kc * 256 + 128:kc * 256 + 256], start=(kc == 0), stop=(kc == 1))
                dst = V(VA.t[:, tc * 4 * 130:(tc + 1) * 4 * 130].rearrange("p (n c) -> p n c", c=130)[:, :, 0:128], VA.name,
                        (0, 128, tc * 4 * 130, (tc + 1) * 4 * 130))
                S.op("dve", lambda e, dst=dst, p4=p4: e.tensor_copy(out=dst.ap, in_=p4[:, :].ap.rearrange("p (n c) -> p n c", c=128)),
                     [p4[:, :]], [dst])
            if h == 7 and l == k.nl - 1 and b == 1:
                for nm, bf in (("QN", QN), ("QR", QR), ("KN", KN), ("VA", VA), ("krT", krT), ("cqn", cqn), ("ckvn", ckvn)):
                    dump(S, k, nm, bf, BF16)
                dump(S, k, "cosT", cosT, F32)
                dump(S, k, "sinT", sinT, F32)
            units = [(qc, kb) for qc in range(nt) for kb in range(4 * qc + 4)]
            pend = None

            def pv(qc, kb, ptile, j0):
                nonlocal oi
                for j in range(j0, 4):
                    ob = k.ps[2 + j]
                    qoff = (j - j0) * 128
                    S.mm(ob[:, 0:130], ptile[:, qoff:qoff + 128], VA[:, kb * 130:(kb + 1) * 130],
                         start=(kb == 0), stop=(kb == 4 * qc + j))
                    if kb == 4 * qc + j:
                        r_, o_ = rl[oi % 2], on[oi % 2]
                        pb = k.pb[oi % 2]
                        oi += 1
                        S.recip(r_[:, :], ob[:, 128:129])
                        S.ts(o_[:, :], ob[:, 0:128], r_[:, 0:1], ALU.mult)
                        S.tr(pb[:, 0:128], o_[:, :], k.ident[:, :])
                        S.tt(ost[qc % 2][:, j * 128:(j + 1) * 128], pb[:, 0:128], mz[qc % 2][:, j * 128:(j + 1) * 128], ALU.mult)
                        if j == 3:
                            S.dma(k.osz[0][h * 128:(h + 1) * 128, qc * 512:(qc + 1) * 512], ost[qc % 2][:, :], q="pool")

            for (qc, kb) in units:
                if kb == 0:
                    r0 = SEG_ROW["mz"] + h * 128
                    S.dma(mz[qc % 2][:, :], k.PT[r0:r0 + 128, qc * 512:(qc + 1) * 512])
                j0 = max(0, kb - 4 * qc)
                n = 512 - j0 * 128
                q0 = qc * 512 + j0 * 128
                ps = k.ps[ui % 2]
                ptile = pt[ui % 3]
                ui += 1
                S.mm(ps[:, 0:n], KN[:, kb * 128:(kb + 1) * 128], QN[:, q0:q0 + n], start=True, stop=False)
                S.mm(ps[:, 0:n], krT[:, kb * 128:(kb + 1) * 128], QR[:, q0:q0 + n], start=False, stop=True)
                S.act(ptile[:, 0:n], ps[:, 0:n], AF.Exp, scale=SC)
                if kb >= 4 * qc:
                    S.memset(ptile[64:128, 0:64], 0.0, eng="pool")
                if pend is not None:
                    pv(*pend)
                pend = (qc, kb, ptile, j0)
            pv(*pend)


def lin_consts(S, k):
    f = S.sbuf("mtmp", [128, 128], F32)
    k.mIU = S.sbuf("mIU", [128, 128], F32)
    k.mSU = S.sbuf("mSU", [128, 128], F32)
    k.mSL = S.sbuf("mSL", [128, 128], F32)
    for m, pat, cm, cmp_ in ((k.mIU, 1, -1, ALU.is_ge), (k.mSU, 1, -1, ALU.is_gt), (k.mSL, -1, 1, ALU.is_gt)):
        S.memset(f[:, :], 1.0)
        S.op("pool", lambda e, m=m, pat=pat, cm=cm, cmp_=cmp_: e.affine_select(
            out=m[:, :].ap, in_=f[:, :].ap, pattern=[[pat, 128]], compare_op=cmp_, fill=0.0, base=0, channel_multiplier=cm),
            [f[:, :]], [m[:, :]])
    k.rmask = S.sbuf("rmask", [128, 512], F32)
    S.memset(k.rmask[:, :], 1.0)
    S.op("pool", lambda e: e.memset(k.rmask[:, :].ap.rearrange("p (n c) -> p n c", c=128)[:, :, 0:1], 0.0), [], [k.rmask[:, :]])
    k.bones = S.sbuf("bones", [128, 128], BF16)
    S.memset(k.bones[:, :], 0.0)
    S.memset(k.bones[0:64, 0:64], 1.0)
    S.memset(k.bones[64:128, 64:128], 1.0)


def scan_cum(S, out, la, rmask):
    S.op("dve", lambda e: e.tensor_tensor_scan(out=out.ap, data0=rmask.ap, data1=la.ap, initial=0.0, op0=ALU.mult, op1=ALU.add),
         [rmask, la], [out])


def c3(v, c=128):
    return v.r("p (n c) -> p n c", c=c)


def phase_rwkv(S, k, l, b):
    T = k.T
    nt = T // 512
    with S.scope():
        TW = S.sbuf("TW", [64, T], BF16)
        AL = S.sbuf("AL", [64, T], BF16)
        w2f = S.sbuf("w2f", [64, 2048], F32)
        w2b = S.sbuf("w2b", [64, 2048], BF16)
        S.dma(w2f[:, 0:1024], k.rwkv_w2[l * 64:(l + 1) * 64, :])
        S.dma(w2f[:, 1024:2048], k.rwkv_a2[l * 64:(l + 1) * 64, :])
        S.copy(w2b[:, :], w2f[:, :], eng="pool")
        cols = S.sbuf("rcols", [128, 8 * 9], F32)
        for i, src in enumerate((k.rwkv_w0, k.rwkv_a0, k.rwkv_k_k, k.rwkv_k_a, k.rwkv_k_a, k.rwkv_r_k, k.rwkv_ln_w, k.rwkv_ln_b)):
            col_load(S, cols[:, i * 8:(i + 1) * 8], src, l * 1024, 8)
        S.ts(cols[:, 32:40], cols[:, 32:40], -1.0, ALU.mult, 1.0, ALU.add)
        rwa = SEG_ROW["rwa"]
        f = [S.sbuf("rf%d" % i, [128, 512], F32) for i in range(20)]
        for tc in range(nt):
            sl = slice(tc * 512, (tc + 1) * 512)
            S.dma(f[0][0:64, :], k.PT[rwa:rwa + 64, sl])
            S.dma(f[1][0:64, :], k.PT[rwa + 64:rwa + 128, sl])
            S.act(TW[:, sl], f[0][0:64, :], AF.Tanh)
            S.copy(AL[:, sl], f[1][0:64, :])
        ARH = S.sbuf("ARH", [128, 1024], BF16)
        BH = S.sbuf("BH", [128, 512], BF16)
        KH = S.sbuf("KH", [128, 512], BF16)
        FE = [S.sbuf("FE%d" % i, [128, 512], BF16) for i in range(4)]
        TK = [S.sbuf("TK%d" % i, [128, 512], BF16) for i in range(4)]
        WT = S.sbuf("WT", [128, 512], BF16)
        ZT = [[S.sbuf("ZT%d_%d" % (i, j), [128, 256], mybir.dt.float32r) for j in range(2)] for i in range(8)]
        NP = [[S.sbuf("NP%d_%d" % (i, j), [128, 128], mybir.dt.float32r) for j in range(2)] for i in range(8)]
        TTb = [S.sbuf("TTb%d" % i, [128, 128], BF16) for i in range(8)]
        AB = [S.sbuf("AB%d" % i, [128, 128], BF16) for i in range(8)]
        AK = [S.sbuf("AK%d" % i, [128, 128], BF16) for i in range(8)]
        AR = [S.sbuf("AR%d" % i, [128, 128], BF16) for i in range(8)]
        GS = [S.sbuf("GS%d" % i, [128, 64], BF16) for i in range(8)]
        UT = [S.sbuf("UT%d" % i, [128, 64], F32) for i in range(8)]
        St = S.sbuf("St", [128, 64], F32)
        Stb = S.sbuf("Stb", [128, 64], BF16)
        Ub = [S.sbuf("Ub%d" % i, [128, 128], BF16) for i in range(2)]
        pcb = S.sbuf("pcb", [128, 4], F32)
        sm = [S.sbuf("sm%d" % i, [128, 16], F32) for i in range(2)]
        sqy = S.sbuf("sqy", [128, 128], F32)
        yn = [S.sbuf("yn%d" % i, [128, 128], BF16) for i in range(2)]
        yf = [S.sbuf("yf%d" % i, [128, 128], F32) for i in range(2)]
        ost = [S.sbuf("rost%d" % i, [128, 512], BF16) for i in range(2)]
        pi = 0

        def PS():
            nonlocal pi
            pi += 1
            return k.ps[pi % 6]

        for hb in range(8):
            cw0, ca0, ckk, cka, comka, crk, clw, clb = [cols[:, i * 8 + hb:i * 8 + hb + 1] for i in range(8)]
            S.memset(St[:, :], 0.0)
            S.memset(Stb[:, :], 0.0)
            for tc in range(nt):
                sl = slice(tc * 512, (tc + 1) * 512)
                r_, kx, v_, lw, a_, kt, kk, kp, cum, e1, e2, ka, bon, rz, tmp, tmp2 = f[:16]
                for dst, nm in ((r_, "rr"), (kx, "rk"), (v_, "rv"), (rz, "rz")):
                    r0 = SEG_ROW[nm] + hb * 128
                    S.dma(dst[:, :], k.PT[r0:r0 + 128, sl])
                p = PS()
                S.mm(p[:, :], w2b[:, hb * 128:(hb + 1) * 128], TW[:, sl])
                S.act(lw[:, :], p[:, :], AF.Sigmoid, bias=cw0)
                S.ts(lw[:, :], lw[:, :], -0.6065306597126334, ALU.mult)
                p = PS()
                S.mm(p[:, :], w2b[:, 1024 + hb * 128:1024 + (hb + 1) * 128], AL[:, sl])
                S.act(a_[:, :], p[:, :], AF.Sigmoid, bias=ca0)
                S.ts(kt[:, :], kx[:, :], ckk, ALU.mult)
                S.act(FE[0][:, :], kt[:, :], AF.Square)
                p = PS()
                S.mm(p[:, :], k.bones[:, :], FE[0][:, :])
                S.act(tmp[:, :], p[:, :], AF.Sqrt)
                S.ts(tmp[:, :], tmp[:, :], 1e-12, ALU.max)
                S.recip(tmp[:, :], tmp[:, :])
                S.tt(kk[:, :], kt[:, :], tmp[:, :], ALU.mult)
                S.ts(tmp[:, :], a_[:, :], cka, ALU.mult, comka, ALU.add)
                S.tt(kp[:, :], kx[:, :], tmp[:, :], ALU.mult)
                S.tt(tmp[:, :], r_[:, :], kp[:, :], ALU.mult, eng="pool")
                S.ts(FE[0][:, :], tmp[:, :], crk, ALU.mult)
                p = PS()
                S.mm(p[:, :], k.bones[:, :], FE[0][:, :])
                S.tt(bon[:, :], v_[:, :], p[:, :], ALU.mult)
                scan_cum(S, cum[:, :], lw[:, :], k.rmask[:, :])
                cend = c3(cum[:, :]).ix(slice(None), slice(None), slice(127, 128))
                S.op("act", lambda e, cend=cend: e.activation(out=pcb[:, :].ap.rearrange("p (n o) -> p n o", o=1), in_=cend.ap, func=AF.Exp),
                     [cum[:, :]], [pcb[:, :]])
                S.act(e1[:, :], cum[:, :], AF.Exp)
                A3 = ARH[:, :].r("p (n two t) -> p n two t", two=2, t=128)
                S.op("dve", lambda e, A3=A3, r_=r_, e1=e1: e.tensor_tensor(out=A3.ap[:, :, 1, :], in0=c3(r_[:, :]).ap, in1=c3(e1[:, :]).ap, op=ALU.mult),
                     [r_[:, :], e1[:, :]], [ARH[:, :]])
                S.act(e2[:, :], cum[:, :], AF.Exp, scale=-1.0)
                S.tt(ka[:, :], kk[:, :], a_[:, :], ALU.mult, eng="pool")
                S.tt(BH[:, :], ka[:, :], e2[:, :], ALU.mult)
                S.tt(KH[:, :], kp[:, :], e2[:, :], ALU.mult)
                S.tt(tmp[:, :], cum[:, :], lw[:, :], ALU.subtract, eng="pool")
                S.act(e1[:, :], tmp[:, :], AF.Exp)
                S.stt(FE[0][:, :], kk[:, :], -1.0, e1[:, :], ALU.mult, ALU.mult)
                S.op("pool", lambda e, A3=A3: e.tensor_copy(out=A3.ap[:, :, 0, :], in_=c3(FE[0][:, :]).ap), [FE[0][:, :]], [ARH[:, :]])
                S.op("dve", lambda e, tmp2=tmp2, cum=cum, cend=cend: e.tensor_tensor(
                    out=c3(tmp2[:, :]).ap, in0=cend.ap.broadcast_to([128, 4, 128]), in1=c3(cum[:, :]).ap, op=ALU.subtract),
                    [cum[:, :]], [tmp2[:, :]])
                S.act(e2[:, :], tmp2[:, :], AF.Exp)
                S.tt(FE[1][:, :], ka[:, :], e2[:, :], ALU.mult)
                S.tt(FE[2][:, :], kp[:, :], e2[:, :], ALU.mult)
                S.copy(FE[3][:, :], v_[:, :], eng="pool")
                for i in range(4):
                    pb = k.pb[i % 2]
                    for c in range(4):
                        S.tr(pb[:, c * 128:(c + 1) * 128], FE[i][:, c * 128:(c + 1) * 128], k.ident[:, :])
                    S.copy(TK[i][:, :], pb[:, 0:512], eng=("act" if i % 2 else "dve"))
                if k.rstage < 1.1:
                    continue
                for c in range(4):
                    for e_ in range(k.ne):
                        ii = c * 2 + e_
                        rows = slice(e_ * 64, (e_ + 1) * 64)
                        ah = ARH[rows, c * 256:c * 256 + 128]
                        arh = ARH[rows, c * 256:c * 256 + 256]
                        bh = BH[rows, c * 128:(c + 1) * 128]
                        kh = KH[rows, c * 128:(c + 1) * 128]
                        p = PS()
                        S.mm(p[:, 0:128], ah, bh)
                        S.tt(NP[ii][0][:, :], p[:, 0:128], k.mSL[:, :], ALU.mult)
                        p = PS()
                        S.mm(p[:, 128:384], bh, arh)
                        S.tt(ZT[ii][0][:, 0:128], p[:, 128:256], k.mSU[:, :], ALU.mult)
                        S.tt(AB[ii][:, :], p[:, 256:384], k.mIU[:, :], ALU.mult)
                        p = PS()
                        S.mm(p[:, 0:256], kh, arh)
                        S.tt(AK[ii][:, :], p[:, 0:128], k.mSU[:, :], ALU.mult)
                        S.tt(AR[ii][:, :], p[:, 128:256], k.mIU[:, :], ALU.mult)
                        S.tt(ZT[ii][1][:, 128:256], ZT[ii][0][:, 0:128], k.identf[:, :], ALU.add, eng="pool")
                for lv in range(1, 7):
                    cur, nxt = (lv - 1) % 2, lv % 2
                    for ii in range(8):
                        p = PS()
                        S.mm(p[:, 0:128], ZT[ii][cur][:, 0:128], NP[ii][cur][:, :])
                        S.copy(NP[ii][nxt][:, :], p[:, 0:128], eng="act")
                        if 2 <= lv < 6:
                            p = PS()
                            S.mm(p[:, 0:256], NP[ii][cur][:, :], ZT[ii][cur][:, 0:256])
                            S.copy(ZT[ii][nxt][:, 0:128], p[:, 0:128], eng="dve")
                            S.tt(ZT[ii][nxt][:, 128:256], p[:, 128:256], ZT[ii][cur][:, 128:256], ALU.add)
                        elif lv < 6:
                            p = PS()
                            S.mm(p[:, 0:128], NP[ii][cur][:, :], ZT[ii][cur][:, 0:128])
                            S.copy(ZT[ii][nxt][:, 0:128], p[:, 0:128], eng="act")
                        else:
                            p = PS()
                            S.mm(p[:, 0:128], NP[ii][cur][:, :], ZT[ii][cur][:, 128:256])
                            S.tt(ZT[ii][nxt][:, 128:256], p[:, 0:128], ZT[ii][cur][:, 128:256], ALU.add)
                for ii in range(8):
                    p = PS()
                    S.mm(p[:, 0:128], NP[ii][0][:, :], ZT[ii][0][:, 128:256])
                    S.tt(TTb[ii][:, :], p[:, 0:128], ZT[ii][0][:, 128:256], ALU.add)
                fin = 6 % 2
                for c in range(4):
                    for e_ in range(k.ne if k.rstage > 1.5 else 0):
                        ii = c * 2 + e_
                        rows = slice(e_ * 64, (e_ + 1) * 64)
                        tt_ = TTb[ii][:, :]
                        hs = slice(c * 128 + e_ * 64, c * 128 + (e_ + 1) * 64)
                        p = PS()
                        S.mm(p[rows, 0:128], TK[0][:, hs], tt_)
                        S.copy(WT[rows, c * 128:(c + 1) * 128], p[rows, 0:128], eng="act")
                        p = PS()
                        S.mm(p[:, 128:192], AK[ii][:, :], TK[3][:, hs])
                        S.copy(GS[ii][:, :], p[:, 128:192])
                        p = PS()
                        S.mm(p[:, 256:320], tt_, GS[ii][:, :])
                        S.copy(UT[ii][:, :], p[:, 256:320], eng="act")
                if k.rstage < 3:
                    continue
                for c in range(4):
                    ub = Ub[c % 2]
                    pUs = (PS(), PS())
                    pYs = (PS(), PS())
                    R = [slice(e_ * 64, (e_ + 1) * 64) for e_ in range(2)]
                    HS = [slice(c * 128 + e_ * 64, c * 128 + (e_ + 1) * 64) for e_ in range(2)]
                    for e_ in range(2):
                        S.mm(pUs[e_][:, 0:64], WT[R[e_], c * 128:(c + 1) * 128], Stb[R[e_], :])
                    for e_ in range(2):
                        ii = c * 2 + e_
                        S.mm(pYs[e_][:, 0:64], ARH[R[e_], c * 256 + 128:c * 256 + 256], Stb[R[e_], :], start=True, stop=False)
                        S.mm(pYs[e_][:, 0:64], AR[ii][:, :], TK[3][:, HS[e_]], start=False, stop=False)
                    for e_ in range(2):
                        ii = c * 2 + e_
                        S.tt(ub[:, R[e_]], pUs[e_][:, 0:64], UT[ii][:, :], ALU.add)
                    for e_ in range(2):
                        ii = c * 2 + e_
                        S.mm(pYs[e_][:, 0:64], AB[ii][:, :], ub[:, R[e_]], start=False, stop=True)
                    pSs = (PS(), PS())
                    for e_ in range(2):
                        S.mm(pSs[e_][R[e_], 0:64], TK[1][:, HS[e_]], ub[:, R[e_]], start=True, stop=False)
                        S.mm(pSs[e_][R[e_], 0:64], TK[2][:, HS[e_]], TK[3][:, HS[e_]], start=False, stop=True)
                    for e_ in range(2):
                        S.stt(Stb[R[e_], :], St[R[e_], :], pcb[R[e_], c:c + 1], pSs[e_][R[e_], 0:64], ALU.mult, ALU.add)
                    for e_ in range(2):
                        S.stt(St[R[e_], :], St[R[e_], :], pcb[R[e_], c:c + 1], pSs[e_][R[e_], 0:64], ALU.mult, ALU.add)

                    s_ = sm[c % 2]
                    for e_ in range(2):
                        S.op("dve", lambda e, s_=s_, e_=e_, py=pYs[e_]: e.tensor_reduce(out=s_[:, e_:e_ + 1].ap, in_=py[:, 0:64].ap, axis=AX.X, op=ALU.add),
                             [pYs[e_][:, 0:64]], [s_[:, e_:e_ + 1]])
                        S.act(sqy[:, e_ * 64:(e_ + 1) * 64], pYs[e_][:, 0:64], AF.Square)
                    S.op("dve", lambda e, s_=s_: e.tensor_reduce(out=s_[:, 2:4].ap, in_=sqy[:, :].ap.rearrange("p (h n) -> p h n", h=2), axis=AX.X, op=ALU.add),
                         [sqy[:, :]], [s_[:, 2:4]])
                    S.ts(s_[:, 4:6], s_[:, 0:2], 1.0 / 64, ALU.mult)
                    S.tt(s_[:, 6:8], s_[:, 4:6], s_[:, 4:6], ALU.mult)
                    S.stt(s_[:, 8:10], s_[:, 2:4], 1.0 / 64, s_[:, 6:8], ALU.mult, ALU.subtract)
                    S.act(s_[:, 10:12], s_[:, 8:10], AF.Sqrt, bias=k.epsc[:, 1:2])
                    S.recip(s_[:, 12:14], s_[:, 10:12])
                    y_ = yn[c % 2]
                    for e_ in range(2):
                        hc = slice(e_ * 64, (e_ + 1) * 64)
                        S.ts(y_[:, hc], pYs[e_][:, 0:64], s_[:, 4 + e_:5 + e_], ALU.subtract, s_[:, 12 + e_:13 + e_], ALU.mult)
                    pb = k.pb[c % 2]
                    S.tr(pb[:, 0:128], y_[:, :], k.ident[:, :])
                    yf_ = yf[c % 2]
                    S.act(yf_[:, :], pb[:, 0:128], AF.Identity, scale=clw, bias=clb)
                    S.tt(yf_[:, :], yf_[:, :], bon[:, c * 128:(c + 1) * 128], ALU.add, eng="pool")
                    S.tt(ost[tc % 2][:, c * 128:(c + 1) * 128], yf_[:, :], rz[:, c * 128:(c + 1) * 128], ALU.mult, eng="pool")
                S.dma(k.osz[1][hb * 128:(hb + 1) * 128, sl], ost[tc % 2][:, :], q="pool")
                if k.dbg and hb == k.dhb and tc == 0 and b == 1:
                    for nm, bf, dt in (("lw", lw, F32), ("a", a_, F32), ("kk", kk, F32), ("kp", kp, F32), ("cum", cum, F32), ("bon", bon, F32),
                                       ("ARH", ARH, BF16), ("BH", BH, BF16), ("KH", KH, BF16), ("TK0", TK[0], BF16), ("TK1", TK[1], BF16),
                                       ("TK2", TK[2], BF16), ("TK3", TK[3], BF16), ("WT", WT, BF16), ("NP0", NP[0][0], mybir.dt.float32r),
                                       ("ZT0", ZT[0][0], mybir.dt.float32r), ("UT0", UT[0], F32), ("AB0", AB[0], BF16), ("AK0", AK[0], BF16), ("AR0", AR[0], BF16),
                                       ("St", St, F32), ("pcb", pcb, F32)):
                        dump(S, k, "r_" + nm, bf, dt)


PHASES = ("mla", "rwkv", "gla")


def _core_inputs(ins, core, T):
    d = {}
    bs = slice(core * NB, (core + 1) * NB)
    d["x"] = np.ascontiguousarray(ins["x"][bs, :T]).reshape(NB * T, D)
    d["c"] = np.ascontiguousarray(ins["c"][bs])
    d["positions"] = np.ascontiguousarray(ins["positions"][bs, :T]).astype(np.int32)
    fr = (10000.0 ** (-np.arange(0, 64, 2, dtype=np.float32) / 64)).astype(np.float32)
    d["rope_freq"] = np.concatenate([fr, fr]).reshape(64, 1)
    for k_, v in ins.items():
        if k_ in ("x", "c", "positions"):
            continue
        v = np.ascontiguousarray(v, dtype=np.float32)
        if k_ in ("ada_b", "norm_pre", "norm_post"):
            d[k_] = v
        elif v.ndim == 2 or k_ == "rwkv_r_k":
            d[k_] = v.reshape(-1, 1)
        else:
            d[k_] = v.reshape(v.shape[0] * v.shape[1], v.shape[2])
    return d


def kernel(**inputs):
    T = inputs["x"].shape[1]
    nc, k = build(T=T, dbg=False, phases=PHASES)
    in_maps = [_core_inputs(inputs, c, T) for c in range(8)]
    res = run_bass_kernel_spmd(nc, in_maps, core_ids=list(range(8)))
    out = np.stack([r["out"].reshape(NB, T, D) for r in res.results], axis=0).reshape(8 * NB, T, D)
    return out.astype(np.float32)


def phase_gla(S, k, l, b):
    T = k.T
    nt = T // 512
    DKS = float(128 ** -0.5)
    with S.scope():
        w2f = S.sbuf("gw2f", [16, 512], F32)
        S.dma(w2f[:, :], k.gla_w2[l * 16:(l + 1) * 16, :])
        cols = S.sbuf("gcols", [128, 8], F32)
        col_load(S, cols[:, 0:4], k.gla_b, l * 512, 4)
        col_load(S, cols[:, 4:6], k.gla_norm, l * 256, 2)
        S.ts(cols[:, 0:4], cols[:, 0:4], -1.0, ALU.mult)
        f = [S.sbuf("gf%d" % i, [128, 512], F32) for i in range(12)]
        glT = S.sbuf("glT", [16, 512], F32)
        QH = S.sbuf("gQH", [128, 512], BF16)
        KH = S.sbuf("gKH", [128, 512], BF16)
        KEf = S.sbuf("gKEf", [128, 512], BF16)
        VBf = [S.sbuf("gVBf%d" % i, [128, 512], BF16) for i in range(2)]
        KEk = S.sbuf("gKEk", [128, 512], BF16)
        VTk = S.sbuf("gVTk", [128, 4 * 256], BF16)
        AT = [S.sbuf("gAT%d" % i, [128, 128], BF16) for i in range(2)]
        St = S.sbuf("gSt", [128, 256], F32)
        Stb = S.sbuf("gStb", [128, 256], BF16)
        pcb = S.sbuf("gpcb", [128, 4], F32)
        sm = [S.sbuf("gsm%d" % i, [128, 4], F32) for i in range(2)]
        junk = S.sbuf("gjunk", [128, 256], F32)
        yn = [S.sbuf("gyn%d" % i, [128, 256], BF16) for i in range(2)]
        ost = [S.sbuf("gost%d" % i, [128, 2 * 512], BF16) for i in range(2)]
        pi = 0

        def PS():
            nonlocal pi
            pi += 1
            return k.ps[pi % 6]

        for h in range(4):
            S.memset(St[:, :], 0.0)
            S.memset(Stb[:, :], 0.0)
            for tc in range(nt):
                sl = slice(tc * 512, (tc + 1) * 512)
                q_, k_, la, cum, e1, tmp, gz0, gz1, v0, v1 = f[:10]
                S.dma(q_[:, :], k.PT[SEG_ROW["gq"] + h * 128:SEG_ROW["gq"] + (h + 1) * 128, sl])
                S.dma(k_[:, :], k.PT[SEG_ROW["gk"] + h * 128:SEG_ROW["gk"] + (h + 1) * 128, sl])
                S.dma(glT[:, :], k.PT[SEG_ROW["gl"]:SEG_ROW["gl"] + 16, sl])
                for j, (vv, gg) in enumerate(((v0, gz0), (v1, gz1))):
                    r0 = SEG_ROW["gv"] + h * 256 + j * 128
                    S.dma(vv[:, :], k.PT[r0:r0 + 128, sl])
                    r0 = SEG_ROW["gz"] + h * 256 + j * 128
                    S.dma(gg[:, :], k.PT[r0:r0 + 128, sl])
                p = PS()
                S.mm(p[:, :], w2f[:, h * 128:(h + 1) * 128], glT[:, :])
                S.act(la[:, :], p[:, :], AF.Exp, scale=-1.0, bias=cols[:, h:h + 1])
                S.act(la[:, :], la[:, :], AF.Ln, bias=k.epsc[:, 2:3])
                S.ts(la[:, :], la[:, :], -1.0 / 16.0, ALU.mult)
                scan_cum(S, cum[:, :], la[:, :], k.rmask[:, :])
                cend = c3(cum[:, :]).ix(slice(None), slice(None), slice(127, 128))
                S.op("act", lambda e, cend=cend: e.activation(out=pcb[:, :].ap.rearrange("p (n o) -> p n o", o=1), in_=cend.ap, func=AF.Exp),
                     [cum[:, :]], [pcb[:, :]])
                S.act(e1[:, :], cum[:, :], AF.Exp)
                S.stt(QH[:, :], q_[:, :], DKS, e1[:, :], ALU.mult, ALU.mult)
                S.act(e1[:, :], cum[:, :], AF.Exp, scale=-1.0)
                S.tt(KH[:, :], k_[:, :], e1[:, :], ALU.mult)
                S.op("dve", lambda e, tmp=tmp, cum=cum, cend=cend: e.tensor_tensor(
                    out=c3(tmp[:, :]).ap, in0=cend.ap.broadcast_to([128, 4, 128]), in1=c3(cum[:, :]).ap, op=ALU.subtract),
                    [cum[:, :]], [tmp[:, :]])
                S.act(e1[:, :], tmp[:, :], AF.Exp)
                S.tt(KEf[:, :], k_[:, :], e1[:, :], ALU.mult)
                S.copy(VBf[0][:, :], v0[:, :], eng="pool")
                S.copy(VBf[1][:, :], v1[:, :], eng="pool")
                pb = k.pb[0]
                for c in range(4):
                    S.tr(pb[:, c * 128:(c + 1) * 128], KEf[:, c * 128:(c + 1) * 128], k.ident[:, :])
                S.copy(KEk[:, :], pb[:, 0:512], eng="act")
                pb = k.pb[1]
                for c in range(4):
                    for j in range(2):
                        S.tr(pb[:, c * 256 + j * 128:c * 256 + (j + 1) * 128], VBf[j][:, c * 128:(c + 1) * 128], k.ident[:, :])
                S.copy(VTk[:, :], pb[:, 0:1024])
                for c in range(4):
                    cs_ = slice(c * 128, (c + 1) * 128)
                    vt = VTk[:, c * 256:(c + 1) * 256]
                    p = PS()
                    S.mm(p[:, 0:128], KH[:, cs_], QH[:, cs_])
                    at = AT[c % 2]
                    S.tt(at[:, :], p[:, 0:128], k.mIU[:, :], ALU.mult)
                    pY = PS()
                    S.mm(pY[:, 0:256], QH[:, cs_], Stb[:, :], start=True, stop=False)
                    S.mm(pY[:, 0:256], at[:, :], vt, start=False, stop=True)
                    pS_ = PS()
                    S.mm(pS_[:, 0:256], KEk[:, cs_], vt)
                    S.stt(St[:, :], St[:, :], pcb[:, c:c + 1], pS_[:, 0:256], ALU.mult, ALU.add)
                    S.copy(Stb[:, :], St[:, :], eng="act")
                    s_ = sm[c % 2]
                    S.act(junk[:, :], pY[:, 0:256], AF.Square, accum=s_[:, 0:1])
                    S.act(s_[:, 1:2], s_[:, 0:1], AF.Sqrt, scale=1.0 / 256, bias=k.epsc[:, 0:1])
                    S.recip(s_[:, 2:3], s_[:, 1:2])
                    y_ = yn[c % 2]
                    S.ts(y_[:, :], pY[:, 0:256], s_[:, 2:3], ALU.mult)
                    pb2 = k.pb[c % 2]
                    for j, gg in enumerate((gz0, gz1)):
                        S.tr(pb2[:, j * 128:(j + 1) * 128], y_[:, j * 128:(j + 1) * 128], k.ident[:, :])
                        S.stt(ost[tc % 2][:, j * 512 + c * 128:j * 512 + (c + 1) * 128], pb2[:, j * 128:(j + 1) * 128],
                              cols[:, 4 + j:5 + j], gg[:, cs_], ALU.mult, ALU.mult)
                for j in range(2):
                    r0 = h * 256 + j * 128
                    S.dma(k.osz[2][r0:r0 + 128, sl], ost[tc % 2][:, j * 512:(j + 1) * 512], q="pool")
```

```python
import contextlib
import numpy as np
import concourse.bass as bass
import concourse.mybir as mybir
from concourse.bass_utils import run_bass_kernel_spmd

F32 = mybir.dt.float32
BF16 = mybir.dt.bfloat16
I32 = mybir.dt.int32
AF = mybir.ActivationFunctionType
ALU = mybir.AluOpType
AX = mybir.AxisListType

D = 1024
NB = 2
L = 2
N_IN = 12240
EPS = 1e-6


class V:
    __slots__ = ("ap", "key", "box")

    def __init__(self, ap, key, box):
        self.ap, self.key, self.box = ap, key, box

    def r(self, pat, **kw):
        return V(self.ap.rearrange(pat, **kw), self.key, self.box)

    def ix(self, *idx):
        return V(self.ap[idx], self.key, self.box)


class Buf:
    def __init__(self, name, t, shape):
        self.name, self.t, self.shape = name, t, shape

    def __getitem__(self, idx):
        if not isinstance(idx, tuple):
            idx = (idx, slice(None))
        p, f = idx
        p0, p1, _ = p.indices(self.shape[0])
        f0, f1, _ = f.indices(self.shape[1])
        return V(self.t[p0:p1, f0:f1], self.name, (p0, p1, f0, f1))


def _ovl(a, b):
    return a[0] < b[1] and b[0] < a[1] and a[2] < b[3] and b[2] < a[3]


def _contains(a, b):
    return a[0] <= b[0] and a[1] >= b[1] and a[2] <= b[2] and a[3] >= b[3]


ENGS = ("pe", "act", "dve", "pool", "sp")


class Sched:
    NDMA = 12

    def __init__(self, nc, es):
        self.nc = nc
        self.es = es
        self.ops = []
        self.recs = {}
        self.nops = 0
        self.sem = {e: es.enter_context(nc.semaphore("s_" + e)) for e in ENGS}
        self.cnt = {e: 0 for e in ENGS}
        self.dsem = {q: [es.enter_context(nc.semaphore("d_%s%d" % (q, i))) for i in range(self.NDMA)]
                     for q in ("sp", "pool")}
        self.dn = {"sp": 0, "pool": 0}
        self.done = {}
        self.waited = {e: {} for e in ENGS}
        self.n_inst = 0
        self.eidx = {}
        self.eidx_n = {}

    def _simulate(self, trace):
        sv = self.simv = getattr(self, "simv", {})
        pos = {e_: 0 for e_ in ENGS}
        prog = True
        while prog:
            prog = False
            for e_ in ENGS:
                while pos[e_] < len(trace[e_]):
                    oid, waits, inc = trace[e_][pos[e_]]
                    if all(sv.get(k_, 0) >= v_ for k_, v_ in waits):
                        if inc is not None:
                            sv[inc[0]] = sv.get(inc[0], 0) + inc[1]
                        pos[e_] += 1
                        prog = True
                    else:
                        break
        stuck = {e_: trace[e_][pos[e_]] for e_ in ENGS if pos[e_] < len(trace[e_])}
        if stuck:
            print("DEADLOCK", stuck, {k_: v_ for k_, v_ in sv.items()})
            raise RuntimeError("deadlock in sync plan")

    @contextlib.contextmanager
    def scope(self):
        old = self.es
        with contextlib.ExitStack() as es:
            self.es = es
            try:
                yield
                self.flush()
            finally:
                self.es = old

    def sbuf(self, name, shape, dt):
        self.uid = getattr(self, "uid", 0) + 1
        name = "%s_u%d" % (name, self.uid)
        t = self.es.enter_context(self.nc.sbuf_tensor(name, list(shape), dt))
        return Buf(name, t, shape)

    def psum(self, name, shape, dt):
        t = self.es.enter_context(self.nc.psum_tensor(name, list(shape), dt))
        return Buf(name, t, shape)

    def dram(self, name, shape, dt, kind="Internal"):
        t = self.nc.dram_tensor(name, list(shape), dt, kind=kind)
        return Buf(name, t, shape)

    def _deps(self, reads, writes):
        deps = set()
        for v in reads:
            for rec in self.recs.get(v.key, ()):
                if rec[1] is not None and _ovl(rec[0], v.box):
                    deps.add(rec[1])
        for v in writes:
            for rec in self.recs.get(v.key, ()):
                if _ovl(rec[0], v.box):
                    if rec[1] is not None:
                        deps.add(rec[1])
                    deps.update(rec[2])
        return deps

    def _update(self, oid, reads, writes):
        for v in reads:
            lst = self.recs.setdefault(v.key, [])
            for rec in lst:
                if rec[0] == v.box:
                    rec[2].append(oid)
                    break
            else:
                lst.append([v.box, None, [oid]])
        for v in writes:
            lst = self.recs.setdefault(v.key, [])
            keep = [rec for rec in lst if not _contains(v.box, rec[0])]
            keep.append([v.box, oid, []])
            self.recs[v.key] = keep

    def op(self, eng, fn, reads=(), writes=(), dma=False, ptr=()):
        oid = self.nops
        self.nops += 1
        deps = self._deps(list(reads) + list(ptr), writes)
        hdeps = self._deps(ptr, ()) if ptr else set()
        if eng != "pe" and not dma:
            ei = self.eidx_n.get(eng, 0)
            for d in self._deps(list(reads), ()):
                pe_ = self.eidx.get(d)
                if pe_ is not None and pe_[0] == eng and ei - pe_[1] <= 3:
                    hdeps.add(d)
        self.eidx_n[eng] = self.eidx_n.get(eng, 0) + 1
        self.eidx[oid] = (eng, self.eidx_n[eng] - 1)
        self._update(oid, list(reads) + list(ptr), writes)
        self.ops.append((oid, eng, fn, deps, dma, hdeps))
        return oid

    def dma(self, out, in_, q="sp", **kw):
        return self.op(q, lambda e: e.dma_start(out=out.ap, in_=in_.ap, **kw), [in_], [out], dma=True)

    def mm(self, out, lhsT, rhs, start=True, stop=True):
        return self.op("pe", lambda e: e.matmul(out.ap, lhsT.ap, rhs.ap, start=start, stop=stop),
                       [lhsT, rhs], [out])

    def tr(self, out, in_, ident):
        return self.op("pe", lambda e: e.transpose(out.ap, in_.ap, ident.ap), [in_, ident], [out])

    def act(self, out, in_, func, bias=None, scale=None, accum=None, eng="act"):
        rd = [in_]
        pt = []
        kw = {}
        if bias is not None:
            if isinstance(bias, V):
                pt.append(bias); kw["bias"] = bias.ap
            else:
                kw["bias"] = float(bias)
        if scale is not None:
            if isinstance(scale, V):
                pt.append(scale); kw["scale"] = scale.ap
            else:
                kw["scale"] = float(scale)
        wr = [out]
        if accum is not None:
            wr.append(accum); kw["accum_out"] = accum.ap
        return self.op("act", lambda e: e.activation(out=out.ap, in_=in_.ap, func=func, **kw), rd, wr, ptr=pt)

    def tt(self, out, a, b, op, eng="dve"):
        return self.op(eng, lambda e: e.tensor_tensor(out=out.ap, in0=a.ap, in1=b.ap, op=op), [a, b], [out])

    def ts(self, out, a, s1, op0, s2=None, op1=None, eng="dve", accum=None):
        rd = [a]
        s1a = s1.ap if isinstance(s1, V) else float(s1)
        s2a = None if s2 is None else (s2.ap if isinstance(s2, V) else float(s2))
        pt = []
        if isinstance(s1, V): pt.append(s1)
        if isinstance(s2, V): pt.append(s2)
        kw = {}
        if op1 is not None: kw["op1"] = op1
        wr = [out]
        if accum is not None:
            kw["accum_out"] = accum.ap; wr.append(accum)
        return self.op(eng, lambda e: e.tensor_scalar(out=out.ap, in0=a.ap, scalar1=s1a, scalar2=s2a, op0=op0, **kw),
                       rd, wr, ptr=pt)

    def stt(self, out, a, s, b, op0, op1, eng="dve"):
        rd = [a, b]
        sa = s.ap if isinstance(s, V) else float(s)
        pt = [s] if isinstance(s, V) else []
        return self.op(eng, lambda e: e.scalar_tensor_tensor(out=out.ap, in0=a.ap, scalar=sa, in1=b.ap, op0=op0, op1=op1),
                       rd, [out], ptr=pt)

    def copy(self, out, in_, eng="dve"):
        if eng == "act":
            return self.op("act", lambda e: e.activation(out=out.ap, in_=in_.ap, func=AF.Copy), [in_], [out])
        return self.op(eng, lambda e: e.tensor_copy(out=out.ap, in_=in_.ap), [in_], [out])

    def memset(self, out, val, eng="pool"):
        return self.op(eng, lambda e: e.memset(out.ap, val), [], [out])

    def recip(self, out, in_):
        return self.op("dve", lambda e: e.reciprocal(out=out.ap, in_=in_.ap), [in_], [out])

    def flush(self):
        ops = self.ops
        self.ops = []
        if not ops:
            return
        eng_of = {o[0]: o[1] for o in ops}
        need = set()
        for oid, eng, fn, deps, dma, hdeps in ops:
            for d in deps:
                if d in self.done:
                    continue
                if d in eng_of and (eng_of[d] != eng or dma or d in hdeps):
                    need.add(d)
        last = {}
        for oid, eng, fn, deps, dma, hdeps in ops:
            last[eng] = oid
        need.update(last.values())
        plan = {e: [] for e in ENGS}
        for oid, eng, fn, deps, dma, hdeps in ops:
            if dma:
                i = self.dn[eng]
                self.dn[eng] += 1
                tok = ("dma", eng, i % self.NDMA, 16 * (i // self.NDMA + 1), i)
                self.done[oid] = tok
            elif oid in need:
                self.cnt[eng] += 1
                tok = ("eng", eng, self.cnt[eng])
                self.done[oid] = tok
            else:
                tok = None
                self.done[oid] = ("impl", eng)
            plan[eng].append((oid, fn, deps, dma, tok, hdeps))
        nc = self.nc
        trace = {e_: [] for e_ in ENGS}
        self._trace = trace
        with nc.Block() as block:
            def mk(eng):
                def body(e):
                    wd = self.waited[eng]
                    for oid, fn, deps, dma, tok, hdeps in plan[eng]:
                        waits = {}
                        for d in deps:
                            dt = self.done.get(d)
                            if dt is None or dt[0] == "impl":
                                continue
                            if dt[0] == "eng":
                                if dt[1] == eng and not dma and d not in hdeps:
                                    continue
                                s = self.sem[dt[1]]
                                val = dt[2]
                                k = ("e", dt[1])
                            else:
                                s = self.dsem[dt[1]][dt[2]]
                                val = dt[3]
                                k = ("d", dt[1], dt[2])
                            if wd.get(k, 0) >= val:
                                continue
                            if waits.get(k, (None, 0))[1] < val:
                                waits[k] = (s, val)
                        if dma:
                            i = tok[4]
                            if i >= self.NDMA:
                                k = ("d", eng, tok[2])
                                val = tok[3] - 16
                                if wd.get(k, 0) < val and waits.get(k, (None, 0))[1] < val:
                                    waits[k] = (self.dsem[eng][tok[2]], val)
                        for k, (s, val) in waits.items():
                            e.wait_ge(s, val)
                            wd[k] = val
                            self.n_inst += 1
                        ins = fn(e)
                        self.n_inst += 1
                        inc = None
                        if tok is not None:
                            if tok[0] == "dma":
                                ins.then_inc(self.dsem[eng][tok[2]], 16)
                                inc = (("d", eng, tok[2]), 16)
                            else:
                                ins.then_inc(self.sem[eng], 1)
                                inc = (("e", eng), 1)
                        trace[eng].append((oid, [(k_, v_[1]) for k_, v_ in waits.items()], inc))
                    if eng in ("sp", "pool"):
                        n = self.dn[eng]
                        for j in range(self.NDMA):
                            if n > j:
                                val = 16 * ((n - 1 - j) // self.NDMA + 1)
                                k = ("d", eng, j)
                                if wd.get(k, 0) < val:
                                    e.wait_ge(self.dsem[eng][j], val)
                                    wd[k] = val
                return body
            block.tensor(mk("pe"))
            block.scalar(mk("act"))
            block.vector(mk("dve"))
            block.gpsimd(mk("pool"))
            block.sync(mk("sp"))
        if getattr(self, "check", False):
            self._simulate(trace)
        self.recs = {}
        self.done = {}
        self.eidx = {}


SEGS = [
    ("cq", 0, 512, "copy"), ("ckv", 512, 256, "copy"), ("kr", 768, 64, "copy"), ("mz", 832, 1024, "silu"),
    ("rr", 1856, 1024, "shift"), ("rk", 2880, 1024, "shift"), ("rv", 3904, 1024, "shift"),
    ("rwa", 4928, 128, "shift"), ("rz", 5056, 1024, "silu"),
    ("gq", 6080, 512, "copy"), ("gk", 6592, 512, "copy"), ("gv", 7104, 1024, "copy"), ("gl", 8128, 16, "copy"),
    ("gz", 8144, 1024, "silu"), ("ga", 9168, 1024, "sigmoid"), ("gb", 10192, 1024, "sigmoid"),
    ("gc", 11216, 1024, "sigmoid"),
]
SEG_ROW = {}
_r = 0
for _n, _c, _w, _e in SEGS:
    SEG_ROW[_n] = _r
    _r += _w
PT_ROWS = _r


class K:
    pass


def v3(v, pat, **kw):
    return v.r(pat, **kw)


def build(T=4096, dbg=False, phases=("mla", "rwkv", "gla"), nl=L):
    nc = bass.Bass("TRN2", target_bir_lowering=False)
    k = K()
    k.phases = phases
    import os
    k.rstage = float(os.environ.get('K_RSTAGE', '3'))
    k.ne = int(os.environ.get('K_NE', '2'))
    k.lvx = int(os.environ.get('K_LVX', '3'))
    k.dhb = int(os.environ.get('K_DHB', '7'))
    k.nl = nl
    k.dbg = dbg
    k.dumped = set()
    k.T = T
    es = contextlib.ExitStack()
    S = Sched(nc, es)
    k.S = S
    ext_in = lambda name, shape, dt=F32: S.dram(name, shape, dt, kind="ExternalInput")
    k.x_in = ext_in("x", [NB * T, D])
    k.c_in = ext_in("c", [NB, D])
    k.pos_in = ext_in("positions", [NB, T], I32)
    k.ada_w = ext_in("ada_w", [L * D, 3 * D])
    k.ada_b = ext_in("ada_b", [L, 3 * D])
    k.norm_pre = ext_in("norm_pre", [L, D])
    k.norm_post = ext_in("norm_post", [L, D])
    k.w_in = ext_in("w_in", [L * D, N_IN])
    k.rwkv_mu = ext_in("rwkv_mu", [L * 3200, 1])
    k.mla_q_norm = ext_in("mla_q_norm", [L * 512, 1])
    k.mla_kv_norm = ext_in("mla_kv_norm", [L * 256, 1])
    k.mla_w_uq = ext_in("mla_w_uq", [L * 512, 1536])
    k.mla_w_ukv = ext_in("mla_w_ukv", [L * 256, 2048])
    k.mla_w_o = ext_in("mla_w_o", [L * 1024, 1024])
    k.rwkv_w0 = ext_in("rwkv_w0", [L * 1024, 1])
    k.rwkv_w2 = ext_in("rwkv_w2", [L * 64, 1024])
    k.rwkv_a0 = ext_in("rwkv_a0", [L * 1024, 1])
    k.rwkv_a2 = ext_in("rwkv_a2", [L * 64, 1024])
    k.rwkv_k_k = ext_in("rwkv_k_k", [L * 1024, 1])
    k.rwkv_k_a = ext_in("rwkv_k_a", [L * 1024, 1])
    k.rwkv_r_k = ext_in("rwkv_r_k", [L * 1024, 1])
    k.rwkv_ln_w = ext_in("rwkv_ln_w", [L * 1024, 1])
    k.rwkv_ln_b = ext_in("rwkv_ln_b", [L * 1024, 1])
    k.rwkv_w_o = ext_in("rwkv_w_o", [L * 1024, 1024])
    k.gla_w2 = ext_in("gla_w2", [L * 16, 512])
    k.gla_b = ext_in("gla_b", [L * 512, 1])
    k.gla_norm = ext_in("gla_norm", [L * 256, 1])
    k.gla_w_o = ext_in("gla_w_o", [L * 1024, 1024])
    k.w_out = ext_in("w_out", [L * 1024, 1024])
    k.rope_freq = ext_in("rope_freq", [64, 1])
    k.out = S.dram("out", [NB * T, D], F32, kind="ExternalOutput")
    okind = "ExternalOutput" if dbg else "Internal"
    k.xs = S.dram("xs", [NB * T, D], F32)
    k.mod = S.dram("modd", [L * NB, 3 * D], F32, kind=okind)
    k.PT = S.dram("PT", [PT_ROWS, T], F32, kind=okind)
    k.osz = [S.dram("osz%d" % i, [D, T], BF16, kind=okind) for i in range(3)]

    k.ident = S.sbuf("ident", [128, 128], BF16)
    k.ones = S.sbuf("onesb", [128, 128], BF16)
    k.identf = S.sbuf("identf", [128, 128], F32)
    S.memset(k.ones[:, :], 1.0)
    S.memset(k.identf[:, :], 1.0)
    S.op("pool", lambda e: e.affine_select(out=k.identf[:, :].ap, in_=k.identf[:, :].ap, pattern=[[1, 128]],
                                           compare_op=ALU.is_equal, fill=0.0, base=0, channel_multiplier=-1),
         [k.identf[:, :]], [k.identf[:, :]])
    S.copy(k.ident[:, :], k.identf[:, :], eng="pool")
    k.ps = [S.psum("ps%d" % i, [128, 512], F32) for i in range(6)]
    k.pb = [S.psum("pb%d" % i, [128, 1024], BF16) for i in range(2)]
    k_eps(S, k)
    lin_consts(S, k)
    S.flush()

    phase_mod(S, k)
    for l in range(nl):
        for b in range(NB):
            phase_norm(S, k, l, b)
            phase_inproj(S, k, l, b)
            if "mla" in k.phases:
                phase_mla(S, k, l, b)
            if "rwkv" in k.phases:
                phase_rwkv(S, k, l, b)
            if "gla" in k.phases:
                phase_gla(S, k, l, b)
            phase_final(S, k, l, b)
    S.flush()
    es.close()
    k.n_inst = S.n_inst
    return nc, k


def phase_mod(S, k):
    with S.scope():
        cT = S.sbuf("cT", [128, 8 * NB], F32)
        sT = S.sbuf("sT", [128, 8 * NB], F32)
        for b in range(NB):
            S.dma(cT[:, :].r("p (c b) -> p c b", b=NB).ix(slice(None), slice(None), slice(b, b + 1)),
                  V(k.c_in.t[b:b + 1, :].rearrange("o (c k) -> k c o", k=128), "c", (b, b + 1, 0, D)),
                  allow_slow_non_contiguous=True)
        S.act(sT[:, :], cT[:, :], AF.Silu)
        wa = [S.sbuf("wa%d" % i, [128, 8 * 512], F32) for i in range(2)]
        bb = S.sbuf("adab", [NB, 3 * D], F32)
        msb = S.sbuf("msb", [NB, 3 * D], F32)
        i = 0
        for l in range(L):
            S.dma(bb[:, :], V(k.ada_b.t[l:l + 1, :].partition_broadcast(NB), "ada_b", (l, l + 1, 0, 3 * D)))
            for cc in range(6):
                w = wa[i % 2]
                S.dma(w[:, :].r("p (c n) -> p c n", c=8),
                      V(k.ada_w.t[l * D:(l + 1) * D, cc * 512:(cc + 1) * 512].rearrange("(c k) n -> k c n", k=128),
                        "ada_w", (l * D, (l + 1) * D, cc * 512, (cc + 1) * 512)))
                ps = k.ps[i % 2]
                for kc in range(8):
                    S.mm(ps[0:NB, :], sT[:, kc * NB:(kc + 1) * NB], w[:, kc * 512:(kc + 1) * 512],
                         start=(kc == 0), stop=(kc == 7))
                S.tt(msb[:, cc * 512:(cc + 1) * 512], ps[0:NB, :], bb[:, cc * 512:(cc + 1) * 512], ALU.add)
                i += 1
            S.dma(k.mod[l * NB:(l + 1) * NB, :], msb[:, :], q="pool")


def bcast_row(buf, r, c0, c1, npart=128):
    return V(buf.t[r:r + 1, c0:c1].partition_broadcast(npart), buf.name, (r, r + 1, c0, c1))


def phase_norm(S, k, l, b):
    T = k.T
    k.es_hT = contextlib.ExitStack()
    old = S.es
    S.es = k.es_hT
    k.hT = S.sbuf("hT", [128, 8 * T], BF16)
    S.es = old
    with S.scope():
        Gb = S.sbuf("Gb", [128, D], F32)
        npb = S.sbuf("npb", [128, D], F32)
        shb = S.sbuf("shb", [128, D], F32)
        S.dma(Gb[:, :], bcast_row(k.mod, l * NB + b, D, 2 * D))
        S.dma(npb[:, :], bcast_row(k.norm_pre, l, 0, D))
        S.dma(shb[:, :], bcast_row(k.mod, l * NB + b, 0, D))
        S.stt(Gb[:, :], Gb[:, :], 1.0, npb[:, :], ALU.add, ALU.mult)
        xin = k.x_in if l == 0 else k.xs
        xt = [S.sbuf("xt%d" % i, [128, D], F32) for i in range(3)]
        junk = S.sbuf("junk", [128, D], F32)
        tmp = [S.sbuf("tmp%d" % i, [128, D], F32) for i in range(2)]
        hb = [S.sbuf("hb%d" % i, [128, D], BF16) for i in range(2)]
        st = [S.sbuf("st%d" % i, [128, 4], F32) for i in range(2)]
        for tt in range(T // 128):
            x = xt[tt % 3]
            s = st[tt % 2]
            S.dma(x[:, :], xin[b * T + tt * 128: b * T + (tt + 1) * 128, :])
            S.act(junk[:, :], x[:, :], AF.Square, accum=s[:, 0:1])
            S.act(s[:, 1:2], s[:, 0:1], AF.Sqrt, scale=1.0 / D, bias=k_eps(S, k))
            S.recip(s[:, 2:3], s[:, 1:2])
            S.stt(tmp[tt % 2][:, :], x[:, :], s[:, 2:3], Gb[:, :], ALU.mult, ALU.mult)
            S.tt(hb[tt % 2][:, :], tmp[tt % 2][:, :], shb[:, :], ALU.add, eng="pool")
            pb = k.pb[tt % 2]
            for c in range(8):
                S.tr(pb[:, c * 128:(c + 1) * 128], hb[tt % 2][:, c * 128:(c + 1) * 128], k.ident[:, :])
            dst = V(k.hT.t[:, :].rearrange("p (c t) -> p c t", c=8)[:, :, tt * 128:(tt + 1) * 128], "hT",
                    (0, 128, tt * 128, (tt + 1) * 128))
            if tt % 2 == 0:
                S.op("act", lambda e, dst=dst, pb=pb: e.activation(out=dst.ap, in_=pb[:, :].ap.rearrange("p (c t) -> p c t", c=8), func=AF.Copy),
                     [pb[:, :]], [dst])
            else:
                S.op("dve", lambda e, dst=dst, pb=pb: e.tensor_copy(out=dst.ap, in_=pb[:, :].ap.rearrange("p (c t) -> p c t", c=8)),
                     [pb[:, :]], [dst])


def k_eps(S, k):
    if not hasattr(k, "epsc"):
        k.epsc = S.sbuf("epsc", [128, 4], F32)
        S.memset(k.epsc[:, 0:1], EPS)
        S.memset(k.epsc[:, 1:2], 64e-5)
        S.memset(k.epsc[:, 2:3], 1.0)
        S.memset(k.epsc[:, 3:4], 0.0)
    return k.epsc[:, 0:1]


def phase_inproj(S, k, l, b):
    T = k.T
    blocks = []
    for name, c0, w, epi in SEGS:
        for j in range(0, w, 128):
            bw = min(128, w - j)
            blocks.append((name, SEG_ROW[name] + j, c0 + j, bw, epi))
    with S.scope():
        wf = [S.sbuf("wf%d" % i, [128, 8 * 128], F32) for i in range(2)]
        wb = [S.sbuf("wb%d" % i, [128, 8 * 128], BF16) for i in range(2)]
        raw = [S.sbuf("raw%d" % i, [128, T + 1], F32) for i in range(2)]
        stg = [S.sbuf("stg%d" % i, [128, 512], F32) for i in range(4)]
        tmp = [S.sbuf("ptmp%d" % i, [128, 512], F32) for i in range(2)]
        mu = [S.sbuf("mu%d" % i, [128, 2], F32) for i in range(2)]
        for r in raw:
            S.memset(r[:, 0:1], 0.0)
        ri = 0
        si = 0
        pi = 0
        ei = 0
        for bi, (name, row, col, bw, epi) in enumerate(blocks):
            f, w = wf[bi % 2], wb[bi % 2]
            S.dma(f[:, 0:8 * bw].r("p (c n) -> p c n", c=8),
                  V(k.w_in.t[l * D:(l + 1) * D, col:col + bw].rearrange("(c k) n -> k c n", k=128), "w_in",
                    (l * D, (l + 1) * D, col, col + bw)))
            S.copy(w[:, 0:8 * bw], f[:, 0:8 * bw], eng="pool")
            if epi == "shift":
                m = mu[ri % 2]
                rw = raw[ri % 2]
                ri += 1
                mo = col - 1856
                S.dma(m[0:bw, 0:1], k.rwkv_mu[l * 3200 + mo: l * 3200 + mo + bw, 0:1])
                S.ts(m[0:bw, 1:2], m[0:bw, 0:1], -1.0, ALU.mult, 1.0, ALU.add, eng="pool")
            for tc in range(T // 512):
                ps = k.ps[pi % 6]
                pi += 1
                for kc in range(8):
                    S.mm(ps[0:bw, :], w[:, kc * bw:(kc + 1) * bw], k.hT[:, kc * T + tc * 512: kc * T + (tc + 1) * 512],
                         start=(kc == 0), stop=(kc == 7))
                st = stg[si % 4]
                si += 1
                if epi == "shift":
                    S.copy(rw[0:bw, 1 + tc * 512: 1 + (tc + 1) * 512], ps[0:bw, :], eng="act")
                    tp = tmp[tc % 2]
                    S.ts(tp[0:bw, :], rw[0:bw, 1 + tc * 512: 1 + (tc + 1) * 512], m[0:bw, 1:2], ALU.mult)
                    S.stt(st[0:bw, :], rw[0:bw, tc * 512: (tc + 1) * 512], m[0:bw, 0:1], tp[0:bw, :], ALU.mult, ALU.add)
                elif epi == "copy":
                    S.copy(st[0:bw, :], ps[0:bw, :], eng=("act" if ei % 2 else "dve"))
                    ei += 1
                else:
                    fn = {"silu": AF.Silu, "sigmoid": AF.Sigmoid}[epi]
                    S.act(st[0:bw, :], ps[0:bw, :], fn)
                S.dma(k.PT[row:row + bw, tc * 512:(tc + 1) * 512], st[0:bw, :], q="pool")
    k.es_hT.close()


def phase_final(S, k, l, b):
    T = k.T
    with S.scope():
        wts = []
        wf = [S.sbuf("fwf%d" % i, [128, 8 * 128], F32) for i in range(2)]
        srcs = [k.mla_w_o, k.rwkv_w_o, k.gla_w_o, k.w_out]
        i = 0
        for wi, src in enumerate(srcs):
            wbuf = S.sbuf("fw%d" % wi, [128, 8 * 1024], BF16)
            wts.append(wbuf)
            for q8 in range(8):
                f = wf[i % 2]
                i += 1
                S.dma(f[:, :].r("p (c n) -> p c n", c=8),
                      V(src.t[l * D:(l + 1) * D, q8 * 128:(q8 + 1) * 128].rearrange("(c k) n -> k c n", k=128),
                        src.name, (l * D, (l + 1) * D, q8 * 128, (q8 + 1) * 128)))
                dst = V(wbuf.t[:, :].rearrange("p (c n) -> p c n", c=8)[:, :, q8 * 128:(q8 + 1) * 128], wbuf.name,
                        (0, 128, 0, 8 * 1024))
                S.op("pool", lambda e, dst=dst, f=f: e.tensor_copy(out=dst.ap, in_=f[:, :].ap.rearrange("p (c n) -> p c n", c=8)),
                     [f[:, :]], [dst])
        GN = S.sbuf("GN", [128, D], F32)
        npb = S.sbuf("fnpb", [128, D], F32)
        S.dma(GN[:, :], bcast_row(k.mod, l * NB + b, 2 * D, 3 * D))
        S.dma(npb[:, :], bcast_row(k.norm_post, l, 0, D))
        S.tt(GN[:, :], GN[:, :], npb[:, :], ALU.mult)
        osz = [[S.sbuf("fo%d_%d" % (x, i), [128, 8 * 512], BF16) for i in range(1)] for x in range(3)]
        gt = [[S.sbuf("fg%d_%d" % (x, i), [128, 512], F32) for i in range(2)] for x in range(3)]
        mg = [S.sbuf("fmg%d" % i, [128, 8 * 512], BF16) for i in range(2)]
        ma = [S.sbuf("fma%d" % i, [128, 512], F32) for i in range(2)]
        mb = [S.sbuf("fmb%d" % i, [128, 512], F32) for i in range(2)]
        xt = [S.sbuf("fx%d" % i, [128, D], F32) for i in range(2)]
        xo = [S.sbuf("fxo%d" % i, [128, D], F32) for i in range(2)]
        junk = S.sbuf("fjunk", [128, 512], F32)
        st = [S.sbuf("fst%d" % i, [128, 8], F32) for i in range(2)]
        xin = k.x_in if l == 0 else k.xs
        xout = k.xs if l < k.nl - 1 else k.out
        gi = 0
        pi = 0
        for tc in range(T // 512):
            for x in range(3):
                S.dma(osz[x][0][:, :].r("p (c t) -> p c t", c=8),
                      V(k.osz[x].t[:, tc * 512:(tc + 1) * 512].rearrange("(c k) t -> k c t", k=128), k.osz[x].name,
                        (0, D, tc * 512, (tc + 1) * 512)))
            m = mg[tc % 2]
            for ob in range(8):
                pss = []
                for x in range(3):
                    ps = k.ps[pi % 6]
                    pi += 1
                    pss.append(ps)
                    for kc in range(8):
                        S.mm(ps[:, :], wts[x][:, kc * 1024 + ob * 128: kc * 1024 + (ob + 1) * 128],
                             osz[x][0][:, kc * 512:(kc + 1) * 512], start=(kc == 0), stop=(kc == 7))
                g = [gt[x][gi % 2] for x in range(3)]
                gi += 1
                for x, nm in enumerate(("ga", "gb", "gc")):
                    r0 = SEG_ROW[nm] + ob * 128
                    S.dma(g[x][:, :], k.PT[r0:r0 + 128, tc * 512:(tc + 1) * 512])
                a_, b_ = ma[ob % 2], mb[ob % 2]
                S.tt(a_[:, :], pss[0][:, :], g[0][:, :], ALU.mult)
                S.tt(b_[:, :], pss[1][:, :], g[1][:, :], ALU.mult)
                S.tt(a_[:, :], a_[:, :], b_[:, :], ALU.add, eng="pool")
                S.tt(b_[:, :], pss[2][:, :], g[2][:, :], ALU.mult)
                S.tt(m[:, ob * 512:(ob + 1) * 512], a_[:, :], b_[:, :], ALU.add, eng="pool")
            for tb in range(4):
                tg = tc * 4 + tb
                x_ = xt[tg % 2]
                o_ = xo[tg % 2]
                s = st[tg % 2]
                S.dma(x_[:, :], xin[b * T + tg * 128: b * T + (tg + 1) * 128, :])
                pss = []
                for half in range(2):
                    ps = k.ps[pi % 6]
                    pi += 1
                    pss.append(ps)
                    for ob in range(8):
                        S.mm(ps[:, :], m[:, ob * 512 + tb * 128: ob * 512 + (tb + 1) * 128],
                             wts[3][:, ob * 1024 + half * 512: ob * 1024 + (half + 1) * 512],
                             start=(ob == 0), stop=(ob == 7))
                    S.act(junk[:, :], ps[:, :], AF.Square, accum=s[:, half:half + 1])
                S.tt(s[:, 2:3], s[:, 0:1], s[:, 1:2], ALU.add)
                S.act(s[:, 3:4], s[:, 2:3], AF.Sqrt, scale=1.0 / D, bias=k_eps(S, k))
                S.recip(s[:, 4:5], s[:, 3:4])
                for half in range(2):
                    sl = slice(half * 512, (half + 1) * 512)
                    S.stt(o_[:, sl], pss[half][:, :], s[:, 4:5], GN[:, sl], ALU.mult, ALU.mult)
                    S.tt(o_[:, sl], o_[:, sl], x_[:, sl], ALU.add, eng="pool")
                S.dma(xout[b * T + tg * 128: b * T + (tg + 1) * 128, :], o_[:, :], q="pool")


def dump(S, k, name, buf, dt):
    if not k.dbg or name in k.dumped:
        return
    k.dumped.add(name)
    d = S.dram("dbg_" + name, list(buf.shape), dt, kind="ExternalOutput")
    S.dma(d[:, :], buf[:, :], q="pool")


def dview(buf, r0, r1, c0, c1, pat=None, **kw):
    ap = buf.t[r0:r1, c0:c1]
    if pat:
        ap = ap.rearrange(pat, **kw)
    return V(ap, buf.name, (r0, r1, c0, c1))


def col_load(S, dst, src, r0, n, q="sp"):
    S.dma(dst, dview(src, r0, r0 + n * 128, 0, 1, "(c k) o -> k (c o)", k=128), q=q, allow_slow_non_contiguous=True)


def rope_tables(S, k, b, cosT, sinT):
    T = k.T
    with S.scope():
        ff = S.sbuf("ff", [64, 2], F32)
        S.dma(ff[:, 0:1], k.rope_freq[:, :])
        pi_ = S.sbuf("posi", [64, T], I32)
        ang = S.sbuf("ang", [64, T], F32)
        kf = S.sbuf("kf", [64, T], F32)
        S.dma(pi_[:, :], bcast_row(k.pos_in, b, 0, T, 64))
        S.copy(ang[:, :], pi_[:, :])
        S.ts(ang[:, :], ang[:, :], ff[:, 0:1], ALU.mult)
        TWO_PI = float(2 * np.pi)
        MAGIC = 12582912.0
        PI_LO = 3.1415925
        for which, dst in ((0, sinT), (1, cosT)):
            off = 0.0 if which == 0 else float(np.pi / 2)
            S.ts(dst[:, :], ang[:, :], off, ALU.add)
            S.ts(kf[:, :], dst[:, :], 1.0 / TWO_PI, ALU.mult)
            S.ts(kf[:, :], kf[:, :], MAGIC, ALU.add)
            S.ts(kf[:, :], kf[:, :], MAGIC, ALU.subtract)
            S.stt(dst[:, :], kf[:, :], -TWO_PI, dst[:, :], ALU.mult, ALU.add)
            S.ts(dst[:, :], dst[:, :], PI_LO, ALU.min, -PI_LO, ALU.max)
            S.act(dst[:, :], dst[:, :], AF.Sin)
        S.ts(sinT[0:32, :], sinT[0:32, :], -1.0, ALU.mult)


def phase_mla(S, k, l, b):
    T = k.T
    nt = T // 512
    nblk = T // 128
    SC = float((128 + 64) ** -0.5)
    with S.scope():
        cqn = S.sbuf("cqn", [128, 4 * T], BF16)
        ckvn = S.sbuf("ckvn", [128, 2 * T], BF16)
        krT = S.sbuf("krT", [64, T], BF16)
        cosT = S.sbuf("cosT", [64, T], F32)
        sinT = S.sbuf("sinT", [64, T], F32)
        gq = S.sbuf("gq", [128, 8], F32)
        col_load(S, gq[:, 0:4], k.mla_q_norm, l * 512, 4)
        col_load(S, gq[:, 4:6], k.mla_kv_norm, l * 256, 2)
        rope_tables(S, k, b, cosT, sinT)
        with S.scope():
            cfb = [S.sbuf("cf%d" % i, [128, 4 * 512], F32) for i in range(2)]
            sqb = [S.sbuf("sq%d" % i, [128, 4 * 512], BF16) for i in range(2)]
            sd = [S.sbuf("sd%d" % i, [128, 512], F32) for i in range(2)]
            i = 0
            for name, nb_, dst, gcol in (("cq", 4, cqn, 0), ("ckv", 2, ckvn, 4)):
                r0 = SEG_ROW[name]
                for tc in range(nt):
                    cf, sq, s_ = cfb[i % 2], sqb[i % 2], sd[i % 2]
                    ps = k.ps[i % 2]
                    i += 1
                    S.dma(cf[:, 0:nb_ * 512].r("p (c t) -> p c t", c=nb_),
                          dview(k.PT, r0, r0 + nb_ * 128, tc * 512, (tc + 1) * 512, "(c k) t -> k c t", k=128))
                    S.act(sq[:, 0:nb_ * 512], cf[:, 0:nb_ * 512], AF.Square)
                    for c in range(nb_):
                        S.mm(ps[:, :], k.ones[:, :], sq[:, c * 512:(c + 1) * 512], start=(c == 0), stop=(c == nb_ - 1))
                    S.act(s_[:, :], ps[:, :], AF.Sqrt, scale=1.0 / (nb_ * 128), bias=k.epsc[:, 0:1])
                    S.recip(s_[:, :], s_[:, :])
                    for c in range(nb_):
                        S.stt(dst[:, c * T + tc * 512: c * T + (tc + 1) * 512], cf[:, c * 512:(c + 1) * 512],
                              gq[:, gcol + c:gcol + c + 1], s_[:, :], ALU.mult, ALU.mult)
            r0 = SEG_ROW["kr"]
            for tc in range(nt):
                cf = cfb[tc % 2]
                sl = slice(tc * 512, (tc + 1) * 512)
                S.dma(cf[0:64, 0:512], k.PT[r0:r0 + 64, sl])
                S.dma(cf[0:32, 512:1024], k.PT[r0 + 32:r0 + 64, sl])
                S.dma(cf[32:64, 512:1024], k.PT[r0:r0 + 32, sl])
                S.tt(cf[0:64, 1024:1536], cf[0:64, 0:512], cosT[:, sl], ALU.mult)
                S.tt(cf[0:64, 1536:2048], cf[0:64, 512:1024], sinT[:, sl], ALU.mult, eng="pool")
                S.tt(krT[:, sl], cf[0:64, 1024:1536], cf[0:64, 1536:2048], ALU.add)
        QN = S.sbuf("QN", [128, T], BF16)
        QR = S.sbuf("QR", [64, T], BF16)
        KN = S.sbuf("KN", [128, T], BF16)
        VA = S.sbuf("VA", [128, nblk * 130], BF16)
        S.op("pool", lambda e: e.memset(VA[:, :].ap.rearrange("p (n c) -> p n c", c=130)[:, :, 128:130], 1.0), [], [VA[:, :]])
        wqf = S.sbuf("wqf", [128, 4 * 256], F32)
        wqb = S.sbuf("wqb", [128, 4 * 256], BF16)
        wkf = S.sbuf("wkf", [128, 2 * 256], F32)
        wkb = S.sbuf("wkb", [128, 2 * 256], BF16)
        rt = [S.sbuf("rt%d" % i, [64, 512], F32) for i in range(2)]
        pt = [S.sbuf("ptt%d" % i, [128, 512], BF16) for i in range(3)]
        mz = [S.sbuf("mzz%d" % i, [128, 512], F32) for i in range(2)]
        ost = [S.sbuf("ost%d" % i, [128, 512], BF16) for i in range(2)]
        on = [S.sbuf("on%d" % i, [128, 128], BF16) for i in range(2)]
        rl = [S.sbuf("rl%d" % i, [128, 1], F32) for i in range(2)]
        ui = 0
        oi = 0
        for h in range(8):
            wq3 = wqf[:, :].r("p (c n) -> p c n", c=4)
            c0 = h * 192
            S.dma(wq3.ix(slice(None), slice(None), slice(0, 192)),
                  dview(k.mla_w_uq, l * 512, (l + 1) * 512, c0, c0 + 192, "(c k) n -> k c n", k=128))
            S.dma(wq3.ix(slice(None), slice(None), slice(192, 224)),
                  dview(k.mla_w_uq, l * 512, (l + 1) * 512, c0 + 160, c0 + 192, "(c k) n -> k c n", k=128))
            S.dma(wq3.ix(slice(None), slice(None), slice(224, 256)),
                  dview(k.mla_w_uq, l * 512, (l + 1) * 512, c0 + 128, c0 + 160, "(c k) n -> k c n", k=128))
            S.copy(wqb[:, :], wqf[:, :], eng="pool")
            S.dma(wkf[:, :].r("p (c n) -> p c n", c=2),
                  dview(k.mla_w_ukv, l * 256, (l + 1) * 256, h * 256, (h + 1) * 256, "(c k) n -> k c n", k=128))
            S.copy(wkb[:, :], wkf[:, :], eng="pool")
            for tc in range(nt):
                sl = slice(tc * 512, (tc + 1) * 512)
                p0, p1, p2, p3, p4 = k.ps[0], k.ps[1], k.ps[2], k.ps[3], k.ps[4]
                for kc in range(4):
                    S.mm(p0[:, :], wqb[:, kc * 256:kc * 256 + 128], cqn[:, kc * T + tc * 512:kc * T + (tc + 1) * 512],
                         start=(kc == 0), stop=(kc == 3))
                S.copy(QN[:, sl], p0[:, :], eng="act")
                for kc in range(4):
                    S.mm(p1[0:64, :], wqb[:, kc * 256 + 128:kc * 256 + 192], cqn[:, kc * T + tc * 512:kc * T + (tc + 1) * 512],
                         start=(kc == 0), stop=(kc == 3))
                for kc in range(4):
                    S.mm(p2[0:64, :], wqb[:, kc * 256 + 192:kc * 256 + 256], cqn[:, kc * T + tc * 512:kc * T + (tc + 1) * 512],
                         start=(kc == 0), stop=(kc == 3))
                S.tt(rt[0][:, :], p1[0:64, :], cosT[:, sl], ALU.mult)
                S.tt(rt[1][:, :], p2[0:64, :], sinT[:, sl], ALU.mult)
                S.tt(QR[:, sl], rt[0][:, :], rt[1][:, :], ALU.add, eng="pool")
                for kc in range(2):
                    S.mm(p3[:, :], wkb[:, kc * 256:kc * 256 + 128], ckvn[:, kc * T + tc * 512:kc * T + (tc + 1) * 512],
                         start=(kc == 0), stop=(kc == 1))
                S.copy(KN[:, sl], p3[:, :], eng="act")
                for tb in range(4):
                    t0 = tc * 512 + tb * 128
                    for kc in range(2):
                        S.mm(p4[:, tb * 128:(tb + 1) * 128], ckvn[:, kc * T + t0:kc * T + t0 + 128],
                             wkb[:, kc * 256 + 128:kc * 256 + 256], start=(kc == 0), stop=(kc == 1))
                dst = V(VA.t[:, tc * 4 * 130:(tc + 1) * 4 * 130].rearrange("p (n c) -> p n c", c=130)[:, :, 0:128], VA.name,
                        (0, 128, tc * 4 * 130, (tc + 1) * 4 * 130))
                S.op("dve", lambda e, dst=dst, p4=p4: e.tensor_copy(out=dst.ap, in_=p4[:, :].ap.rearrange("p (n c) -> p n c", c=128)),
                     [p4[:, :]], [dst])
            if h == 7 and l == k.nl - 1 and b == 1:
                for nm, bf in (("QN", QN), ("QR", QR), ("KN", KN), ("VA", VA), ("krT", krT), ("cqn", cqn), ("ckvn", ckvn)):
                    dump(S, k, nm, bf, BF16)
                dump(S, k, "cosT", cosT, F32)
                dump(S, k, "sinT", sinT, F32)
            units = [(qc, kb) for qc in range(nt) for kb in range(4 * qc + 4)]
            pend = None

            def pv(qc, kb, ptile, j0):
                nonlocal oi
                for j in range(j0, 4):
                    ob = k.ps[2 + j]
                    qoff = (j - j0) * 128
                    S.mm(ob[:, 0:130], ptile[:, qoff:qoff + 128], VA[:, kb * 130:(kb + 1) * 130],
                         start=(kb == 0), stop=(kb == 4 * qc + j))
                    if kb == 4 * qc + j:
                        r_, o_ = rl[oi % 2], on[oi % 2]
                        pb = k.pb[oi % 2]
                        oi += 1
                        S.recip(r_[:, :], ob[:, 128:129])
                        S.ts(o_[:, :], ob[:, 0:128], r_[:, 0:1], ALU.mult)
                        S.tr(pb[:, 0:128], o_[:, :], k.ident[:, :])
                        S.tt(ost[qc % 2][:, j * 128:(j + 1) * 128], pb[:, 0:128], mz[qc % 2][:, j * 128:(j + 1) * 128], ALU.mult)
                        if j == 3:
                            S.dma(k.osz[0][h * 128:(h + 1) * 128, qc * 512:(qc + 1) * 512], ost[qc % 2][:, :], q="pool")

            for (qc, kb) in units:
                if kb == 0:
                    r0 = SEG_ROW["mz"] + h * 128
                    S.dma(mz[qc % 2][:, :], k.PT[r0:r0 + 128, qc * 512:(qc + 1) * 512])
                j0 = max(0, kb - 4 * qc)
                n = 512 - j0 * 128
                q0 = qc * 512 + j0 * 128
                ps = k.ps[ui % 2]
                ptile = pt[ui % 3]
                ui += 1
                S.mm(ps[:, 0:n], KN[:, kb * 128:(kb + 1) * 128], QN[:, q0:q0 + n], start=True, stop=False)
                S.mm(ps[:, 0:n], krT[:, kb * 128:(kb + 1) * 128], QR[:, q0:q0 + n], start=False, stop=True)
                S.act(ptile[:, 0:n], ps[:, 0:n], AF.Exp, scale=SC)
                if kb >= 4 * qc:
                    S.memset(ptile[64:128, 0:64], 0.0, eng="pool")
                if pend is not None:
                    pv(*pend)
                pend = (qc, kb, ptile, j0)
            pv(*pend)


def lin_consts(S, k):
    f = S.sbuf("mtmp", [128, 128], F32)
    k.mIU = S.sbuf("mIU", [128, 128], F32)
    k.mSU = S.sbuf("mSU", [128, 128], F32)
    k.mSL = S.sbuf("mSL", [128, 128], F32)
    S.memset(f[:, :], 1.0, eng="dve")
    for m, pat, cm, cmp_ in ((k.mIU, 1, -1, ALU.is_ge), (k.mSU, 1, -1, ALU.is_gt), (k.mSL, -1, 1, ALU.is_gt)):
        S.op("pool", lambda e, m=m, pat=pat, cm=cm, cmp_=cmp_: e.affine_select(
            out=m[:, :].ap, in_=f[:, :].ap, pattern=[[pat, 128]], compare_op=cmp_, fill=0.0, base=0, channel_multiplier=cm),
            [f[:, :]], [m[:, :]])
    k.rmask = S.sbuf("rmask", [128, 512], F32)
    S.memset(k.rmask[:, :], 1.0)
    S.op("pool", lambda e: e.memset(k.rmask[:, :].ap.rearrange("p (n c) -> p n c", c=128)[:, :, 0:1], 0.0), [], [k.rmask[:, :]])
    k.bones = S.sbuf("bones", [128, 128], BF16)
    S.memset(k.bones[:, :], 0.0)
    S.memset(k.bones[0:64, 0:64], 1.0)
    S.memset(k.bones[64:128, 64:128], 1.0)


def scan_cum(S, out, la, rmask):
    S.op("dve", lambda e: e.tensor_tensor_scan(out=out.ap, data0=rmask.ap, data1=la.ap, initial=0.0, op0=ALU.mult, op1=ALU.add),
         [rmask, la], [out])


def c3(v, c=128):
    return v.r("p (n c) -> p n c", c=c)


def phase_rwkv(S, k, l, b):
    T = k.T
    nt = T // 512
    with S.scope():
        TW = S.sbuf("TW", [64, T], BF16)
        AL = S.sbuf("AL", [64, T], BF16)
        w2f = S.sbuf("w2f", [64, 2048], F32)
        w2b = S.sbuf("w2b", [64, 2048], BF16)
        S.dma(w2f[:, 0:1024], k.rwkv_w2[l * 64:(l + 1) * 64, :])
        S.dma(w2f[:, 1024:2048], k.rwkv_a2[l * 64:(l + 1) * 64, :])
        S.copy(w2b[:, :], w2f[:, :], eng="pool")
        cols = S.sbuf("rcols", [128, 8 * 9], F32)
        for i, src in enumerate((k.rwkv_w0, k.rwkv_a0, k.rwkv_k_k, k.rwkv_k_a, k.rwkv_k_a, k.rwkv_r_k, k.rwkv_ln_w, k.rwkv_ln_b)):
            col_load(S, cols[:, i * 8:(i + 1) * 8], src, l * 1024, 8)
        S.ts(cols[:, 32:40], cols[:, 32:40], -1.0, ALU.mult, 1.0, ALU.add)
        rwa = SEG_ROW["rwa"]
        f = [S.sbuf("rf%d" % i, [128, 512], F32) for i in range(20)]
        for tc in range(nt):
            sl = slice(tc * 512, (tc + 1) * 512)
            S.dma(f[0][0:64, :], k.PT[rwa:rwa + 64, sl])
            S.dma(f[1][0:64, :], k.PT[rwa + 64:rwa + 128, sl])
            S.act(TW[:, sl], f[0][0:64, :], AF.Tanh)
            S.copy(AL[:, sl], f[1][0:64, :])
        ARH = S.sbuf("ARH", [128, 1024], BF16)
        BH = S.sbuf("BH", [128, 512], BF16)
        KH = S.sbuf("KH", [128, 512], BF16)
        FE = [S.sbuf("FE%d" % i, [128, 512], BF16) for i in range(4)]
        TK = [S.sbuf("TK%d" % i, [128, 512], BF16) for i in range(4)]
        WT = S.sbuf("WT", [128, 512], BF16)
        ZT = [[S.sbuf("ZT%d_%d" % (i, j), [128, 256], mybir.dt.float32r) for j in range(2)] for i in range(8)]
        NP = [[S.sbuf("NP%d_%d" % (i, j), [128, 128], mybir.dt.float32r) for j in range(2)] for i in range(8)]
        TTb = [S.sbuf("TTb%d" % i, [128, 128], BF16) for i in range(8)]
        AB = [S.sbuf("AB%d" % i, [128, 128], BF16) for i in range(8)]
        AK = [S.sbuf("AK%d" % i, [128, 128], BF16) for i in range(8)]
        AR = [S.sbuf("AR%d" % i, [128, 128], BF16) for i in range(8)]
        GS = [S.sbuf("GS%d" % i, [128, 64], BF16) for i in range(8)]
        UT = [S.sbuf("UT%d" % i, [128, 64], F32) for i in range(8)]
        St = S.sbuf("St", [128, 64], F32)
        Stb = S.sbuf("Stb", [128, 64], BF16)
        Ub = [S.sbuf("Ub%d" % i, [128, 128], BF16) for i in range(2)]
        pcb = S.sbuf("pcb", [128, 4], F32)
        sm = [S.sbuf("sm%d" % i, [128, 16], F32) for i in range(2)]
        sqy = S.sbuf("sqy", [128, 128], F32)
        yn = [S.sbuf("yn%d" % i, [128, 128], BF16) for i in range(2)]
        yf = [S.sbuf("yf%d" % i, [128, 128], F32) for i in range(2)]
        ost = [S.sbuf("rost%d" % i, [128, 512], BF16) for i in range(2)]
        pi = 0

        def PS():
            nonlocal pi
            pi += 1
            return k.ps[pi % 6]

        for hb in range(8):
            cw0, ca0, ckk, cka, comka, crk, clw, clb = [cols[:, i * 8 + hb:i * 8 + hb + 1] for i in range(8)]
            S.memset(St[:, :], 0.0)
            S.memset(Stb[:, :], 0.0)
            for tc in range(nt):
                sl = slice(tc * 512, (tc + 1) * 512)
                r_, kx, v_, lw, a_, kt, kk, kp, cum, e1, e2, ka, bon, rz, tmp, tmp2 = f[:16]
                for dst, nm in ((r_, "rr"), (kx, "rk"), (v_, "rv"), (rz, "rz")):
                    r0 = SEG_ROW[nm] + hb * 128
                    S.dma(dst[:, :], k.PT[r0:r0 + 128, sl])
                p = PS()
                S.mm(p[:, :], w2b[:, hb * 128:(hb + 1) * 128], TW[:, sl])
                S.act(lw[:, :], p[:, :], AF.Sigmoid, bias=cw0)
                S.ts(lw[:, :], lw[:, :], -0.6065306597126334, ALU.mult)
                p = PS()
                S.mm(p[:, :], w2b[:, 1024 + hb * 128:1024 + (hb + 1) * 128], AL[:, sl])
                S.act(a_[:, :], p[:, :], AF.Sigmoid, bias=ca0)
                S.ts(kt[:, :], kx[:, :], ckk, ALU.mult)
                S.act(FE[0][:, :], kt[:, :], AF.Square)
                p = PS()
                S.mm(p[:, :], k.bones[:, :], FE[0][:, :])
                S.act(tmp[:, :], p[:, :], AF.Sqrt)
                S.ts(tmp[:, :], tmp[:, :], 1e-12, ALU.max)
                S.recip(tmp[:, :], tmp[:, :])
                S.tt(kk[:, :], kt[:, :], tmp[:, :], ALU.mult)
                S.ts(tmp[:, :], a_[:, :], cka, ALU.mult, comka, ALU.add)
                S.tt(kp[:, :], kx[:, :], tmp[:, :], ALU.mult)
                S.tt(tmp[:, :], r_[:, :], kp[:, :], ALU.mult, eng="pool")
                S.ts(FE[0][:, :], tmp[:, :], crk, ALU.mult)
                p = PS()
                S.mm(p[:, :], k.bones[:, :], FE[0][:, :])
                S.tt(bon[:, :], v_[:, :], p[:, :], ALU.mult)
                scan_cum(S, cum[:, :], lw[:, :], k.rmask[:, :])
                cend = c3(cum[:, :]).ix(slice(None), slice(None), slice(127, 128))
                S.op("act", lambda e, cend=cend: e.activation(out=pcb[:, :].ap.rearrange("p (n o) -> p n o", o=1), in_=cend.ap, func=AF.Exp),
                     [cum[:, :]], [pcb[:, :]])
                S.act(e1[:, :], cum[:, :], AF.Exp)
                A3 = ARH[:, :].r("p (n two t) -> p n two t", two=2, t=128)
                S.op("dve", lambda e, A3=A3, r_=r_, e1=e1: e.tensor_tensor(out=A3.ap[:, :, 1, :], in0=c3(r_[:, :]).ap, in1=c3(e1[:, :]).ap, op=ALU.mult),
                     [r_[:, :], e1[:, :]], [ARH[:, :]])
                S.act(e2[:, :], cum[:, :], AF.Exp, scale=-1.0)
                S.tt(ka[:, :], kk[:, :], a_[:, :], ALU.mult, eng="pool")
                S.tt(BH[:, :], ka[:, :], e2[:, :], ALU.mult)
                S.tt(KH[:, :], kp[:, :], e2[:, :], ALU.mult)
                S.tt(tmp[:, :], cum[:, :], lw[:, :], ALU.subtract, eng="pool")
                S.act(e1[:, :], tmp[:, :], AF.Exp)
                S.stt(FE[0][:, :], kk[:, :], -1.0, e1[:, :], ALU.mult, ALU.mult)
                S.op("pool", lambda e, A3=A3: e.tensor_copy(out=A3.ap[:, :, 0, :], in_=c3(FE[0][:, :]).ap), [FE[0][:, :]], [ARH[:, :]])
                S.op("dve", lambda e, tmp2=tmp2, cum=cum, cend=cend: e.tensor_tensor(
                    out=c3(tmp2[:, :]).ap, in0=cend.ap.broadcast_to([128, 4, 128]), in1=c3(cum[:, :]).ap, op=ALU.subtract),
                    [cum[:, :]], [tmp2[:, :]])
                S.act(e2[:, :], tmp2[:, :], AF.Exp)
                S.tt(FE[1][:, :], ka[:, :], e2[:, :], ALU.mult)
                S.tt(FE[2][:, :], kp[:, :], e2[:, :], ALU.mult)
                S.copy(FE[3][:, :], v_[:, :], eng="pool")
                for i in range(4):
                    pb = k.pb[i % 2]
                    for c in range(4):
                        S.tr(pb[:, c * 128:(c + 1) * 128], FE[i][:, c * 128:(c + 1) * 128], k.ident[:, :])
                    S.copy(TK[i][:, :], pb[:, 0:512], eng=("act" if i % 2 else "dve"))
                if k.rstage < 1.1:
                    continue
                for c in range(4):
                    for e_ in range(k.ne):
                        ii = c * 2 + e_
                        rows = slice(e_ * 64, (e_ + 1) * 64)
                        ah = ARH[rows, c * 256:c * 256 + 128]
                        arh = ARH[rows, c * 256:c * 256 + 256]
                        bh = BH[rows, c * 128:(c + 1) * 128]
                        kh = KH[rows, c * 128:(c + 1) * 128]
                        p = PS()
                        S.mm(p[:, 0:128], ah, bh)
                        S.tt(NP[ii][0][:, :], p[:, 0:128], k.mSL[:, :], ALU.mult)
                        p = PS()
                        S.mm(p[:, 128:384], bh, arh)
                        S.tt(ZT[ii][0][:, 0:128], p[:, 128:256], k.mSU[:, :], ALU.mult)
                        S.tt(AB[ii][:, :], p[:, 256:384], k.mIU[:, :], ALU.mult)
                        p = PS()
                        S.mm(p[:, 0:256], kh, arh)
                        S.tt(AK[ii][:, :], p[:, 0:128], k.mSU[:, :], ALU.mult)
                        S.tt(AR[ii][:, :], p[:, 128:256], k.mIU[:, :], ALU.mult)
                        S.tt(ZT[ii][1][:, 128:256], ZT[ii][0][:, 0:128], k.identf[:, :], ALU.add, eng="pool")
                for lv in range(1, 7):
                    cur, nxt = (lv - 1) % 2, lv % 2
                    for ii in range(8):
                        p = PS()
                        S.mm(p[:, 0:128], ZT[ii][cur][:, 0:128], NP[ii][cur][:, :])
                        S.copy(NP[ii][nxt][:, :], p[:, 0:128], eng="act")
                        if 2 <= lv < 6:
                            p = PS()
                            S.mm(p[:, 0:256], NP[ii][cur][:, :], ZT[ii][cur][:, 0:256])
                            S.copy(ZT[ii][nxt][:, 0:128], p[:, 0:128], eng="dve")
                            S.tt(ZT[ii][nxt][:, 128:256], p[:, 128:256], ZT[ii][cur][:, 128:256], ALU.add)
                        elif lv < 6:
                            p = PS()
                            S.mm(p[:, 0:128], NP[ii][cur][:, :], ZT[ii][cur][:, 0:128])
                            S.copy(ZT[ii][nxt][:, 0:128], p[:, 0:128], eng="act")
                        else:
                            p = PS()
                            S.mm(p[:, 0:128], NP[ii][cur][:, :], ZT[ii][cur][:, 128:256])
                            S.tt(ZT[ii][nxt][:, 128:256], p[:, 0:128], ZT[ii][cur][:, 128:256], ALU.add)
                for ii in range(8):
                    p = PS()
                    S.mm(p[:, 0:128], NP[ii][0][:, :], ZT[ii][0][:, 128:256])
                    S.tt(TTb[ii][:, :], p[:, 0:128], ZT[ii][0][:, 128:256], ALU.add)
                fin = 6 % 2
                for c in range(4):
                    for e_ in range(k.ne if k.rstage > 1.5 else 0):
                        ii = c * 2 + e_
                        rows = slice(e_ * 64, (e_ + 1) * 64)
                        tt_ = TTb[ii][:, :]
                        hs = slice(c * 128 + e_ * 64, c * 128 + (e_ + 1) * 64)
                        p = PS()
                        S.mm(p[rows, 0:128], TK[0][:, hs], tt_)
                        S.copy(WT[rows, c * 128:(c + 1) * 128], p[rows, 0:128], eng="act")
                        p = PS()
                        S.mm(p[:, 128:192], AK[ii][:, :], TK[3][:, hs])
                        S.copy(GS[ii][:, :], p[:, 128:192])
                        p = PS()
                        S.mm(p[:, 256:320], tt_, GS[ii][:, :])
                        S.copy(UT[ii][:, :], p[:, 256:320], eng="act")
                if k.rstage < 3:
                    continue
                for c in range(4):
                    ub = Ub[c % 2]
                    pUs = (PS(), PS())
                    pYs = (PS(), PS())
                    R = [slice(e_ * 64, (e_ + 1) * 64) for e_ in range(2)]
                    HS = [slice(c * 128 + e_ * 64, c * 128 + (e_ + 1) * 64) for e_ in range(2)]
                    for e_ in range(2):
                        S.mm(pUs[e_][:, 0:64], WT[R[e_], c * 128:(c + 1) * 128], Stb[R[e_], :])
                    for e_ in range(2):
                        ii = c * 2 + e_
                        S.mm(pYs[e_][:, 0:64], ARH[R[e_], c * 256 + 128:c * 256 + 256], Stb[R[e_], :], start=True, stop=False)
                        S.mm(pYs[e_][:, 0:64], AR[ii][:, :], TK[3][:, HS[e_]], start=False, stop=False)
                    for e_ in range(2):
                        ii = c * 2 + e_
                        S.tt(ub[:, R[e_]], pUs[e_][:, 0:64], UT[ii][:, :], ALU.add)
                    for e_ in range(2):
                        ii = c * 2 + e_
                        S.mm(pYs[e_][:, 0:64], AB[ii][:, :], ub[:, R[e_]], start=False, stop=True)
                    pSs = (PS(), PS())
                    for e_ in range(2):
                        S.mm(pSs[e_][R[e_], 0:64], TK[1][:, HS[e_]], ub[:, R[e_]], start=True, stop=False)
                        S.mm(pSs[e_][R[e_], 0:64], TK[2][:, HS[e_]], TK[3][:, HS[e_]], start=False, stop=True)
                    for e_ in range(2):
                        S.stt(St[R[e_], :], St[R[e_], :], pcb[R[e_], c:c + 1], pSs[e_][R[e_], 0:64], ALU.mult, ALU.add)
                    S.copy(Stb[:, :], St[:, :], eng="act")

                    s_ = sm[c % 2]
                    for e_ in range(2):
                        S.op("dve", lambda e, s_=s_, e_=e_, py=pYs[e_]: e.tensor_reduce(out=s_[:, e_:e_ + 1].ap, in_=py[:, 0:64].ap, axis=AX.X, op=ALU.add),
                             [pYs[e_][:, 0:64]], [s_[:, e_:e_ + 1]])
                        S.act(sqy[:, e_ * 64:(e_ + 1) * 64], pYs[e_][:, 0:64], AF.Square)
                    S.op("dve", lambda e, s_=s_: e.tensor_reduce(out=s_[:, 2:4].ap, in_=sqy[:, :].ap.rearrange("p (h n) -> p h n", h=2), axis=AX.X, op=ALU.add),
                         [sqy[:, :]], [s_[:, 2:4]])
                    S.ts(s_[:, 4:6], s_[:, 0:2], 1.0 / 64, ALU.mult)
                    S.tt(s_[:, 6:8], s_[:, 4:6], s_[:, 4:6], ALU.mult)
                    S.stt(s_[:, 8:10], s_[:, 2:4], 1.0 / 64, s_[:, 6:8], ALU.mult, ALU.subtract)
                    S.act(s_[:, 10:12], s_[:, 8:10], AF.Sqrt, bias=k.epsc[:, 1:2])
                    S.recip(s_[:, 12:14], s_[:, 10:12])
                    y_ = yn[c % 2]
                    for e_ in range(2):
                        hc = slice(e_ * 64, (e_ + 1) * 64)
                        S.ts(y_[:, hc], pYs[e_][:, 0:64], s_[:, 4 + e_:5 + e_], ALU.subtract, s_[:, 12 + e_:13 + e_], ALU.mult)
                    pb = k.pb[c % 2]
                    S.tr(pb[:, 0:128], y_[:, :], k.ident[:, :])
                    yf_ = yf[c % 2]
                    S.act(yf_[:, :], pb[:, 0:128], AF.Identity, scale=clw, bias=clb)
                    S.tt(yf_[:, :], yf_[:, :], bon[:, c * 128:(c + 1) * 128], ALU.add, eng="pool")
                    S.tt(ost[tc % 2][:, c * 128:(c + 1) * 128], yf_[:, :], rz[:, c * 128:(c + 1) * 128], ALU.mult, eng="pool")
                S.dma(k.osz[1][hb * 128:(hb + 1) * 128, sl], ost[tc % 2][:, :], q="pool")
                if k.dbg and hb == k.dhb and tc == 0 and b == 1:
                    for nm, bf, dt in (("lw", lw, F32), ("a", a_, F32), ("kk", kk, F32), ("kp", kp, F32), ("cum", cum, F32), ("bon", bon, F32),
                                       ("ARH", ARH, BF16), ("BH", BH, BF16), ("KH", KH, BF16), ("TK0", TK[0], BF16), ("TK1", TK[1], BF16),
                                       ("TK2", TK[2], BF16), ("TK3", TK[3], BF16), ("WT", WT, BF16), ("NP0", NP[0][0], mybir.dt.float32r),
                                       ("ZT0", ZT[0][0], mybir.dt.float32r), ("UT0", UT[0], F32), ("AB0", AB[0], BF16), ("AK0", AK[0], BF16), ("AR0", AR[0], BF16),
                                       ("St", St, F32), ("pcb", pcb, F32)):
                        dump(S, k, "r_" + nm, bf, dt)


PHASES = ("mla", "rwkv", "gla")


def _core_inputs(ins, core, T):
    d = {}
    bs = slice(core * NB, (core + 1) * NB)
    d["x"] = np.ascontiguousarray(ins["x"][bs, :T]).reshape(NB * T, D)
    d["c"] = np.ascontiguousarray(ins["c"][bs])
    d["positions"] = np.ascontiguousarray(ins["positions"][bs, :T]).astype(np.int32)
    fr = (10000.0 ** (-np.arange(0, 64, 2, dtype=np.float32) / 64)).astype(np.float32)
    d["rope_freq"] = np.concatenate([fr, fr]).reshape(64, 1)
    for k_, v in ins.items():
        if k_ in ("x", "c", "positions"):
            continue
        v = np.ascontiguousarray(v, dtype=np.float32)
        if k_ in ("ada_b", "norm_pre", "norm_post"):
            d[k_] = v
        elif v.ndim == 2 or k_ == "rwkv_r_k":
            d[k_] = v.reshape(-1, 1)
        else:
            d[k_] = v.reshape(v.shape[0] * v.shape[1], v.shape[2])
    return d


def kernel(**inputs):
    T = inputs["x"].shape[1]
    nc, k = build(T=T, dbg=False, phases=PHASES)
    in_maps = [_core_inputs(inputs, c, T) for c in range(8)]
    res = run_bass_kernel_spmd(nc, in_maps, core_ids=list(range(8)))
    out = np.stack([r["out"].reshape(NB, T, D) for r in res.results], axis=0).reshape(8 * NB, T, D)
    return out.astype(np.float32)


def phase_gla(S, k, l, b):
    T = k.T
    nt = T // 512
    DKS = float(128 ** -0.5)
    with S.scope():
        w2f = S.sbuf("gw2f", [16, 512], F32)
        S.dma(w2f[:, :], k.gla_w2[l * 16:(l + 1) * 16, :])
        cols = S.sbuf("gcols", [128, 8], F32)
        col_load(S, cols[:, 0:4], k.gla_b, l * 512, 4)
        col_load(S, cols[:, 4:6], k.gla_norm, l * 256, 2)
        S.ts(cols[:, 0:4], cols[:, 0:4], -1.0, ALU.mult)
        f = [S.sbuf("gf%d" % i, [128, 512], F32) for i in range(12)]
        glT = S.sbuf("glT", [16, 512], F32)
        QH = S.sbuf("gQH", [128, 512], BF16)
        KH = S.sbuf("gKH", [128, 512], BF16)
        KEf = S.sbuf("gKEf", [128, 512], BF16)
        VBf = [S.sbuf("gVBf%d" % i, [128, 512], BF16) for i in range(2)]
        KEk = S.sbuf("gKEk", [128, 512], BF16)
        VTk = S.sbuf("gVTk", [128, 4 * 256], BF16)
        AT = [S.sbuf("gAT%d" % i, [128, 128], BF16) for i in range(2)]
        St = S.sbuf("gSt", [128, 256], F32)
        Stb = S.sbuf("gStb", [128, 256], BF16)
        pcb = S.sbuf("gpcb", [128, 4], F32)
        sm = [S.sbuf("gsm%d" % i, [128, 4], F32) for i in range(2)]
        junk = S.sbuf("gjunk", [128, 256], F32)
        yn = [S.sbuf("gyn%d" % i, [128, 256], BF16) for i in range(2)]
        ost = [S.sbuf("gost%d" % i, [128, 2 * 512], BF16) for i in range(2)]
        pi = 0

        def PS():
            nonlocal pi
            pi += 1
            return k.ps[pi % 6]

        for h in range(4):
            S.memset(St[:, :], 0.0)
            S.memset(Stb[:, :], 0.0)
            for tc in range(nt):
                sl = slice(tc * 512, (tc + 1) * 512)
                q_, k_, la, cum, e1, tmp, gz0, gz1, v0, v1 = f[:10]
                S.dma(q_[:, :], k.PT[SEG_ROW["gq"] + h * 128:SEG_ROW["gq"] + (h + 1) * 128, sl])
                S.dma(k_[:, :], k.PT[SEG_ROW["gk"] + h * 128:SEG_ROW["gk"] + (h + 1) * 128, sl])
                S.dma(glT[:, :], k.PT[SEG_ROW["gl"]:SEG_ROW["gl"] + 16, sl])
                for j, (vv, gg) in enumerate(((v0, gz0), (v1, gz1))):
                    r0 = SEG_ROW["gv"] + h * 256 + j * 128
                    S.dma(vv[:, :], k.PT[r0:r0 + 128, sl])
                    r0 = SEG_ROW["gz"] + h * 256 + j * 128
                    S.dma(gg[:, :], k.PT[r0:r0 + 128, sl])
                p = PS()
                S.mm(p[:, :], w2f[:, h * 128:(h + 1) * 128], glT[:, :])
                S.act(la[:, :], p[:, :], AF.Exp, scale=-1.0, bias=cols[:, h:h + 1])
                S.act(la[:, :], la[:, :], AF.Ln, bias=k.epsc[:, 2:3])
                S.ts(la[:, :], la[:, :], -1.0 / 16.0, ALU.mult)
                scan_cum(S, cum[:, :], la[:, :], k.rmask[:, :])
                cend = c3(cum[:, :]).ix(slice(None), slice(None), slice(127, 128))
                S.op("act", lambda e, cend=cend: e.activation(out=pcb[:, :].ap.rearrange("p (n o) -> p n o", o=1), in_=cend.ap, func=AF.Exp),
                     [cum[:, :]], [pcb[:, :]])
                S.act(e1[:, :], cum[:, :], AF.Exp)
                S.stt(QH[:, :], q_[:, :], DKS, e1[:, :], ALU.mult, ALU.mult)
                S.act(e1[:, :], cum[:, :], AF.Exp, scale=-1.0)
                S.tt(KH[:, :], k_[:, :], e1[:, :], ALU.mult)
                S.op("dve", lambda e, tmp=tmp, cum=cum, cend=cend: e.tensor_tensor(
                    out=c3(tmp[:, :]).ap, in0=cend.ap.broadcast_to([128, 4, 128]), in1=c3(cum[:, :]).ap, op=ALU.subtract),
                    [cum[:, :]], [tmp[:, :]])
                S.act(e1[:, :], tmp[:, :], AF.Exp)
                S.tt(KEf[:, :], k_[:, :], e1[:, :], ALU.mult)
                S.copy(VBf[0][:, :], v0[:, :], eng="pool")
                S.copy(VBf[1][:, :], v1[:, :], eng="pool")
                pb = k.pb[0]
                for c in range(4):
                    S.tr(pb[:, c * 128:(c + 1) * 128], KEf[:, c * 128:(c + 1) * 128], k.ident[:, :])
                S.copy(KEk[:, :], pb[:, 0:512], eng="act")
                pb = k.pb[1]
                for c in range(4):
                    for j in range(2):
                        S.tr(pb[:, c * 256 + j * 128:c * 256 + (j + 1) * 128], VBf[j][:, c * 128:(c + 1) * 128], k.ident[:, :])
                S.copy(VTk[:, :], pb[:, 0:1024])
                for c in range(4):
                    cs_ = slice(c * 128, (c + 1) * 128)
                    vt = VTk[:, c * 256:(c + 1) * 256]
                    p = PS()
                    S.mm(p[:, 0:128], KH[:, cs_], QH[:, cs_])
                    at = AT[c % 2]
                    S.tt(at[:, :], p[:, 0:128], k.mIU[:, :], ALU.mult)
                    pY = PS()
                    S.mm(pY[:, 0:256], QH[:, cs_], Stb[:, :], start=True, stop=False)
                    S.mm(pY[:, 0:256], at[:, :], vt, start=False, stop=True)
                    pS_ = PS()
                    S.mm(pS_[:, 0:256], KEk[:, cs_], vt)
                    S.stt(St[:, :], St[:, :], pcb[:, c:c + 1], pS_[:, 0:256], ALU.mult, ALU.add)
                    S.copy(Stb[:, :], St[:, :], eng="act")
                    s_ = sm[c % 2]
                    S.act(junk[:, :], pY[:, 0:256], AF.Square, accum=s_[:, 0:1])
                    S.act(s_[:, 1:2], s_[:, 0:1], AF.Sqrt, scale=1.0 / 256, bias=k.epsc[:, 0:1])
                    S.recip(s_[:, 2:3], s_[:, 1:2])
                    y_ = yn[c % 2]
                    S.ts(y_[:, :], pY[:, 0:256], s_[:, 2:3], ALU.mult)
                    pb2 = k.pb[c % 2]
                    for j, gg in enumerate((gz0, gz1)):
                        S.tr(pb2[:, j * 128:(j + 1) * 128], y_[:, j * 128:(j + 1) * 128], k.ident[:, :])
                        S.stt(ost[tc % 2][:, j * 512 + c * 128:j * 512 + (c + 1) * 128], pb2[:, j * 128:(j + 1) * 128],
                              cols[:, 4 + j:5 + j], gg[:, cs_], ALU.mult, ALU.mult)
                for j in range(2):
                    r0 = h * 256 + j * 128
                    S.dma(k.osz[2][r0:r0 + 128, sl], ost[tc % 2][:, j * 512:(j + 1) * 512], q="pool")
```
